# Optimizing a Trainium2 kernel written in Bass

```python
import math
import jax
import jax.numpy as jnp
from jax import lax
import numpy as np

D_MODEL = 1024
BATCH = 2
SEQ = 8192
DEPTH = 4
DEC_BATCH = 128
DEC_SEQ = 1
PAST_LEN = 8192
PAGE_SIZE = 128

D_MIX = D_MODEL
HEAD_DIM = 64
A_WIDTH = D_MIX // 4
A_HEADS = A_WIDTH // HEAD_DIM
A_DECAY_LORA = 64
A_AAA_LORA = 64
A_GATE_LORA = 128
A_PROJ = 3 * A_WIDTH + A_DECAY_LORA + A_AAA_LORA + A_GATE_LORA
B_WIDTH = D_MIX // 4
B_HEADS = B_WIDTH // HEAD_DIM
B_KV_HEADS = 2
B_GROUP = B_HEADS // B_KV_HEADS
B_PROJ = B_WIDTH + 2 * B_KV_HEADS * HEAD_DIM
WINDOW = 128
BLOCK = 128
C_WIDTH = D_MIX - A_WIDTH - B_WIDTH
C_HEADS = C_WIDTH // HEAD_DIM
C_GROUPS = 2
C_HPG = C_HEADS // C_GROUPS
D_STATE = 128
SSM_CONV = 4
CHUNK = 128
CONV_DIM = C_WIDTH + 2 * C_GROUPS * D_STATE
C_PROJ = C_WIDTH + CONV_DIM + C_HEADS
PROJ = A_PROJ + B_PROJ + C_PROJ
D_FF = ((8 * D_MODEL // 3 + 127) // 128) * 128
FFN_CONV = 3
NORM_EPS = 1e-6
GN_EPS = 64e-5

kernel_name = "hymba_rwkv7_swa_mamba2_convglu_step"

F32 = jnp.float32


def rms_norm(x, g):
    xf = x.astype(F32)
    y = xf * lax.rsqrt(jnp.mean(xf * xf, axis=-1, keepdims=True) + NORM_EPS)
    return (y * g.astype(F32)).astype(x.dtype)


def causal_dwconv(u, buf, w, b):
    k_w = w.shape[0]
    t = u.shape[1]
    full = jnp.concatenate([buf.astype(u.dtype), u], axis=1)
    y = b
    for j in range(k_w):
        y = y + full[:, j:j + t] * w[j]
    return y, full[:, t:]


def rwkv7_mixer(pa, shift_prev, wkv0, lp):
    n, t, _ = pa.shape
    prev = jnp.concatenate([shift_prev[:, None].astype(pa.dtype), pa[:, :-1]], axis=1)
    xs = pa + (prev - pa) * lp['rwkv_mu']
    cuts = [A_WIDTH, 2 * A_WIDTH, 3 * A_WIDTH, 3 * A_WIDTH + A_DECAY_LORA,
            3 * A_WIDTH + A_DECAY_LORA + A_AAA_LORA]
    r, k, v, lw, la, lg = jnp.split(xs, cuts, axis=-1)
    w_log = -jax.nn.softplus(-(lp['rwkv_w0'] + jnp.tanh(lw) @ lp['rwkv_w2'])) - 0.5
    a = jax.nn.sigmoid(lp['rwkv_a0'] + la @ lp['rwkv_a2'])
    g = jax.nn.sigmoid(lg) @ lp['rwkv_g2']

    def heads(z):
        return z.reshape(n, t, A_HEADS, HEAD_DIM).astype(F32)

    kk = heads(k * lp['rwkv_k_k'])
    kk = kk / jnp.maximum(jnp.sqrt(jnp.sum(kk * kk, axis=-1, keepdims=True)), 1e-12)
    k = k * (1 + (a - 1) * lp['rwkv_k_a'])
    rh, kh, vh, ah = heads(r), heads(k), heads(v), heads(a)
    decay = jnp.exp(-jnp.exp(heads(w_log)))

    def step(s, inp):
        r_t, k_t, v_t, d_t, a_t, b_t = inp
        sa = jnp.einsum('nhvk,nhk->nhv', s, a_t)
        s = s * d_t[:, :, None, :] + sa[..., None] * b_t[:, :, None, :] + v_t[..., None] * k_t[:, :, None, :]
        return s, jnp.einsum('nhvk,nhk->nhv', s, r_t)

    def tm(z):
        return jnp.moveaxis(z, 1, 0)

    s_fin, y = lax.scan(step, wkv0.astype(F32),
                        (tm(rh), tm(kh), tm(vh), tm(decay), tm(-kk), tm(kk * ah)))
    y = jnp.moveaxis(y, 0, 1)
    mean = jnp.mean(y, axis=-1, keepdims=True)
    var = jnp.mean(jnp.square(y - mean), axis=-1, keepdims=True)
    y = ((y - mean) * lax.rsqrt(var + GN_EPS)).reshape(n, t, A_WIDTH)
    y = y * lp['rwkv_ln_g'].astype(F32) + lp['rwkv_ln_b'].astype(F32)
    bonus = jnp.sum(rh * kh * lp['rwkv_r_k'].astype(F32), axis=-1, keepdims=True) * vh
    y = (y + bonus.reshape(n, t, A_WIDTH)) * g.astype(F32)
    return y.astype(pa.dtype), pa[:, -1], s_fin.astype(wkv0.dtype)


def sink_weights(s, mask, sinks):
    sink = sinks.astype(F32).reshape(B_KV_HEADS, B_GROUP, 1, 1)
    s = jnp.where(mask, s, -jnp.inf)
    m = jnp.maximum(jnp.max(s, axis=-1, keepdims=True), sink)
    p = jnp.exp(s - m)
    return p / (jnp.sum(p, axis=-1, keepdims=True) + jnp.exp(sink - m))


def swa_mixer(pb, cache_k, cache_v, lp, decode):
    n, t, _ = pb.shape
    q, k, v = jnp.split(pb, [B_WIDTH, B_WIDTH + B_KV_HEADS * HEAD_DIM], axis=-1)
    q = rms_norm(q.reshape(n, t, B_KV_HEADS, B_GROUP, HEAD_DIM), lp['attn_q_norm_g'])
    k = rms_norm(k.reshape(n, t, B_KV_HEADS, HEAD_DIM), lp['attn_k_norm_g'])
    v = v.reshape(n, t, B_KV_HEADS, HEAD_DIM)
    scale = HEAD_DIM ** -0.5
    if decode:
        kf = jnp.concatenate([cache_k.astype(k.dtype), k], axis=1)
        vf = jnp.concatenate([cache_v.astype(v.dtype), v], axis=1)
        s = jnp.einsum('ntkgd,nskd->nkgts', q, kf, preferred_element_type=F32) * scale
        rel = jnp.arange(t)[:, None] + WINDOW - jnp.arange(WINDOW + t)[None, :]
        p = sink_weights(s, (rel >= 0) & (rel <= WINDOW), lp['attn_sinks'])
        o = jnp.einsum('nkgts,nskd->ntkgd', p, vf.astype(F32))
        new_k, new_v = kf[:, t:], vf[:, t:]
    else:
        nb = t // BLOCK
        qb = q.reshape(n, nb, BLOCK, B_KV_HEADS, B_GROUP, HEAD_DIM)
        kb = k.reshape(n, nb, BLOCK, B_KV_HEADS, HEAD_DIM)
        vb = v.reshape(n, nb, BLOCK, B_KV_HEADS, HEAD_DIM)

        def with_prev(z):
            zp = jnp.pad(z, ((0, 0), (1, 0), (0, 0), (0, 0), (0, 0)))[:, :-1]
            return jnp.concatenate([zp, z], axis=2)

        kc, vc = with_prev(kb), with_prev(vb)
        s = jnp.einsum('nbqkgd,nbskd->nbkgqs', qb, kc, preferred_element_type=F32) * scale
        rel = jnp.arange(BLOCK)[:, None] + BLOCK - jnp.arange(2 * BLOCK)[None, :]
        band = (rel >= 0) & (rel <= WINDOW)
        has_prev = (jnp.arange(nb)[:, None, None] > 0) | (jnp.arange(2 * BLOCK) >= BLOCK)[None, None, :]
        mask = (band[None] & has_prev)[None, :, None, None]
        p = sink_weights(s, mask, lp['attn_sinks'])
        o = jnp.einsum('nbkgqs,nbskd->nbqkgd', p, vc.astype(F32))
        new_k, new_v = k[:, -WINDOW:], v[:, -WINDOW:]
    return o.reshape(n, t, B_WIDTH).astype(pb.dtype), new_k, new_v


def ssd_chunked(xdt, da, bm, cm, h0):
    n, t = xdt.shape[:2]
    nc = t // CHUNK
    x = xdt.reshape(n, nc, CHUNK, C_GROUPS, C_HPG, HEAD_DIM)
    acum = jnp.cumsum(da.reshape(n, nc, CHUNK, C_GROUPS, C_HPG), axis=2)
    b = bm.reshape(n, nc, CHUNK, C_GROUPS, D_STATE)
    c = cm.reshape(n, nc, CHUNK, C_GROUPS, D_STATE)
    causal = jnp.tril(jnp.ones((CHUNK, CHUNK), dtype=bool))[:, :, None, None]
    seg = acum[:, :, :, None] - acum[:, :, None, :]
    lmat = jnp.exp(jnp.where(causal, seg, -jnp.inf))
    cb = jnp.einsum('nclgd,ncmgd->nclmg', c, b)
    y_diag = jnp.einsum('nclmg,nclmge,ncmgep->nclgep', cb, lmat, x)
    decay_to_end = jnp.exp(acum[:, :, -1:] - acum)
    states = jnp.einsum('ncmgd,ncmge,ncmgep->ncgepd', b, decay_to_end, x)
    chunk_decay = jnp.exp(acum[:, :, -1])

    def step(h, inp):
        st, dec = inp
        return h * dec[..., None, None] + st, h

    h_init = h0.reshape(n, C_GROUPS, C_HPG, HEAD_DIM, D_STATE)
    h_fin, h_prev = lax.scan(step, h_init, (jnp.moveaxis(states, 1, 0), jnp.moveaxis(chunk_decay, 1, 0)))
    h_prev = jnp.moveaxis(h_prev, 0, 1)
    y_off = jnp.einsum('nclgd,ncgepd,nclge->nclgep', c, h_prev, jnp.exp(acum))
    y = (y_diag + y_off).reshape(n, t, C_HEADS, HEAD_DIM)
    return y, h_fin.reshape(n, C_HEADS, HEAD_DIM, D_STATE)


def ssd_recurrent(xdt, da, bm, cm, h0):
    bh = jnp.repeat(bm, C_HPG, axis=2)
    ch = jnp.repeat(cm, C_HPG, axis=2)

    def step(h, inp):
        x_t, a_t, b_t, c_t = inp
        h = h * jnp.exp(a_t)[..., None, None] + x_t[..., None] * b_t[:, :, None, :]
        return h, jnp.einsum('nhpd,nhd->nhp', h, c_t)

    h_fin, y = lax.scan(step, h0, (jnp.moveaxis(xdt, 1, 0), jnp.moveaxis(da, 1, 0),
                                   jnp.moveaxis(bh, 1, 0), jnp.moveaxis(ch, 1, 0)))
    return jnp.moveaxis(y, 0, 1), h_fin


def mamba2_mixer(pc, conv_buf, h0, lp, decode):
    n, t, _ = pc.shape
    z, xbc, dt = jnp.split(pc, [C_WIDTH, C_WIDTH + CONV_DIM], axis=-1)
    xbc, new_buf = causal_dwconv(xbc, conv_buf, lp['ssm_conv_w'], lp['ssm_conv_b'])
    xbc = jax.nn.silu(xbc)
    x, bm, cm = jnp.split(xbc, [C_WIDTH, C_WIDTH + C_GROUPS * D_STATE], axis=-1)
    x = x.reshape(n, t, C_HEADS, HEAD_DIM).astype(F32)
    bm = bm.reshape(n, t, C_GROUPS, D_STATE).astype(F32)
    cm = cm.reshape(n, t, C_GROUPS, D_STATE).astype(F32)
    dt = jax.nn.softplus(dt.astype(F32) + lp['ssm_dt_bias'].astype(F32))
    a = -jnp.exp(lp['ssm_a_log'].astype(F32))
    ssd = ssd_recurrent if decode else ssd_chunked
    y, h_fin = ssd(x * dt[..., None], dt * a, bm, cm, h0.astype(F32))
    y = (y + lp['ssm_d'].astype(F32)[:, None] * x).reshape(n, t, C_WIDTH)
    y = rms_norm(y * jax.nn.silu(z.astype(F32)), lp['ssm_norm_g'])
    return y.astype(pc.dtype), new_buf, h_fin.astype(h0.dtype)


def conv_glu(h, buf, lp):
    gate, val = jnp.split(h @ lp['ffn_w_up'], 2, axis=-1)
    gate, new_buf = causal_dwconv(gate, buf, lp['ffn_conv_w'], lp['ffn_conv_b'])
    return (jax.nn.silu(gate) * val) @ lp['ffn_w_down'], new_buf


def trunk_layer(x, c, shift_prev, wkv0, swa_k, swa_v, ssm_conv_buf, ssm_h0, ffn_buf, lp, decode):
    mod = jax.nn.silu(c) @ lp['ada_w'] + lp['ada_b']
    sh1, sc1, g1, sh2, sc2, g2 = jnp.split(mod[:, None, :], 6, axis=-1)
    h = rms_norm(x, lp['norm_mix_g']) * (1 + sc1) + sh1
    pa, pb, pc = jnp.split(h @ lp['w_in'], [A_PROJ, A_PROJ + B_PROJ], axis=-1)
    ya, new_shift, new_wkv = rwkv7_mixer(pa, shift_prev, wkv0, lp)
    yb, new_k, new_v = swa_mixer(pb, swa_k, swa_v, lp, decode)
    yc, new_conv, new_ssm = mamba2_mixer(pc, ssm_conv_buf, ssm_h0, lp, decode)
    x = x + g1 * (jnp.concatenate([ya, yb, yc], axis=-1) @ lp['w_out'])
    h = rms_norm(x, lp['norm_ffn_g']) * (1 + sc2) + sh2
    f, new_ffn = conv_glu(h, ffn_buf, lp)
    x = x + g2 * f
    return x, (new_shift, new_wkv, new_k, new_v, new_conv, new_ssm, new_ffn)


def setup_inputs(seed: int = 0) -> dict:
    key = jax.random.key(seed)
    ks = iter(jax.random.split(key, 64))

    def nrm(shape, s=1.0):
        return s * jax.random.normal(next(ks), shape, F32)

    def uni(shape, lo, hi):
        return jax.random.uniform(next(ks), shape, F32, lo, hi)

    L = DEPTH
    dt0 = jnp.exp(uni((L, C_HEADS), math.log(1e-3), math.log(1e-1)))
    return {
        'x_prompt': nrm((BATCH, SEQ, D_MODEL)),
        'x_sample': nrm((DEC_BATCH, DEC_SEQ, D_MODEL)),
        'c_prompt': nrm((BATCH, D_MODEL)),
        'c_sample': nrm((DEC_BATCH, D_MODEL)),
        'state_rwkv_shift': nrm((L, DEC_BATCH, A_PROJ)),
        'state_rwkv_wkv': nrm((L, DEC_BATCH, A_HEADS, HEAD_DIM, HEAD_DIM), 0.3),
        'cache_swa_k': nrm((L, DEC_BATCH, WINDOW, B_KV_HEADS, HEAD_DIM)),
        'cache_swa_v': nrm((L, DEC_BATCH, WINDOW, B_KV_HEADS, HEAD_DIM)),
        'state_ssm_conv': nrm((L, DEC_BATCH, SSM_CONV - 1, CONV_DIM)),
        'state_ssm': nrm((L, DEC_BATCH, C_HEADS, HEAD_DIM, D_STATE), 0.1),
        'state_ffn_conv': nrm((L, DEC_BATCH, FFN_CONV - 1, D_FF)),
        'ada_w': nrm((L, D_MODEL, 6 * D_MODEL), D_MODEL ** -0.5),
        'ada_b': nrm((L, 6 * D_MODEL), 0.02),
        'norm_mix_g': 1.0 + nrm((L, D_MODEL), 0.02),
        'norm_ffn_g': 1.0 + nrm((L, D_MODEL), 0.02),
        'w_in': nrm((L, D_MODEL, PROJ), D_MODEL ** -0.5),
        'w_out': nrm((L, D_MIX, D_MODEL), D_MIX ** -0.5),
        'rwkv_mu': uni((L, A_PROJ), 0.0, 1.0),
        'rwkv_w0': uni((L, A_WIDTH), -6.5, -1.5),
        'rwkv_w2': nrm((L, A_DECAY_LORA, A_WIDTH), 0.1 * A_DECAY_LORA ** -0.5),
        'rwkv_a0': nrm((L, A_WIDTH), 0.1),
        'rwkv_a2': nrm((L, A_AAA_LORA, A_WIDTH), 0.5 * A_AAA_LORA ** -0.5),
        'rwkv_g2': nrm((L, A_GATE_LORA, A_WIDTH), A_GATE_LORA ** -0.5),
        'rwkv_k_k': 0.85 + nrm((L, A_WIDTH), 0.05),
        'rwkv_k_a': 1.0 + nrm((L, A_WIDTH), 0.05),
        'rwkv_r_k': nrm((L, A_HEADS, HEAD_DIM), 0.1),
        'rwkv_ln_g': 1.0 + nrm((L, A_WIDTH), 0.02),
        'rwkv_ln_b': nrm((L, A_WIDTH), 0.02),
        'attn_q_norm_g': 1.0 + nrm((L, HEAD_DIM), 0.02),
        'attn_k_norm_g': 1.0 + nrm((L, HEAD_DIM), 0.02),
        'attn_sinks': nrm((L, B_HEADS), 0.5),
        'ssm_conv_w': nrm((L, SSM_CONV, CONV_DIM), SSM_CONV ** -0.5),
        'ssm_conv_b': nrm((L, CONV_DIM), 0.02),
        'ssm_dt_bias': dt0 + jnp.log(-jnp.expm1(-dt0)),
        'ssm_a_log': jnp.log(uni((L, C_HEADS), 1.0, 16.0)),
        'ssm_d': 1.0 + nrm((L, C_HEADS), 0.02),
        'ssm_norm_g': 1.0 + nrm((L, C_WIDTH), 0.02),
        'ffn_w_up': nrm((L, D_MODEL, 2 * D_FF), D_MODEL ** -0.5),
        'ffn_conv_w': nrm((L, FFN_CONV, D_FF), FFN_CONV ** -0.5),
        'ffn_conv_b': nrm((L, D_FF), 0.02),
        'ffn_w_down': nrm((L, D_FF, D_MODEL), D_FF ** -0.5),
    }


def reference(x_prompt, x_sample, c_prompt, c_sample,
              state_rwkv_shift, state_rwkv_wkv, cache_swa_k, cache_swa_v,
              state_ssm_conv, state_ssm, state_ffn_conv,
              ada_w, ada_b, norm_mix_g, norm_ffn_g, w_in, w_out,
              rwkv_mu, rwkv_w0, rwkv_w2, rwkv_a0, rwkv_a2, rwkv_g2,
              rwkv_k_k, rwkv_k_a, rwkv_r_k, rwkv_ln_g, rwkv_ln_b,
              attn_q_norm_g, attn_k_norm_g, attn_sinks,
              ssm_conv_w, ssm_conv_b, ssm_dt_bias, ssm_a_log, ssm_d, ssm_norm_g,
              ffn_w_up, ffn_conv_w, ffn_conv_b, ffn_w_down):
    nb = x_prompt.shape[0]
    fdt = x_prompt.dtype
    zp_shift = jnp.zeros((nb, A_PROJ), fdt)
    zp_wkv = jnp.zeros((nb, A_HEADS, HEAD_DIM, HEAD_DIM), F32)
    zp_sconv = jnp.zeros((nb, SSM_CONV - 1, CONV_DIM), fdt)
    zp_ssm = jnp.zeros((nb, C_HEADS, HEAD_DIM, D_STATE), F32)
    zp_ffn = jnp.zeros((nb, FFN_CONV - 1, D_FF), fdt)
    prompt_new = [[] for _ in range(7)]
    sample_new = [[] for _ in range(7)]
    yp, ys = x_prompt, x_sample
    for i in range(DEPTH):
        lp = {
            'ada_w': ada_w[i], 'ada_b': ada_b[i], 'norm_mix_g': norm_mix_g[i], 'norm_ffn_g': norm_ffn_g[i],
            'w_in': w_in[i], 'w_out': w_out[i],
            'rwkv_mu': rwkv_mu[i], 'rwkv_w0': rwkv_w0[i], 'rwkv_w2': rwkv_w2[i], 'rwkv_a0': rwkv_a0[i],
            'rwkv_a2': rwkv_a2[i], 'rwkv_g2': rwkv_g2[i], 'rwkv_k_k': rwkv_k_k[i], 'rwkv_k_a': rwkv_k_a[i],
            'rwkv_r_k': rwkv_r_k[i], 'rwkv_ln_g': rwkv_ln_g[i], 'rwkv_ln_b': rwkv_ln_b[i],
            'attn_q_norm_g': attn_q_norm_g[i], 'attn_k_norm_g': attn_k_norm_g[i], 'attn_sinks': attn_sinks[i],
            'ssm_conv_w': ssm_conv_w[i], 'ssm_conv_b': ssm_conv_b[i], 'ssm_dt_bias': ssm_dt_bias[i],
            'ssm_a_log': ssm_a_log[i], 'ssm_d': ssm_d[i], 'ssm_norm_g': ssm_norm_g[i],
            'ffn_w_up': ffn_w_up[i], 'ffn_conv_w': ffn_conv_w[i], 'ffn_conv_b': ffn_conv_b[i],
            'ffn_w_down': ffn_w_down[i],
        }
        yp, new_p = trunk_layer(yp, c_prompt, zp_shift, zp_wkv, None, None, zp_sconv, zp_ssm, zp_ffn,
                                lp, False)
        ys, new_s = trunk_layer(ys, c_sample, state_rwkv_shift[i], state_rwkv_wkv[i], cache_swa_k[i],
                                cache_swa_v[i], state_ssm_conv[i], state_ssm[i], state_ffn_conv[i], lp, True)
        for lst, arr in zip(prompt_new, new_p):
            lst.append(arr)
        for lst, arr in zip(sample_new, new_s):
            lst.append(arr)
    p_shift, p_wkv, p_k, p_v, p_conv, p_ssm, p_ffn = [jnp.stack(l) for l in prompt_new]
    s_shift, s_wkv, s_k, s_v, s_conv, s_ssm, s_ffn = [jnp.stack(l) for l in sample_new]
    return (yp, ys, p_shift, p_wkv, p_k, p_v, p_conv, p_ssm, p_ffn,
            s_shift, s_wkv, s_k, s_v, s_conv, s_ssm, s_ffn)
```

```python
import math
import numpy as np
import concourse.bass as bass
import concourse.mybir as mybir
from concourse.bass_utils import run_bass_kernel_spmd

F32 = mybir.dt.float32
BF16 = mybir.dt.bfloat16
ALU = mybir.AluOpType
AF = mybir.ActivationFunctionType
AX = mybir.AxisListType

SEM_ROLL = 30000
D = 1024
PROJ = 3080
DFF = 2816
NCH_FF = 22
EPS = 1e-6
GN_EPS = 64e-5


class Ev:
    __slots__ = ("sem", "val", "key")

    def __init__(self, sem, val, key):
        self.sem, self.val, self.key = sem, val, key


class Tl:
    __slots__ = ("t", "tr", "name", "off", "excl")

    def __init__(self, t, name="", tr=None, off=0):
        self.t, self.name, self.off = t, name, off
        self.excl = False
        self.tr = tr if tr is not None else [None, []]

    @property
    def lw(self):
        return self.tr[0]

    @lw.setter
    def lw(self, v):
        self.tr[0] = v

    @property
    def rd(self):
        return self.tr[1]

    @rd.setter
    def rd(self, v):
        self.tr[1] = v

    def __getitem__(self, k):
        return self.t[k]


class Q:
    def __init__(self, fw, eng, name):
        self.fw, self.eng, self.name = fw, eng, name
        self.nsem = 0
        self.waited = {}
        self.pend_r, self.pend_w = [], []
        self.new_sem()

    def new_sem(self):
        self.sem = self.fw.nc.alloc_semaphore(f"s_{self.name}_{self.nsem}")
        self.nsem += 1
        self.cnt = 0


class FW:
    def __init__(self, nc):
        self.nc = nc
        self.pe = Q(self, nc.tensor, "pe")
        self.act = Q(self, nc.scalar, "act")
        self.dve = Q(self, nc.vector, "dve")
        self.pool = Q(self, nc.gpsimd, "pool")
        self.sp = Q(self, nc.sync, "sp")
        self.ndma = 32
        self.dsem = [nc.alloc_semaphore(f"s_dma_{i}") for i in range(self.ndma)]
        self.dcnt = [0] * self.ndma
        self.dnext = 0
        self.dlast = [None] * self.ndma
        self.n_inst = 0
        self.swd = []
        self.tmap = {}

    def _wait(self, q, ev):
        if ev is None:
            return
        if q.waited.get(ev.key, 0) >= ev.val:
            return
        q.eng.wait_ge(ev.sem, ev.val)
        q.waited[ev.key] = ev.val

    def _deps(self, q, reads, writes):
        for t in reads:
            self._wait(q, t.lw)
            if t.excl:
                for e in t.rd:
                    self._wait(q, e)
        for t in writes:
            self._wait(q, t.lw)
            for e in t.rd:
                self._wait(q, e)

    def _commit(self, ev, reads, writes):
        for t in writes:
            t.lw = ev
            t.rd = []
        for t in reads:
            if t.lw is not ev:
                t.rd = [e for e in t.rd if e.key != ev.key]
                t.rd.append(ev)

    def op(self, q, fn, reads=(), writes=(), inc=True):
        self._deps(q, reads, writes)
        ins = fn()
        self.n_inst += 1
        q.pend_r.extend(reads)
        q.pend_w.extend(writes)
        if not inc:
            return None
        if q.cnt >= SEM_ROLL:
            q.new_sem()
        q.cnt += 1
        ins.then_inc(q.sem, 1)
        ev = Ev(q.sem, q.cnt, id(q.sem))
        self._commit(ev, q.pend_r, q.pend_w)
        q.pend_r, q.pend_w = [], []
        return ev

    def dma(self, q, out, in_, reads=(), writes=(), **kw):
        self._deps(q, reads, writes)
        if q is self.pool:
            sem = self.nc.alloc_semaphore(f"s_swdma_{self.n_inst}")
            ins = q.eng.dma_start(out=out, in_=in_, **kw)
            self.n_inst += 1
            ins.then_inc(sem, 16)
            ev = Ev(sem, 16, id(sem))
            self.swd.append(ev)
            self._commit(ev, reads, writes)
            return ev
        i = self.dnext
        self.dnext = (self.dnext + 1) % self.ndma
        self._wait(q, self.dlast[i])
        ins = q.eng.dma_start(out=out, in_=in_, **kw)
        self.n_inst += 1
        self.dcnt[i] += 16
        ins.then_inc(self.dsem[i], 16)
        ev = Ev(self.dsem[i], self.dcnt[i], id(self.dsem[i]))
        self.dlast[i] = ev
        self._commit(ev, reads, writes)
        return ev


WNAMES = ['ada_w', 'ada_b', 'norm_mix_g', 'norm_ffn_g', 'w_in', 'w_out', 'rwkv_mu', 'rwkv_w0', 'rwkv_w2',
          'rwkv_a0', 'rwkv_a2', 'rwkv_g2', 'rwkv_k_k', 'rwkv_k_a', 'rwkv_r_k', 'rwkv_ln_g', 'rwkv_ln_b',
          'attn_q_norm_g', 'attn_k_norm_g', 'attn_sinks', 'ssm_conv_w', 'ssm_conv_b', 'ssm_dt_bias',
          'ssm_a_log', 'ssm_d', 'ssm_norm_g', 'ffn_w_up', 'ffn_conv_w', 'ffn_conv_b', 'ffn_w_down']


def wshapes(L):
    return {
        'ada_w': [L, D, 6 * D], 'ada_b': [L, 6 * D], 'norm_mix_g': [L, D], 'norm_ffn_g': [L, D],
        'w_in': [L, D, PROJ], 'w_out': [L, D, D], 'rwkv_mu': [L, 1024], 'rwkv_w0': [L, 256],
        'rwkv_w2': [L, 64, 256], 'rwkv_a0': [L, 256], 'rwkv_a2': [L, 64, 256], 'rwkv_g2': [L, 128, 256],
        'rwkv_k_k': [L, 256], 'rwkv_k_a': [L, 256], 'rwkv_r_k': [L, 256], 'rwkv_ln_g': [L, 256],
        'rwkv_ln_b': [L, 256], 'attn_q_norm_g': [L, 64], 'attn_k_norm_g': [L, 64], 'attn_sinks': [L, 4],
        'ssm_conv_w': [L, 4, 1024], 'ssm_conv_b': [L, 1024], 'ssm_dt_bias': [L, 8], 'ssm_a_log': [L, 8],
        'ssm_d': [L, 8], 'ssm_norm_g': [L, 512], 'ffn_w_up': [L, D, 2 * DFF], 'ffn_conv_w': [L, 3, DFF],
        'ffn_conv_b': [L, DFF], 'ffn_w_down': [L, DFF, D],
    }


class _Stop(Exception):
    pass


def build(SEQ, L, NS=16, do_decode=True, stop=None, dbg=False):
    NT = SEQ // 128
    nc = bass.Bass("TRN2", target_bir_lowering=False)
    fw = FW(nc)
    V, S, G, T = nc.vector, nc.scalar, nc.gpsimd, nc.tensor

    cur = {"i": 0}

    def cp(name):
        if stop == name or stop == f"{name}@{cur['i']}":
            raise _Stop()

    def dv(fn, r=(), w=()):
        return fw.op(fw.dve, fn, r, w)

    def ac(fn, r=(), w=()):
        return fw.op(fw.act, fn, r, w)

    def pl(fn, r=(), w=()):
        return fw.op(fw.pool, fn, r, w)

    def mm(fn, r=(), w=(), inc=True, dense=False):
        return fw.op(fw.pe, fn, r, w, inc if dense else True)

    def dma(out, in_, r=(), w=(), q=None, **kw):
        return fw.dma(q or fw.sp, out, in_, r, w, **kw)

    def din(name, shape):
        return Tl(nc.dram_tensor(name, shape, F32, kind="ExternalInput").ap(), name)

    def dout(name, shape):
        return Tl(nc.dram_tensor(name, shape, F32, kind="ExternalOutput").ap(), name)

    def dscr(name, shape):
        return Tl(nc.dram_tensor(name, shape, F32, kind="Internal").ap(), name)

    BIGN = 53200
    big = nc.alloc_sbuf_tensor("big", [128, BIGN], F32)
    reg = {"off": 0, "mark": 0}

    def sb(name, shape, dt=F32, alias=None):
        n = 1
        for d_ in shape[1:]:
            n *= d_
        words = n if dt == F32 else (n + 1) // 2
        words = (words + 7) // 8 * 8
        if alias is not None:
            off = alias.off
        else:
            off = reg["off"]
            assert off + words <= BIGN, f"SBUF region overflow at {name}: {off}+{words}"
            reg["off"] = off + words
        ap = big[0:shape[0], off:off + words]
        if dt != F32:
            ap = ap.bitcast(dt)
        ap = ap[:, 0:n]
        if len(shape) == 3:
            ap = ap.rearrange("p (a b) -> p a b", a=shape[1])
        elif len(shape) == 4:
            ap = ap.rearrange("p (a b c) -> p a b c", a=shape[1], b=shape[2])
        fw.tmap[name] = (off, tuple(shape), dt)
        return Tl(ap, name, tr=(alias.tr if alias is not None else None), off=off)

    def barrier():
        evs = []
        for q in (fw.pe, fw.act, fw.dve, fw.pool):
            if q.cnt > 0:
                evs.append(Ev(q.sem, q.cnt, id(q.sem)))
        evs += [e for e in fw.dlast if e is not None] + list(fw.swd)
        for q in (fw.pe, fw.act, fw.dve, fw.pool, fw.sp):
            for e in evs:
                fw._wait(q, e)

    def phase_reset():
        barrier()
        reg["off"] = reg["mark"]

    def psb(name, shape=(128, 512), dt=F32):
        t = Tl(nc.alloc_psum_tensor(name, list(shape), dt), name)
        t.excl = True
        return t

    x_p = din("x_p", [SEQ, D])
    x_s = din("x_s", [NS, D])
    c_all = din("c_all", [NS + 1, D])
    st_shift = din("st_shift", [L, NS, 1024])
    st_wkv = din("st_wkv", [L, NS * 4, 4096])
    st_k = din("st_k", [L, NS, 128, 2, 64])
    st_v = din("st_v", [L, NS, 128, 2, 64])
    st_conv = din("st_conv", [L, NS, 3, 1024])
    st_ssm = din("st_ssm", [L, NS * 8, 8192])
    st_ffn = din("st_ffn", [L, NS, 2, DFF])
    Wd = {k: din(k, s) for k, s in wshapes(L).items()}

    y_p = dout("y_p", [SEQ, D])
    y_s = dout("y_s", [NS, D])
    o_pshift = dout("o_pshift", [L, 1024])
    o_pwkv = dout("o_pwkv", [L, 4, 64, 64])
    o_pk = dout("o_pk", [L, 128, 128])
    o_pv = dout("o_pv", [L, 128, 128])
    o_pconv = dout("o_pconv", [L, 3, 1024])
    o_pssm = dout("o_pssm", [L, 512, 128])
    o_pffn = dout("o_pffn", [L, 2, DFF])
    o_sshift = dout("o_sshift", [L, NS, 1024])
    o_swkv = dout("o_swkv", [L, NS * 4, 4096])
    o_sk = dout("o_sk", [L, NS, 128, 2, 64])
    o_sv = dout("o_sv", [L, NS, 128, 2, 64])
    o_sconv = dout("o_sconv", [L, NS, 3, 1024])
    o_sssm = dout("o_sssm", [L, NS * 8, 8192])
    o_sffn = dout("o_sffn", [L, NS, 2, DFF])
    outs = [y_p, y_s, o_pshift, o_pwkv, o_pk, o_pv, o_pconv, o_pssm, o_pffn, o_sshift, o_swkv, o_sk, o_sv,
            o_sconv, o_sssm, o_sffn]

    dbg_o = dout("dbg_o", [SEQ, D]) if dbg else None
    xa = dscr("xa", [SEQ, D])
    xb = dscr("xb", [SEQ, D])
    modrow = dscr("modrow", [L, 6 * D])

    ident = sb("ident", [128, 128])
    identb = sb("identb", [128, 128], BF16)
    ones = sb("ones", [128, 128])
    blk = sb("blk", [128, 128])
    hsel = sb("hsel", [128, 2])
    m_us = sb("m_us", [128, 256])
    m_ls = sb("m_ls", [128, 128])
    tri = sb("tri", [128, 128])
    pl(lambda: G.memset(ones[:], 1.0), w=[ones])
    pl(lambda: G.memset(ident[:], 1.0), w=[ident])
    pl(lambda: G.affine_select(ident[:], ident[:], [[-1, 128]], ALU.is_equal, 0.0, base=0, channel_multiplier=1),
       r=[ident], w=[ident])
    dv(lambda: V.tensor_copy(identb[:], ident[:]), r=[ident], w=[identb])
    pl(lambda: G.memset(m_us[:], 1.0), w=[m_us])
    pl(lambda: G.affine_select(m_us[:, 0:128], m_us[:, 0:128], [[1, 128]], ALU.is_gt, 0.0, base=0,
                               channel_multiplier=-1), r=[m_us], w=[m_us])
    pl(lambda: G.affine_select(m_us[:, 128:256], m_us[:, 128:256], [[1, 128]], ALU.is_ge, 0.0, base=0,
                               channel_multiplier=-1), r=[m_us], w=[m_us])
    pl(lambda: G.memset(m_ls[:], 1.0), w=[m_ls])
    pl(lambda: G.affine_select(m_ls[:], m_ls[:], [[-1, 128]], ALU.is_gt, 0.0, base=0, channel_multiplier=1),
       r=[m_ls], w=[m_ls])
    pl(lambda: G.tensor_copy(tri[:], m_us[:, 128:256]), r=[m_us], w=[tri])
    pl(lambda: G.memset(blk[:], 0.0), w=[blk])
    pl(lambda: G.memset(blk[0:64, 0:64], 1.0), w=[blk])
    pl(lambda: G.memset(blk[64:128, 64:128], 1.0), w=[blk])
    pl(lambda: G.memset(hsel[:], 0.0), w=[hsel])
    pl(lambda: G.memset(hsel[0:64, 0:1], 1.0), w=[hsel])
    pl(lambda: G.memset(hsel[64:128, 1:2], 1.0), w=[hsel])
    bd32 = sb("bd32", [128, 128]); off1 = sb("off1", [128, 128]); off2 = sb("off2", [128, 128])
    for (t_, bs) in ((bd32, 32), (off1, 64)):
        pl(lambda t_=t_: G.memset(t_[:], 1.0), w=[t_])
        v3 = t_[:].rearrange("p (b j) -> p b j", j=bs)
        pl(lambda v3=v3, bs=bs: G.affine_select(v3, v3, [[-bs, 128 // bs], [0, bs]], ALU.is_ge, 0.0, base=0,
                                                channel_multiplier=1), r=[t_], w=[t_])
        pl(lambda v3=v3, bs=bs: G.affine_select(v3, v3, [[bs, 128 // bs], [0, bs]], ALU.is_ge, 0.0, base=bs - 1,
                                                channel_multiplier=-1), r=[t_], w=[t_])
    pl(lambda: G.tensor_scalar(off2[:], off1[:], -1.0, 1.0, ALU.mult, ALU.add), r=[off1], w=[off2])
    pl(lambda: G.tensor_tensor(off1[:], off1[:], bd32[:], ALU.subtract), r=[off1, bd32], w=[off1])
    negm = sb("negm", [128, 128])
    pl(lambda: G.memset(negm[:], 0.0), w=[negm])
    pl(lambda: G.affine_select(negm[:], negm[:], [[1, 128]], ALU.is_ge, -1e30, base=0, channel_multiplier=-1),
       r=[negm], w=[negm])
    m_lower = sb("m_lower", [128, 128])
    pl(lambda: G.memset(m_lower[:], 1.0), w=[m_lower])
    pl(lambda: G.affine_select(m_lower[:], m_lower[:], [[-1, 128]], ALU.is_ge, 0.0, base=0, channel_multiplier=1),
       r=[m_lower], w=[m_lower])


    pT = psb("pT", (128, 1024), BF16)
    pb = [psb(f"pb{i}") for i in range(7)]

    cmu = sb("cmu", [128, 8]); cmu1 = sb("cmu1", [128, 8])
    cw0 = sb("cw0", [128, 2]); ca0 = sb("ca0", [128, 2]); ckk = sb("ckk", [128, 2]); cka = sb("cka", [128, 2])
    cka1 = sb("cka1", [128, 2]); crk = sb("crk", [128, 2])
    w2t = sb("w2t", [128, 256]); a2t = sb("a2t", [128, 256]); g2t = sb("g2t", [128, 256])
    lngb = sb("lngb", [128, 512])
    qkg = sb("qkg", [128, 128])
    esink = sb("esink", [128, 4])
    cconvw = sb("cconvw", [128, 4, 8]); cconvb = sb("cconvb", [128, 8])
    r8 = sb("r8", [128, 24])
    sng = sb("sng", [128, 512])
    fcw = sb("fcw", [128, 3, NCH_FF]); fcb = sb("fcb", [128, NCH_FF])
    ostg = sb("ostg", [64, 512])
    reg["mark"] = reg["off"]
    NR = NS + 1
    modrows_s = dscr("modrows_s", [L, NS, 6 * D])

    def ada_layer(l):
        phase_reset()
        c_sb = sb("c_sb", [NR, D])
        cT = sb("cT", [128, 8, NR])
        mod_s = sb("mod_s", [NR, 6 * D])
        adab = sb("adab", [NR, 6 * D])
        adaw = [sb(f"adaw{i}", [128, 8, 512]) for i in range(2)]
        dma(c_sb[:], c_all[:, :], r=[c_all], w=[c_sb])
        ac(lambda: S.activation(c_sb[:], c_sb[:], AF.Silu), r=[c_sb], w=[c_sb])
        for kc in range(8):
            mm(lambda kc=kc: T.transpose(pb[0][:, kc * NR:(kc + 1) * NR], c_sb[:, kc * 128:(kc + 1) * 128],
                                         ident[0:NR, 0:NR]), r=[c_sb, ident], w=[pb[0]])
        dv(lambda: V.tensor_copy(cT[:].rearrange("p k n -> p (k n)"), pb[0][:, 0:8 * NR]), r=[pb[0]], w=[cT])
        dma(adab[:], Wd['ada_b'][l:l + 1, :].partition_broadcast(NR), r=[Wd['ada_b']], w=[adab])
        for gi in range(12):
            wt = adaw[gi % 2]
            dma(wt[:], Wd['ada_w'][l, :, gi * 512:(gi + 1) * 512].rearrange("(k p) n -> p k n", p=128),
                r=[Wd['ada_w']], w=[wt], q=fw.sp)
            ps = pb[gi % 2]
            for kc in range(8):
                mm(lambda kc=kc, wt=wt, ps=ps: T.matmul(ps[0:NR, :], cT[:, kc, :], wt[:, kc, :], start=(kc == 0),
                                                        stop=(kc == 7)), r=[cT, wt], w=[ps], inc=(kc == 7), dense=True)
            dv(lambda gi=gi, ps=ps: V.tensor_tensor(mod_s[:, gi * 512:(gi + 1) * 512], ps[0:NR, :],
                                                    adab[:, gi * 512:(gi + 1) * 512], ALU.add),
               r=[ps, adab], w=[mod_s])
        dma(modrow[l:l + 1, :], mod_s[NS:NS + 1, :], r=[mod_s], w=[modrow])
        dma(modrows_s[l], mod_s[0:NS, :], r=[mod_s], w=[modrows_s])

    def load_mod(l, base, gname):
        modp = sb("modp", [128, 3 * D])
        gtmp = sb("gtmp", [128, D], alias=hf)
        dma(modp[:], modrow[l:l + 1, base:base + 3 * D].partition_broadcast(128), r=[modrow], w=[modp])
        dma(gtmp[:], Wd[gname][l:l + 1, :].partition_broadcast(128), r=[Wd[gname]], w=[gtmp])
        dv(lambda: V.scalar_tensor_tensor(modp[:, D:2 * D], modp[:, D:2 * D], 1.0, gtmp[:], ALU.add, ALU.mult),
           r=[modp, gtmp], w=[modp])
        return modp

    def load_consts(l):
        W = Wd
        dma(cmu[:], W['rwkv_mu'][l, :].rearrange("(k p) -> p k", p=128), r=[W['rwkv_mu']], w=[cmu],
            allow_slow_non_contiguous=True)
        dv(lambda: V.tensor_scalar(cmu1[:], cmu[:], -1.0, 1.0, ALU.mult, ALU.add), r=[cmu], w=[cmu1])
        for t_, nm in ((cw0, 'rwkv_w0'), (ca0, 'rwkv_a0'), (ckk, 'rwkv_k_k'), (cka, 'rwkv_k_a'), (crk, 'rwkv_r_k')):
            dma(t_[:], W[nm][l, :].rearrange("(k p) -> p k", p=128), r=[W[nm]], w=[t_],
                allow_slow_non_contiguous=True)
        dv(lambda: V.tensor_scalar(cka1[:], cka[:], -1.0, 1.0, ALU.mult, ALU.add), r=[cka], w=[cka1])
        dma(w2t[0:64, :], W['rwkv_w2'][l, :, :], r=[W['rwkv_w2']], w=[w2t])
        dma(a2t[64:128, :], W['rwkv_a2'][l, :, :], r=[W['rwkv_a2']], w=[a2t])
        dma(a2t[0:64, :], W['rwkv_a2'][l, :, :], r=[W['rwkv_a2']], w=[a2t])
        dma(g2t[:], W['rwkv_g2'][l, :, :], r=[W['rwkv_g2']], w=[g2t])
        dma(lngb[:, 0:256], W['rwkv_ln_g'][l:l + 1, :].partition_broadcast(128), r=[W['rwkv_ln_g']], w=[lngb])
        dma(lngb[:, 256:512], W['rwkv_ln_b'][l:l + 1, :].partition_broadcast(128), r=[W['rwkv_ln_b']], w=[lngb])
        dma(qkg[:, 0:64], W['attn_q_norm_g'][l:l + 1, :].partition_broadcast(128), r=[W['attn_q_norm_g']], w=[qkg])
        dma(qkg[:, 64:128], W['attn_k_norm_g'][l:l + 1, :].partition_broadcast(128), r=[W['attn_k_norm_g']],
            w=[qkg])
        dma(esink[:], W['attn_sinks'][l:l + 1, :].partition_broadcast(128), r=[W['attn_sinks']], w=[esink])
        ac(lambda: S.activation(esink[:], esink[:], AF.Exp), r=[esink], w=[esink])
        dma(cconvw[:], W['ssm_conv_w'][l, :, :].rearrange("j (k p) -> p j k", p=128), r=[W['ssm_conv_w']],
            w=[cconvw], allow_slow_non_contiguous=True)
        dma(cconvb[:], W['ssm_conv_b'][l, :].rearrange("(k p) -> p k", p=128), r=[W['ssm_conv_b']], w=[cconvb],
            allow_slow_non_contiguous=True)
        dma(r8[:, 0:8], W['ssm_dt_bias'][l:l + 1, :].partition_broadcast(128), r=[W['ssm_dt_bias']], w=[r8])
        dma(r8[:, 8:16], W['ssm_a_log'][l:l + 1, :].partition_broadcast(128), r=[W['ssm_a_log']], w=[r8])
        dma(r8[:, 16:24], W['ssm_d'][l:l + 1, :].partition_broadcast(128), r=[W['ssm_d']], w=[r8])
        ac(lambda: S.activation(r8[:, 8:16], r8[:, 8:16], AF.Exp), r=[r8], w=[r8])
        dv(lambda: V.tensor_scalar(r8[:, 8:16], r8[:, 8:16], -1.0, None, ALU.mult), r=[r8], w=[r8])
        dma(sng[:], W['ssm_norm_g'][l:l + 1, :].partition_broadcast(128), r=[W['ssm_norm_g']], w=[sng])
        dma(fcw[:], W['ffn_conv_w'][l, :, :].rearrange("j (k p) -> p j k", p=128), r=[W['ffn_conv_w']], w=[fcw],
            allow_slow_non_contiguous=True)
        dma(fcb[:], W['ffn_conv_b'][l, :].rearrange("(k p) -> p k", p=128), r=[W['ffn_conv_b']], w=[fcb],
            allow_slow_non_contiguous=True)

    def rmsnorm_mod(xtile, out_bf):
        ac(lambda: S.activation(junk[:], xtile[:], AF.Square, accum_out=ss[:, 0:1]), r=[xtile], w=[junk, ss])
        dv(lambda: V.tensor_scalar(ss[:, 1:2], ss[:, 0:1], 1.0 / D, EPS, ALU.mult, ALU.add), r=[ss], w=[ss])
        ac(lambda: S.activation(ss[:, 1:2], ss[:, 1:2], AF.Sqrt), r=[ss], w=[ss])
        dv(lambda: V.reciprocal(ss[:, 2:3], ss[:, 1:2]), r=[ss], w=[ss])
        dv(lambda: V.scalar_tensor_tensor(hf[:], xtile[:], ss[:, 2:3], modp[:, D:2 * D], ALU.mult, ALU.mult),
           r=[xtile, ss, modp], w=[hf])
        dv(lambda: V.tensor_tensor(out_bf[:], hf[:], modp[:, 0:D], ALU.add), r=[hf, modp], w=[out_bf])

    def to_fm(src_bf, dst):
        for kc in range(8):
            mm(lambda kc=kc: T.transpose(pT[:, kc * 128:(kc + 1) * 128], src_bf[:, kc * 128:(kc + 1) * 128], identb[:]),
               r=[src_bf, identb], w=[pT])
        ac(lambda: S.copy(dst[:].rearrange("p k n -> p (k n)"), pT[:]), r=[pT], w=[dst])

    xs_d = dscr("xs_d", [NS, D])
    vec_d = dscr("vec_d", [NS, 4, 6, 64])
    yd_d = dscr("yd_d", [NS * 4, 64])
    qkv_d = dscr("qkv_d", [NS, 512])
    od_d = dscr("od_d", [2 * NS, 128])
    ssd_d = dscr("ssd_d", [NS, 8, 321])
    ysd_d = dscr("ysd_d", [NS * 8, 64])
    N2 = 2 * NS

    def drow(dst_ap, dst_t, src_t, src_ap, n=NS):
        dma(dst_ap, src_ap.partition_broadcast(n), r=[src_t], w=[dst_t])

    def d_norm_mod(l, base, gname, xsd, hbd):
        modd = sb("modd", [NS, 3 * D]); hfd = sb("hfd", [NS, D]); gt = sb("gt", [NS, D], alias=hfd); sd = sb("sd", [NS, 8])
        jk = sb("jk", [NS, D], BF16, alias=hbd)
        dma(modd[:], modrows_s[l, :, base:base + 3 * D], r=[modrows_s], w=[modd])
        drow(gt[:], gt, Wd[gname], Wd[gname][l:l + 1, :])
        dv(lambda: V.scalar_tensor_tensor(modd[:, D:2 * D], modd[:, D:2 * D], 1.0, gt[:], ALU.add, ALU.mult),
           r=[modd, gt], w=[modd])
        ac(lambda: S.activation(jk[:], xsd[:], AF.Square, accum_out=sd[:, 0:1]), r=[xsd], w=[jk, sd])
        dv(lambda: V.tensor_scalar(sd[:, 1:2], sd[:, 0:1], 1.0 / D, EPS, ALU.mult, ALU.add), r=[sd], w=[sd])
        ac(lambda: S.activation(sd[:, 1:2], sd[:, 1:2], AF.Sqrt), r=[sd], w=[sd])
        dv(lambda: V.reciprocal(sd[:, 2:3], sd[:, 1:2]), r=[sd], w=[sd])
        dv(lambda: V.scalar_tensor_tensor(hfd[:], xsd[:], sd[:, 2:3], modd[:, D:2 * D], ALU.mult, ALU.mult),
           r=[xsd, sd, modd], w=[hfd])
        dv(lambda: V.tensor_tensor(hbd[:], hfd[:], modd[:, 0:D], ALU.add), r=[hfd, modd], w=[hbd])
        return modd, hfd

    def d_to_fm(src_bf, dst, nch):
        for c0 in range(0, nch, 8):
            n = min(8, nch - c0)
            for kc in range(n):
                mm(lambda kc=kc, c0=c0: T.transpose(pT[:, kc * NS:(kc + 1) * NS], src_bf[:, (c0 + kc) * 128:(c0 + kc + 1) * 128],
                                                    identb[0:NS, 0:NS]), r=[src_bf, identb], w=[pT])
            ac(lambda c0=c0, n=n: S.copy(dst[:, c0:c0 + n, :], pT[:, 0:n * NS].rearrange("p (k n) -> p k n", k=n)),
               r=[pT], w=[dst])

    def d_linear(xT, nk, wv, wt_, c0, ncols, dst, dcol):
        done = 0
        gi = 0
        while done < ncols:
            n = min(512, ncols - done)
            ps = pb[gi % 2]
            for kc in range(nk):
                mm(lambda kc=kc, ps=ps, n=n, done=done: T.matmul(ps[0:NS, 0:n], xT[:, kc, :],
                                                                 wv[:, kc, c0 + done:c0 + done + n], start=(kc == 0),
                                                                 stop=(kc == nk - 1)), r=[xT, wt_], w=[ps],
                   inc=(kc == nk - 1), dense=True)
            if dst is not None:
                ac(lambda ps=ps, n=n, done=done: S.copy(dst[:, dcol + done:dcol + done + n], ps[0:NS, 0:n]), r=[ps], w=[dst])
            done += n
            gi += 1

    def d_residual(xsd, modd, hfd, ps_list):
        for n2, ps in enumerate(ps_list):
            dv(lambda n2=n2, ps=ps: V.tensor_tensor(hfd[:, n2 * 512:(n2 + 1) * 512], ps[0:NS, :],
                                                    modd[:, 2 * D + n2 * 512:2 * D + (n2 + 1) * 512], ALU.mult),
               r=[ps, modd], w=[hfd])
        dv(lambda: V.tensor_tensor(xsd[:], xsd[:], hfd[:], ALU.add), r=[xsd, hfd], w=[xsd])

    def decode_mixer(l):
        W = Wd
        phase_reset()
        wmix = sb("wmix", [128, 8 * PROJ + 8 * D], BF16)
        w_in_v = wmix[:, 0:8 * PROJ].rearrange("p (k n) -> p k n", k=8)
        w_out_v = wmix[:, 8 * PROJ:8 * PROJ + 8 * D].rearrange("p (k n) -> p k n", k=8)
        fw.dma(fw.pool, w_in_v, W['w_in'][l].rearrange("(k p) n -> p k n", p=128), reads=[W['w_in']], writes=[wmix])
        fw.dma(fw.pool, w_out_v, W['w_out'][l].rearrange("(k p) n -> p k n", p=128), reads=[W['w_out']], writes=[wmix])
        xsd = sb("xsd", [NS, D]); hbd = sb("hbd", [NS, D], BF16); hTd = sb("hTd", [128, 8, NS], BF16)
        Pd = sb("Pd", [NS, PROJ]); ymd = sb("ymd", [NS, D])
        src = x_s if l == 0 else xs_d
        dma(xsd[:], src[:, :], r=[src], w=[xsd])
        modd, hfd = d_norm_mod(l, 0, 'norm_mix_g', xsd, hbd)
        d_to_fm(hbd, hTd, 8)
        d_linear(hTd, 8, w_in_v, wmix, 0, PROJ, Pd, 0)
        rows = sb("rows", [NS, 1024 + 5 * 256])
        drow(rows[:, 0:1024], rows, W['rwkv_mu'], W['rwkv_mu'][l:l + 1, :])
        for j, nm in enumerate(('rwkv_w0', 'rwkv_a0', 'rwkv_k_k', 'rwkv_k_a', 'rwkv_r_k')):
            drow(rows[:, 1024 + j * 256:1024 + (j + 1) * 256], rows, W[nm], W[nm][l:l + 1, :])
        R0 = 1024
        prv = sb("prv", [NS, 1024]); xq = sb("xq", [NS, 1024])
        dma(prv[:], st_shift[l], r=[st_shift], w=[prv])
        dma(o_sshift[l], Pd[:, 0:1024], r=[Pd], w=[o_sshift])
        dv(lambda: V.tensor_tensor(prv[:], prv[:], Pd[:, 0:1024], ALU.subtract), r=[prv, Pd], w=[prv])
        dv(lambda: V.tensor_tensor(prv[:], prv[:], rows[:, 0:1024], ALU.mult), r=[prv, rows], w=[prv])
        dv(lambda: V.tensor_tensor(xq[:], prv[:], Pd[:, 0:1024], ALU.add), r=[prv, Pd], w=[xq])
        lsg = sb("lsg", [NS, 256]); lT = sb("lT", [128, 3 * NS])
        ac(lambda: S.activation(lsg[:, 0:64], xq[:, 768:832], AF.Tanh), r=[xq], w=[lsg])
        ac(lambda: S.copy(lsg[:, 64:128], xq[:, 832:896]), r=[xq], w=[lsg])
        ac(lambda: S.activation(lsg[:, 128:256], xq[:, 896:1024], AF.Sigmoid), r=[xq], w=[lsg])
        mm(lambda: T.transpose(pb[2][0:64, 0:NS], lsg[:, 0:64], ident[0:NS, 0:NS]), r=[lsg, ident], w=[pb[2]])
        mm(lambda: T.transpose(pb[2][0:64, NS:2 * NS], lsg[:, 64:128], ident[0:NS, 0:NS]), r=[lsg, ident], w=[pb[2]])
        mm(lambda: T.transpose(pb[2][:, 2 * NS:3 * NS], lsg[:, 128:256], ident[0:NS, 0:NS]), r=[lsg, ident], w=[pb[2]])
        dv(lambda: V.tensor_copy(lT[0:64, 0:2 * NS], pb[2][0:64, 0:2 * NS]), r=[pb[2]], w=[lT])
        dv(lambda: V.tensor_copy(lT[:, 2 * NS:3 * NS], pb[2][:, 2 * NS:3 * NS]), r=[pb[2]], w=[lT])
        mm(lambda: T.matmul(pb[3][0:NS, 0:256], lT[0:64, 0:NS], w2t[0:64, :], start=True, stop=True), r=[lT, w2t], w=[pb[3]])
        mm(lambda: T.matmul(pb[3][0:NS, 256:512], lT[0:64, NS:2 * NS], a2t[0:64, :], start=True, stop=True), r=[lT, a2t],
           w=[pb[3]])
        mm(lambda: T.matmul(pb[4][0:NS, 0:256], lT[:, 2 * NS:3 * NS], g2t[:], start=True, stop=True), r=[lT, g2t], w=[pb[4]])
        v6 = sb("v6", [NS, 6, 256])
        aa_ = sb("aa_", [NS, 256]); kk_ = sb("kk_", [NS, 256]); gate_ = sb("gate_", [NS, 256]); t_ = sb("t_", [NS, 256])
        s4 = sb("s4", [NS, 16])
        dv(lambda: V.tensor_tensor(t_[:], pb[3][0:NS, 0:256], rows[:, R0:R0 + 256], ALU.add), r=[pb[3], rows], w=[t_])
        ac(lambda: S.activation(t_[:], t_[:], AF.Sigmoid), r=[t_], w=[t_])
        ac(lambda: S.activation(v6[:, 3, :], t_[:], AF.Exp, scale=-math.exp(-0.5)), r=[t_], w=[v6])
        dv(lambda: V.tensor_tensor(aa_[:], pb[3][0:NS, 256:512], rows[:, R0 + 256:R0 + 512], ALU.add), r=[pb[3], rows],
           w=[aa_])
        ac(lambda: S.activation(aa_[:], aa_[:], AF.Sigmoid), r=[aa_], w=[aa_])
        ac(lambda: S.copy(gate_[:], pb[4][0:NS, 0:256]), r=[pb[4]], w=[gate_])
        dv(lambda: V.tensor_tensor(kk_[:], xq[:, 256:512], rows[:, R0 + 512:R0 + 768], ALU.mult), r=[xq, rows], w=[kk_])
        dv(lambda: V.tensor_tensor(t_[:], kk_[:], kk_[:], ALU.mult), r=[kk_], w=[t_])
        dv(lambda: V.tensor_reduce(s4[:, 0:4], t_[:].rearrange("p (h d) -> p h d", h=4), AX.X, ALU.add), r=[t_], w=[s4])
        ac(lambda: S.activation(s4[:, 0:4], s4[:, 0:4], AF.Sqrt), r=[s4], w=[s4])
        dv(lambda: V.tensor_scalar(s4[:, 0:4], s4[:, 0:4], 1e-12, None, ALU.max), r=[s4], w=[s4])
        dv(lambda: V.reciprocal(s4[:, 0:4], s4[:, 0:4]), r=[s4], w=[s4])
        dv(lambda: V.tensor_tensor(kk_[:].rearrange("p (h d) -> p h d", h=4), kk_[:].rearrange("p (h d) -> p h d", h=4),
                                   s4[:, 0:4].unsqueeze(2).to_broadcast([NS, 4, 64]), ALU.mult), r=[kk_, s4], w=[kk_])
        dv(lambda: V.tensor_tensor(t_[:], aa_[:], rows[:, R0 + 768:R0 + 1024], ALU.mult), r=[aa_, rows], w=[t_])
        dv(lambda: V.tensor_tensor(t_[:], t_[:], rows[:, R0 + 768:R0 + 1024], ALU.subtract), r=[t_, rows], w=[t_])
        dv(lambda: V.scalar_tensor_tensor(v6[:, 1, :], t_[:], 1.0, xq[:, 256:512], ALU.add, ALU.mult), r=[t_, xq], w=[v6])
        ac(lambda: S.copy(v6[:, 0, :], xq[:, 0:256]), r=[xq], w=[v6])
        ac(lambda: S.copy(v6[:, 2, :], xq[:, 512:768]), r=[xq], w=[v6])
        dv(lambda: V.tensor_scalar(v6[:, 4, :], kk_[:], -1.0, None, ALU.mult), r=[kk_], w=[v6])
        dv(lambda: V.tensor_tensor(v6[:, 5, :], kk_[:], aa_[:], ALU.mult), r=[kk_, aa_], w=[v6])
        for j in range(6):
            dma(vec_d[:, :, j, :], v6[:, j, :].rearrange("p (h d) -> p h d", h=4), r=[v6], w=[vec_d])
        dbig = sb("dbig", [128, 10368])
        Sd = dbig[0:64, 0:4096].rearrange("p (v k) -> p v k", v=64)
        tmpS = dbig[0:64, 4096:8192].rearrange("p (v k) -> p v k", v=64)
        vv = sb("vv", [64, 6, 64]); sa = sb("sa", [64, 64]); yv = sb("yv", [64, 64])
        dma(vv[:], vec_d[:].rearrange("n h j d -> (n h) j d"), r=[vec_d], w=[vv])
        dma(dbig[0:64, 0:4096], st_wkv[l], r=[st_wkv], w=[dbig])

        def bv(j):
            return vv[:, j, :].unsqueeze(1).to_broadcast([64, 64, 64])

        def bk(ap2):
            return ap2.unsqueeze(2).to_broadcast([64, 64, 64])
        dv(lambda: V.tensor_tensor(tmpS, Sd, bv(4), ALU.mult), r=[dbig, vv], w=[dbig])
        dv(lambda: V.tensor_reduce(sa[:], tmpS, AX.X, ALU.add), r=[dbig], w=[sa])
        dv(lambda: V.tensor_tensor(Sd, Sd, bv(3), ALU.mult), r=[dbig, vv], w=[dbig])
        dv(lambda: V.tensor_tensor(tmpS, bk(sa[:]), bv(5), ALU.mult), r=[sa, vv], w=[dbig])
        dv(lambda: V.tensor_tensor(Sd, Sd, tmpS, ALU.add), r=[dbig], w=[dbig])
        dv(lambda: V.tensor_tensor(tmpS, bk(vv[:, 2, :]), bv(1), ALU.mult), r=[vv], w=[dbig])
        dv(lambda: V.tensor_tensor(Sd, Sd, tmpS, ALU.add), r=[dbig], w=[dbig])
        dv(lambda: V.tensor_tensor(tmpS, Sd, bv(0), ALU.mult), r=[dbig, vv], w=[dbig])
        dv(lambda: V.tensor_reduce(yv[:], tmpS, AX.X, ALU.add), r=[dbig], w=[yv])
        dma(o_swkv[l], dbig[0:64, 0:4096], r=[dbig], w=[o_swkv])
        dma(yd_d[:, :], yv[:], r=[yv], w=[yd_d])
        yr = sb("yr", [NS, 256]); y2 = sb("y2", [NS, 256])
        dma(yr[:], yd_d[:].rearrange("(n h) d -> n (h d)", h=4), r=[yd_d], w=[yr])
        y3 = yr[:].rearrange("p (h d) -> p h d", h=4)
        dv(lambda: V.tensor_reduce(s4[:, 0:4], y3, AX.X, ALU.add), r=[yr], w=[s4])
        dv(lambda: V.tensor_tensor(y2[:], yr[:], yr[:], ALU.mult), r=[yr], w=[y2])
        dv(lambda: V.tensor_reduce(s4[:, 4:8], y2[:].rearrange("p (h d) -> p h d", h=4), AX.X, ALU.add), r=[y2], w=[s4])
        dv(lambda: V.tensor_scalar(s4[:, 0:8], s4[:, 0:8], 1.0 / 64, None, ALU.mult), r=[s4], w=[s4])
        dv(lambda: V.tensor_tensor(s4[:, 8:12], s4[:, 0:4], s4[:, 0:4], ALU.mult), r=[s4], w=[s4])
        dv(lambda: V.tensor_tensor(s4[:, 8:12], s4[:, 4:8], s4[:, 8:12], ALU.subtract), r=[s4], w=[s4])
        dv(lambda: V.tensor_scalar(s4[:, 8:12], s4[:, 8:12], GN_EPS, None, ALU.add), r=[s4], w=[s4])
        ac(lambda: S.activation(s4[:, 8:12], s4[:, 8:12], AF.Sqrt), r=[s4], w=[s4])
        dv(lambda: V.reciprocal(s4[:, 12:16], s4[:, 8:12]), r=[s4], w=[s4])
        dv(lambda: V.tensor_tensor(y3, y3, s4[:, 0:4].unsqueeze(2).to_broadcast([NS, 4, 64]), ALU.subtract), r=[yr, s4], w=[yr])
        dv(lambda: V.tensor_tensor(y3, y3, s4[:, 12:16].unsqueeze(2).to_broadcast([NS, 4, 64]), ALU.mult), r=[yr, s4], w=[yr])
        dv(lambda: V.tensor_tensor(yr[:], yr[:], lngb[0:NS, 0:256], ALU.mult), r=[yr, lngb], w=[yr])
        dv(lambda: V.tensor_tensor(yr[:], yr[:], lngb[0:NS, 256:512], ALU.add), r=[yr, lngb], w=[yr])
        dv(lambda: V.tensor_tensor(y2[:], v6[:, 0, :], v6[:, 1, :], ALU.mult), r=[v6], w=[y2])
        dv(lambda: V.tensor_tensor(y2[:], y2[:], rows[:, R0 + 1024:R0 + 1280], ALU.mult), r=[y2, rows], w=[y2])
        dv(lambda: V.tensor_reduce(s4[:, 0:4], y2[:].rearrange("p (h d) -> p h d", h=4), AX.X, ALU.add), r=[y2], w=[s4])
        dv(lambda: V.tensor_tensor(y2[:].rearrange("p (h d) -> p h d", h=4), v6[:, 2, :].rearrange("p (h d) -> p h d", h=4),
                                   s4[:, 0:4].unsqueeze(2).to_broadcast([NS, 4, 64]), ALU.mult), r=[v6, s4], w=[y2])
        dv(lambda: V.tensor_tensor(yr[:], yr[:], y2[:], ALU.add), r=[yr, y2], w=[yr])
        dv(lambda: V.tensor_tensor(ymd[:, 0:256], yr[:], gate_[:], ALU.mult), r=[yr, gate_], w=[ymd])
        qd = sb("qd", [NS, 512], alias=prv); s8 = sb("s8", [NS, 8])
        dv(lambda: V.tensor_tensor(qd[:, 0:384], Pd[:, 1024:1408], Pd[:, 1024:1408], ALU.mult), r=[Pd], w=[qd])
        dv(lambda: V.tensor_reduce(s8[:, 0:6], qd[:, 0:384].rearrange("p (h d) -> p h d", h=6), AX.X, ALU.add), r=[qd], w=[s8])
        dv(lambda: V.tensor_scalar(s8[:, 0:6], s8[:, 0:6], 1.0 / 64, EPS, ALU.mult, ALU.add), r=[s8], w=[s8])
        ac(lambda: S.activation(s8[:, 0:6], s8[:, 0:6], AF.Sqrt), r=[s8], w=[s8])
        dv(lambda: V.reciprocal(s8[:, 0:6], s8[:, 0:6]), r=[s8], w=[s8])
        dv(lambda: V.tensor_tensor(qd[:, 0:384].rearrange("p (h d) -> p h d", h=6),
                                   Pd[:, 1024:1408].rearrange("p (h d) -> p h d", h=6),
                                   s8[:, 0:6].unsqueeze(2).to_broadcast([NS, 6, 64]), ALU.mult), r=[Pd, s8], w=[qd])
        dv(lambda: V.tensor_tensor(qd[:, 0:256].rearrange("p (h d) -> p h d", h=4), qd[:, 0:256].rearrange("p (h d) -> p h d", h=4),
                                   qkg[0:NS, 0:64].unsqueeze(1).to_broadcast([NS, 4, 64]), ALU.mult), r=[qd, qkg], w=[qd])
        dv(lambda: V.tensor_tensor(qd[:, 256:384].rearrange("p (h d) -> p h d", h=2),
                                   qd[:, 256:384].rearrange("p (h d) -> p h d", h=2),
                                   qkg[0:NS, 64:128].unsqueeze(1).to_broadcast([NS, 2, 64]), ALU.mult), r=[qd, qkg], w=[qd])
        ac(lambda: S.copy(qd[:, 384:512], Pd[:, 1408:1536]), r=[Pd], w=[qd])
        dma(qkv_d[:, :], qd[:], r=[qd], w=[qkv_d])
        q2 = sb("q2", [N2, 128]); esd = sb("esd", [N2, 2]); sc = sb("sc", [N2, 2, 129]); dn = sb("dn", [N2, 4])
        o2 = sb("o2", [N2, 128])
        KF = dbig[0:N2, 0:129 * 64].rearrange("p (s d) -> p s d", s=129)
        TM_ = dbig[0:N2, 8256:8256 + 2112]
        for kh in range(2):
            ps_ = slice(kh * NS, (kh + 1) * NS)
            dma(q2[ps_, :], qkv_d[:, kh * 128:(kh + 1) * 128], r=[qkv_d], w=[q2])
            dma(esd[ps_, :], W['attn_sinks'][l:l + 1, 2 * kh:2 * kh + 2].partition_broadcast(NS), r=[W['attn_sinks']], w=[esd])
        ac(lambda: S.activation(esd[:], esd[:], AF.Exp), r=[esd], w=[esd])
        for which, (st_c, off_new, o_c) in enumerate(((st_k, 256, o_sk), (st_v, 384, o_sv))):
            for kh in range(2):
                ps_ = slice(kh * NS, (kh + 1) * NS)
                dma(KF[ps_, 0:128, :], st_c[l, :, :, kh, :], r=[st_c], w=[dbig])
                dma(KF[ps_, 128, :], qkv_d[:, off_new + kh * 64:off_new + (kh + 1) * 64], r=[qkv_d], w=[dbig])
            dma(o_c[l, :, 0:127, :, :], st_c[l, :, 1:128, :, :], r=[st_c], w=[o_c])
            dma(o_c[l, :, 127, :, :], qkv_d[:, off_new:off_new + 128].rearrange("n (k d) -> n k d", k=2), r=[qkv_d], w=[o_c])
            if which == 0:
                for g in range(2):
                    for (s0, s1) in ((0, 33), (33, 66), (66, 99), (99, 129)):
                        tv = TM_[:, 0:(s1 - s0) * 64].rearrange("p (s d) -> p s d", d=64)
                        dv(lambda g=g, s0=s0, s1=s1, tv=tv: V.tensor_tensor(
                            tv, KF[:, s0:s1, :], q2[:, g * 64:(g + 1) * 64].unsqueeze(1).to_broadcast([N2, s1 - s0, 64]),
                            ALU.mult), r=[dbig, q2], w=[dbig])
                        dv(lambda g=g, s0=s0, s1=s1, tv=tv: V.tensor_reduce(sc[:, g, s0:s1], tv, AX.X, ALU.add), r=[dbig],
                           w=[sc])
                ac(lambda: S.activation(sc[:], sc[:], AF.Exp, scale=0.125), r=[sc], w=[sc])
                dv(lambda: V.tensor_reduce(dn[:, 0:2], sc[:], AX.X, ALU.add), r=[sc], w=[dn])
                dv(lambda: V.tensor_tensor(dn[:, 0:2], dn[:, 0:2], esd[:], ALU.add), r=[dn, esd], w=[dn])
                dv(lambda: V.reciprocal(dn[:, 2:4], dn[:, 0:2]), r=[dn], w=[dn])
                dv(lambda: V.tensor_tensor(sc[:], sc[:], dn[:, 2:4].unsqueeze(2).to_broadcast([N2, 2, 129]), ALU.mult),
                   r=[sc, dn], w=[sc])
            else:
                for g in range(2):
                    for (d0, d1) in ((0, 16), (16, 32), (32, 48), (48, 64)):
                        tv = TM_[:, 0:(d1 - d0) * 129].rearrange("p (d s) -> p d s", s=129)
                        dv(lambda g=g, d0=d0, d1=d1, tv=tv: V.tensor_tensor(
                            tv, KF[:, :, d0:d1].rearrange("p s d -> p d s"),
                            sc[:, g, :].unsqueeze(1).to_broadcast([N2, d1 - d0, 129]), ALU.mult), r=[dbig, sc], w=[dbig])
                        dv(lambda g=g, d0=d0, d1=d1, tv=tv: V.tensor_reduce(o2[:, g * 64 + d0:g * 64 + d1], tv, AX.X, ALU.add),
                           r=[dbig], w=[o2])
        dma(od_d[:, :], o2[:], r=[o2], w=[od_d])
        for kh in range(2):
            dma(ymd[:, 256 + kh * 128:256 + (kh + 1) * 128], od_d[kh * NS:(kh + 1) * NS, :], r=[od_d], w=[ymd])
        cbuf = sb("cbuf", [NS, 3, 256]); cw = sb("cw", [NS, 5, 256], alias=rows); cv = sb("cv", [NS, 1024], alias=prv)
        for j in range(2):
            dma(o_sconv[l, :, j, :], st_conv[l, :, j + 1, :], r=[st_conv], w=[o_sconv])
        dma(o_sconv[l, :, 2, :], Pd[:, 2048:3072], r=[Pd], w=[o_sconv])
        for cc in range(4):
            cs = slice(cc * 256, (cc + 1) * 256)
            dma(cbuf[:], st_conv[l, :, :, cs], r=[st_conv], w=[cbuf])
            for j in range(4):
                drow(cw[:, j, :], cw, W['ssm_conv_w'], W['ssm_conv_w'][l, j:j + 1, cs])
            drow(cw[:, 4, :], cw, W['ssm_conv_b'], W['ssm_conv_b'][l:l + 1, cs])
            dv(lambda cs=cs: V.tensor_tensor(cv[:, cs], Pd[:, 2048 + cs.start:2048 + cs.stop], cw[:, 3, :], ALU.mult),
               r=[Pd, cw], w=[cv])
            dv(lambda cs=cs: V.tensor_tensor(cv[:, cs], cv[:, cs], cw[:, 4, :], ALU.add), r=[cv, cw], w=[cv])
            for j in range(3):
                dv(lambda cs=cs, j=j: V.tensor_tensor(t_[:], cbuf[:, j, :], cw[:, j, :], ALU.mult), r=[cbuf, cw], w=[t_])
                dv(lambda cs=cs: V.tensor_tensor(cv[:, cs], cv[:, cs], t_[:], ALU.add), r=[cv, t_], w=[cv])
        ac(lambda: S.activation(cv[:], cv[:], AF.Silu), r=[cv], w=[cv])
        d8 = sb("d8", [NS, 24]); zsd = sb("zsd", [NS, 512], alias=xq); repb = sb("repb", [NS, 8, 128]); yc_ = sb("yc_", [NS, 512])
        dv(lambda: V.tensor_tensor(d8[:, 0:8], Pd[:, 3072:3080], r8[0:NS, 0:8], ALU.add), r=[Pd, r8], w=[d8])
        ac(lambda: S.activation(d8[:, 0:8], d8[:, 0:8], AF.Exp), r=[d8], w=[d8])
        ac(lambda: S.activation(d8[:, 0:8], d8[:, 0:8], AF.Ln, bias=1.0), r=[d8], w=[d8])
        dv(lambda: V.tensor_tensor(d8[:, 8:16], d8[:, 0:8], r8[0:NS, 8:16], ALU.mult), r=[d8, r8], w=[d8])
        ac(lambda: S.activation(d8[:, 16:24], d8[:, 8:16], AF.Exp), r=[d8], w=[d8])
        dma(ssd_d[:, :, 320], d8[:, 16:24], r=[d8], w=[ssd_d], allow_slow_non_contiguous=True)
        dv(lambda: V.tensor_tensor(yc_[:].rearrange("p (h d) -> p h d", h=8), cv[:, 0:512].rearrange("p (h d) -> p h d", h=8),
                                   d8[:, 0:8].unsqueeze(2).to_broadcast([NS, 8, 64]), ALU.mult), r=[cv, d8], w=[yc_])
        dma(ssd_d[:, :, 0:64], yc_[:].rearrange("p (h d) -> p h d", h=8), r=[yc_], w=[ssd_d])
        for (o_, c0) in ((64, 512), (192, 768)):
            for g in range(2):
                dv(lambda c0=c0, g=g: V.tensor_copy(
                    repb[:, g * 4:(g + 1) * 4, :],
                    cv[:, c0 + g * 128:c0 + (g + 1) * 128].unsqueeze(1).to_broadcast([NS, 4, 128])), r=[cv], w=[repb])
            dma(ssd_d[:, :, o_:o_ + 128], repb[:], r=[repb], w=[ssd_d])
        pv = sb("pv", [128, 321]); yh = sb("yh", [128, 64])
        dma(pv[:], ssd_d[:].rearrange("n h f -> (n h) f"), r=[ssd_d], w=[pv])
        dma(dbig[:, 0:8192], st_ssm[l], r=[st_ssm], w=[dbig])
        for (p0_, p1_) in ((0, 16), (16, 32), (32, 48), (48, 64)):
            Hh = dbig[:, p0_ * 128:p1_ * 128].rearrange("p (q s) -> p q s", s=128)
            Th = dbig[:, 8192:8192 + 2048].rearrange("p (q s) -> p q s", s=128)
            dv(lambda Th=Th, p0_=p0_, p1_=p1_: V.tensor_tensor(
                Th, pv[:, p0_:p1_].unsqueeze(2).to_broadcast([128, 16, 128]),
                pv[:, 64:192].unsqueeze(1).to_broadcast([128, 16, 128]), ALU.mult), r=[pv], w=[dbig])
            dv(lambda Hh=Hh, Th=Th: V.scalar_tensor_tensor(Hh, Hh, pv[:, 320:321], Th, ALU.mult, ALU.add), r=[dbig, pv],
               w=[dbig])
            dv(lambda Hh=Hh, Th=Th: V.tensor_tensor(Th, Hh, pv[:, 192:320].unsqueeze(1).to_broadcast([128, 16, 128]),
                                                    ALU.mult), r=[dbig, pv], w=[dbig])
            dv(lambda Th=Th, p0_=p0_, p1_=p1_: V.tensor_reduce(yh[:, p0_:p1_], Th, AX.X, ALU.add), r=[dbig], w=[yh])
        dma(o_sssm[l], dbig[:, 0:8192], r=[dbig], w=[o_sssm])
        dma(ysd_d[:, :], yh[:], r=[yh], w=[ysd_d])
        dma(yc_[:], ysd_d[:].rearrange("(n h) d -> n (h d)", h=8), r=[ysd_d], w=[yc_])
        dv(lambda: V.tensor_tensor(zsd[:].rearrange("p (h d) -> p h d", h=8), cv[:, 0:512].rearrange("p (h d) -> p h d", h=8),
                                   r8[0:NS, 16:24].unsqueeze(2).to_broadcast([NS, 8, 64]), ALU.mult), r=[cv, r8], w=[zsd])
        dv(lambda: V.tensor_tensor(yc_[:], yc_[:], zsd[:], ALU.add), r=[yc_, zsd], w=[yc_])
        ac(lambda: S.activation(zsd[:], Pd[:, 1536:2048], AF.Silu), r=[Pd], w=[zsd])
        dv(lambda: V.tensor_tensor(yc_[:], yc_[:], zsd[:], ALU.mult), r=[yc_, zsd], w=[yc_])
        ac(lambda: S.activation(zsd[:], yc_[:], AF.Square, accum_out=s8[:, 6:7]), r=[yc_], w=[zsd, s8])
        dv(lambda: V.tensor_scalar(s8[:, 6:7], s8[:, 6:7], 1.0 / 512, EPS, ALU.mult, ALU.add), r=[s8], w=[s8])
        ac(lambda: S.activation(s8[:, 6:7], s8[:, 6:7], AF.Sqrt), r=[s8], w=[s8])
        dv(lambda: V.reciprocal(s8[:, 7:8], s8[:, 6:7]), r=[s8], w=[s8])
        dv(lambda: V.scalar_tensor_tensor(ymd[:, 512:1024], yc_[:], s8[:, 7:8], sng[0:NS, :], ALU.mult, ALU.mult),
           r=[yc_, s8, sng], w=[ymd])
        dv(lambda: V.tensor_copy(hbd[:], ymd[:]), r=[ymd], w=[hbd])
        d_to_fm(hbd, hTd, 8)
        d_linear(hTd, 8, w_out_v, wmix, 0, D, None, 0)
        d_residual(xsd, modd, hfd, [pb[0], pb[1]])
        dma(xs_d[:, :], xsd[:], r=[xsd], w=[xs_d])

    def decode_ffn(l):
        W = Wd
        phase_reset()
        wffn = sb("wffn", [128, 8 * 5632 + 22 * 1024], BF16)
        w_up_v = wffn[:, 0:8 * 5632].rearrange("p (k n) -> p k n", k=8)
        w_dn_v = wffn[:, 8 * 5632:8 * 5632 + 22 * 1024].rearrange("p (k n) -> p k n", k=22)
        fw.dma(fw.pool, w_up_v, W['ffn_w_up'][l].rearrange("(k p) n -> p k n", p=128), reads=[W['ffn_w_up']], writes=[wffn])
        fw.dma(fw.pool, w_dn_v, W['ffn_w_down'][l].rearrange("(k p) n -> p k n", p=128), reads=[W['ffn_w_down']],
               writes=[wffn])
        xsd = sb("xsd", [NS, D]); hbd = sb("hbd", [NS, D], BF16); hTd = sb("hTd", [128, 8, NS], BF16)
        dma(xsd[:], xs_d[:, :], r=[xs_d], w=[xsd])
        modd, hfd = d_norm_mod(l, 3 * D, 'norm_ffn_g', xsd, hbd)
        d_to_fm(hbd, hTd, 8)
        gv = sb("gv", [NS, 2 * DFF])
        d_linear(hTd, 8, w_up_v, wffn, 0, 2 * DFF, gv, 0)
        fbuf = sb("fbuf", [NS, 2, 256]); fcw_ = sb("fcw_", [NS, 4, 256]); ft = sb("ft", [NS, 256])
        pbf = sb("pbf", [NS, DFF], BF16, alias=gv); pT_ = sb("pT_", [128, NCH_FF, NS], BF16)
        dma(o_sffn[l, :, 1, :], gv[:, 0:DFF], r=[gv], w=[o_sffn])
        dma(o_sffn[l, :, 0, :], st_ffn[l, :, 1, :], r=[st_ffn], w=[o_sffn])
        for c0 in range(0, DFF, 256):
            n = min(256, DFF - c0)
            dma(fbuf[:, :, 0:n], st_ffn[l, :, :, c0:c0 + n], r=[st_ffn], w=[fbuf])
            for j in range(3):
                drow(fcw_[:, j, 0:n], fcw_, W['ffn_conv_w'], W['ffn_conv_w'][l, j:j + 1, c0:c0 + n])
            drow(fcw_[:, 3, 0:n], fcw_, W['ffn_conv_b'], W['ffn_conv_b'][l:l + 1, c0:c0 + n])
            dv(lambda c0=c0, n=n: V.tensor_tensor(ft[:, 0:n], gv[:, c0:c0 + n], fcw_[:, 2, 0:n], ALU.mult), r=[gv, fcw_], w=[ft])
            dv(lambda n=n: V.tensor_tensor(ft[:, 0:n], ft[:, 0:n], fcw_[:, 3, 0:n], ALU.add), r=[ft, fcw_], w=[ft])
            for j in range(2):
                dv(lambda j=j, n=n: V.tensor_tensor(fbuf[:, j, 0:n], fbuf[:, j, 0:n], fcw_[:, j, 0:n], ALU.mult),
                   r=[fbuf, fcw_], w=[fbuf])
                dv(lambda j=j, n=n: V.tensor_tensor(ft[:, 0:n], ft[:, 0:n], fbuf[:, j, 0:n], ALU.add), r=[ft, fbuf], w=[ft])
            ac(lambda n=n: S.activation(ft[:, 0:n], ft[:, 0:n], AF.Silu), r=[ft], w=[ft])
            dv(lambda c0=c0, n=n: V.tensor_tensor(pbf[:, c0:c0 + n], ft[:, 0:n], gv[:, DFF + c0:DFF + c0 + n], ALU.mult),
               r=[ft, gv], w=[pbf])
        d_to_fm(pbf, pT_, NCH_FF)
        d_linear(pT_, NCH_FF, w_dn_v, wffn, 0, D, None, 0)
        d_residual(xsd, modd, hfd, [pb[0], pb[1]])
        dma(xs_d[:, :], xsd[:], r=[xsd], w=[xs_d])
        if l == L - 1:
            dma(y_s[:, :], xsd[:], r=[xsd], w=[y_s])

    try:
        for l in range(L):
            ada_layer(l)
            if stop == 'ada':
                break
            phase_reset()
            load_consts(l)
            xsrc = x_p if l == 0 else xb
            wmix = sb("wmix", [128, 8 * PROJ + 8 * D], BF16)
            w_in_v = wmix[:, 0:8 * PROJ].rearrange("p (k n) -> p k n", k=8)
            w_out_v = wmix[:, 8 * PROJ:8 * PROJ + 8 * D].rearrange("p (k n) -> p k n", k=8)
            fw.dma(fw.pool, w_in_v, Wd['w_in'][l].rearrange("(k p) n -> p k n", p=128), reads=[Wd['w_in']],
                   writes=[wmix])
            fw.dma(fw.pool, w_out_v, Wd['w_out'][l].rearrange("(k p) n -> p k n", p=128), reads=[Wd['w_out']],
                   writes=[wmix])
            xt = [sb("xt0", [128, D])]
            ss = sb("ss", [128, 8])
            hf = sb("hf", [128, D])
            hb = sb("hb", [128, D], BF16)
            hT = sb("hT", [128, 8, 128], BF16)
            junk = sb("junk", [128, D], BF16, alias=hT)
            paF = sb("paF", [128, 8, 129])
            xs = sb("xs", [128, 8, 128])
            tmpF = sb("tmpF", [128, 8, 128])
            xbcF = sb("xbcF", [128, 8, 131])
            cvF = sb("cvF", [128, 8, 128])
            bq = sb("bq", [128, 512])
            zs = sb("zs", [128, 512])
            dtr = sb("dtr", [128, 8])
            ymix = sb("ymix", [128, D])
            ymb = sb("ymb", [128, D], BF16, alias=hb)
            ymT = sb("ymT", [128, 8, 128], BF16, alias=hT)

            lwT = sb("lwT", [128, 128])
            sgl = sb("sgl", [128, 128])
            logd = sb("logd", [128, 2, 128]); aa = sb("aa", [128, 2, 128]); kk = sb("kk", [128, 2, 128])
            kmod = sb("kmod", [128, 2, 128]); cum = sb("cum", [128, 2, 128]); t2 = sb("t2", [128, 2, 128])
            ePt = sb("ePt", [128, 2, 128]); ePi = sb("ePi", [128, 2, 128]); ePm = sb("ePm", [128, 2, 128])
            ePe = sb("ePe", [128, 2, 128]); kka = sb("kka", [128, 2, 128])
            AR = sb("AR", [128, 2, 2, 128])
            BK = sb("BK", [128, 2, 2, 128])
            BhKh = sb("BhKh", [128, 2, 2, 128])
            rkr = sb("rkr", [128, 2, 128])
            tmq = sb("tmq", [128, 4, 4, 64])
            Am = sb("Am", [128, 4, 2, 256])
            Bm = [sb(f"Bm{i}", [128, 4, 128]) for i in range(2)]
            BmT = [sb(f"BmT{i}", [128, 4, 128]) for i in range(2)]
            XT = [sb(f"XT{i}", [128, 4, 128]) for i in range(2)]
            Xn = [sb(f"Xn{i}", [128, 4, 128]) for i in range(2)]
            Afl = sb("Afl", [128, 4, 128])
            AZ = sb("AZ", [128, 4, 128])
            WU = sb("WU", [128, 4, 256])
            MTN = sb("MTN", [64, 4, 128])
            QT = sb("QT", [64, 4, 128])
            ST = sb("ST", [64, 4, 64])
            dP = sb("dP", [64, 4])
            ywk = sb("ywk", [128, 256]); ysq = sb("ysq", [128, 256]); gst = sb("gst", [128, 16]); gateT = sb("gateT", [128, 256])
            rkb = sb("rkb", [128, 4])

            qn = sb("qn", [128, 384]); st8 = sb("st8", [128, 8])
            qT = sb("qT", [64, 4, 128])
            kT = [sb(f"kT{i}", [64, 2, 128]) for i in range(2)]
            vaug = [sb(f"vaug{i}", [128, 2, 65]) for i in range(2)]
            pexp = sb("pexp", [128, 2, 2, 256], alias=xs)

            xtok = sb("xtok", [128, 512], alias=Bm[0]); xdt = sb("xdt", [128, 512], alias=Bm[1]); dts = sb("dts", [128, 8]); das = sb("das", [128, 8])
            dabc = sb("dabc", [128, 8, 128], alias=tmpF); acum = sb("acum", [128, 8]); tot = sb("tot", [128, 16])
            seg = sb("seg", [128, 4, 128], alias=XT[0]); GT = sb("GT", [128, 8, 128], alias=Am); Cs = sb("Cs", [128, 8, 128], alias=WU)
            Btok = sb("Btok", [128, 256]); cbs = sb("cbs", [128, 256]); xdte = sb("xdte", [128, 512], alias=BmT[0]); hTs = sb("hTs", [128, 512])
            eac = sb("eac", [128, 8, 128], alias=tmq); yss = sb("yss", [128, 512], alias=BmT[1]); cdec = sb("cdec", [128, 16])

            modp = load_mod(l, 0, 'norm_mix_g')
            dv(lambda: V.memset(ST[:], 0.0), w=[ST])
            dv(lambda: V.memset(hTs[:], 0.0), w=[hTs])
            dv(lambda: V.memset(paF[:, :, 0:1], 0.0), w=[paF])
            dv(lambda: V.memset(xbcF[:, :, 0:3], 0.0), w=[xbcF])
            for i in range(2):
                dv(lambda i=i: V.memset(vaug[i][:], 1.0), w=[vaug[i]])

            for i in range(NT):
                cur['i'] = i
                xti = xt[0]
                dma(xti[:], xsrc[i * 128:(i + 1) * 128, :], r=[xsrc], w=[xti])
                rmsnorm_mod(xti, hb)
                to_fm(hb, hT)
                for (c0, dstT, off, pss) in ((0, paF, 1, (pb[0], pb[1])), (2048, xbcF, 3, (pb[2], pb[3]))):
                    for oc in range(8):
                        ps = pss[oc // 4]
                        for kc in range(8):
                            mm(lambda oc=oc, kc=kc, ps=ps, c0=c0: T.matmul(
                                ps[:, (oc % 4) * 128:(oc % 4 + 1) * 128],
                                w_in_v[:, kc, c0 + oc * 128:c0 + (oc + 1) * 128], hT[:, kc, :],
                                start=(kc == 0), stop=(kc == 7)), r=[wmix, hT], w=[ps], inc=(kc == 7 and oc % 4 == 3), dense=True)
                    for h2 in range(2):
                        ac(lambda h2=h2, dstT=dstT, off=off, pss=pss: S.copy(
                            dstT[:, h2 * 4:(h2 + 1) * 4, off:off + 128],
                            pss[h2][:].rearrange("p (k n) -> p k n", k=4)), r=[pss[h2]], w=[dstT])
                for (c0, n, ps) in ((1024, 512, pb[4]), (1536, 512, pb[5]), (3072, 8, pb[6])):
                    for kc in range(8):
                        mm(lambda kc=kc, c0=c0, n=n, ps=ps: T.matmul(ps[:, 0:n], hT[:, kc, :], w_in_v[:, kc, c0:c0 + n],
                                                                     start=(kc == 0), stop=(kc == 7)),
                           r=[wmix, hT], w=[ps], inc=(kc == 7), dense=True)
                dv(lambda: V.tensor_copy(bq[:], pb[4][:]), r=[pb[4]], w=[bq])
                ac(lambda: S.activation(zs[:], pb[5][:], AF.Silu), r=[pb[5]], w=[zs])
                dv(lambda: V.tensor_copy(dtr[:], pb[6][:, 0:8]), r=[pb[6]], w=[dtr])

                cp('projE')
                if stop == 'proj':
                    continue
                dv(lambda: V.tensor_tensor(tmpF[:], paF[:, :, 0:128], cmu[:].unsqueeze(2).to_broadcast([128, 8, 128]),
                                           ALU.mult), r=[paF, cmu], w=[tmpF])
                dv(lambda: V.tensor_tensor(xs[:], paF[:, :, 1:129], cmu1[:].unsqueeze(2).to_broadcast([128, 8, 128]),
                                           ALU.mult), r=[paF, cmu1], w=[xs])
                dv(lambda: V.tensor_tensor(xs[:], xs[:], tmpF[:], ALU.add), r=[xs, tmpF], w=[xs])
                cp('q0')
                if i == NT - 1:
                    mm(lambda: T.transpose(pb[0][0:8, 0:128], paF[:, :, 128], ident[:]), r=[paF, ident], w=[pb[0]])
                    dv(lambda: V.tensor_copy(ostg[0:8, 0:128], pb[0][0:8, 0:128]), r=[pb[0]], w=[ostg])
                    dma(o_pshift[l].rearrange("(k p) -> k p", p=128), ostg[0:8, 0:128], r=[ostg], w=[o_pshift])
                ac(lambda: S.copy(paF[:, :, 0:1], paF[:, :, 128:129]), r=[paF], w=[paF])
                cp('q1')
                ac(lambda: S.activation(lwT[0:64, :], xs[0:64, 6, :], AF.Tanh), r=[xs], w=[lwT])
                ac(lambda: S.activation(sgl[:], xs[:, 7, :], AF.Sigmoid), r=[xs], w=[sgl])
                cp('q2')
                for c in range(2):
                    mm(lambda c=c: T.matmul(pb[0][:, c * 128:(c + 1) * 128], w2t[0:64, c * 128:(c + 1) * 128], lwT[0:64, :],
                                            start=True, stop=True), r=[w2t, lwT], w=[pb[0]], inc=False)
                    mm(lambda c=c: T.matmul(pb[0][:, 256 + c * 128:256 + (c + 1) * 128],
                                            a2t[64:128, c * 128:(c + 1) * 128], xs[64:128, 6, :], start=True, stop=True),
                       r=[a2t, xs], w=[pb[0]], inc=(c == 1))
                mm(lambda: T.matmul(pb[1][:, 0:256], sgl[:], g2t[:], start=True, stop=True), r=[sgl, g2t], w=[pb[1]])
                cp('q3')
                for c in range(2):
                    ac(lambda c=c: S.activation(logd[:, c, :], pb[0][:, c * 128:(c + 1) * 128], AF.Sigmoid,
                                                bias=cw0[:, c:c + 1]), r=[pb[0], cw0], w=[logd])
                    ac(lambda c=c: S.activation(aa[:, c, :], pb[0][:, 256 + c * 128:256 + (c + 1) * 128], AF.Sigmoid,
                                                bias=ca0[:, c:c + 1]), r=[pb[0], ca0], w=[aa])
                cp('q4')
                ac(lambda: S.copy(gateT[:], pb[1][:, 0:256]), r=[pb[1]], w=[gateT])
                dv(lambda: V.tensor_scalar(logd[:], logd[:], -math.exp(-0.5), None, ALU.mult), r=[logd], w=[logd])
                cp('r1')
                dv(lambda: V.tensor_tensor(kk[:], xs[:, 2:4, :], ckk[:].unsqueeze(2).to_broadcast([128, 2, 128]), ALU.mult),
                   r=[xs, ckk], w=[kk])
                dv(lambda: V.tensor_tensor(t2[:], kk[:], kk[:], ALU.mult), r=[kk], w=[t2])
                mm(lambda: T.matmul(pb[1][:, 256:512], blk[:], t2[:].rearrange("p c n -> p (c n)"), start=True, stop=True),
                   r=[blk, t2], w=[pb[1]])
                ac(lambda: S.activation(t2[:].rearrange("p c n -> p (c n)"), pb[1][:, 256:512], AF.Sqrt), r=[pb[1]], w=[t2])
                dv(lambda: V.tensor_scalar(t2[:], t2[:], 1e-12, None, ALU.max), r=[t2], w=[t2])
                dv(lambda: V.reciprocal(t2[:], t2[:]), r=[t2], w=[t2])
                dv(lambda: V.tensor_tensor(kk[:], kk[:], t2[:], ALU.mult), r=[kk, t2], w=[kk])
                cp('r2')
                dv(lambda: V.tensor_tensor(kmod[:], aa[:], cka[:].unsqueeze(2).to_broadcast([128, 2, 128]), ALU.mult),
                   r=[aa, cka], w=[kmod])
                dv(lambda: V.tensor_tensor(kmod[:], kmod[:], cka1[:].unsqueeze(2).to_broadcast([128, 2, 128]), ALU.add),
                   r=[kmod, cka1], w=[kmod])
                dv(lambda: V.tensor_tensor(kmod[:], kmod[:], xs[:, 2:4, :], ALU.mult), r=[kmod, xs], w=[kmod])
                cp('r3')
                for c in range(2):
                    dv(lambda c=c: V.tensor_tensor_scan(cum[:, c, :], ones[:, 0:128], logd[:, c, :], 0.0, ALU.mult, ALU.add),
                       r=[ones, logd], w=[cum])
                dv(lambda: V.tensor_tensor(t2[:], cum[:], logd[:], ALU.subtract), r=[cum, logd], w=[t2])
                ac(lambda: S.activation(ePt[:], cum[:], AF.Exp), r=[cum], w=[ePt])
                ac(lambda: S.activation(ePi[:], cum[:], AF.Exp, scale=-1.0), r=[cum], w=[ePi])
                ac(lambda: S.activation(ePm[:], t2[:], AF.Exp), r=[t2], w=[ePm])
                for c in range(2):
                    ac(lambda c=c: S.activation(ePe[:, c, :], cum[:, c, :], AF.Exp, scale=-1.0, bias=cum[:, c, 127:128]),
                       r=[cum], w=[ePe])
                dv(lambda: V.scalar_tensor_tensor(AR[:, :, 0, :], kk[:], -1.0, ePm[:], ALU.mult, ALU.mult),
                   r=[kk, ePm], w=[AR])
                dv(lambda: V.tensor_tensor(AR[:, :, 1, :], xs[:, 0:2, :], ePt[:], ALU.mult), r=[xs, ePt], w=[AR])
                dv(lambda: V.tensor_tensor(kka[:], kk[:], aa[:], ALU.mult), r=[kk, aa], w=[kka])
                dv(lambda: V.tensor_tensor(BK[:, :, 0, :], kka[:], ePi[:], ALU.mult), r=[kka, ePi], w=[BK])
                dv(lambda: V.tensor_tensor(BK[:, :, 1, :], kmod[:], ePi[:], ALU.mult), r=[kmod, ePi], w=[BK])
                dv(lambda: V.tensor_tensor(BhKh[:, :, 0, :], kka[:], ePe[:], ALU.mult), r=[kka, ePe], w=[BhKh])
                dv(lambda: V.tensor_tensor(BhKh[:, :, 1, :], kmod[:], ePe[:], ALU.mult), r=[kmod, ePe], w=[BhKh])
                dv(lambda: V.tensor_tensor(rkr[:], xs[:, 0:2, :], kmod[:], ALU.mult), r=[xs, kmod], w=[rkr])
                dv(lambda: V.tensor_tensor(rkr[:], rkr[:], crk[:].unsqueeze(2).to_broadcast([128, 2, 128]), ALU.mult),
                   r=[rkr, crk], w=[rkr])
                cp('r4')
                for h in range(4):
                    mm(lambda h=h: T.matmul(pb[2][0:64, 2 * h:2 * h + 2], ident[:, (h % 2) * 64:(h % 2) * 64 + 64],
                                            ePt[:, h // 2, 126:128], start=True, stop=True), r=[ident, ePt], w=[pb[2]],
                       inc=(h == 3))
                dv(lambda: V.tensor_copy(dP[:], pb[2][0:64, 0:8].rearrange("p (h t) -> p h t", t=2)[:, :, 1]), r=[pb[2]],
                   w=[dP])
                cp('r5')
                srcs = [lambda c: AR[:, c, 0, :], lambda c: xs[:, 4 + c, :], lambda c: BhKh[:, c, 0, :],
                        lambda c: BhKh[:, c, 1, :]]
                srct = [AR, xs, BhKh, BhKh]
                for qi in range(4):
                    ps = pb[3] if qi < 2 else pb[4]
                    for c in range(2):
                        mm(lambda qi=qi, c=c, ps=ps: T.transpose(ps[:, ((qi % 2) * 2 + c) * 128:((qi % 2) * 2 + c + 1) * 128],
                                                                 srcs[qi](c), ident[:]), r=[srct[qi], ident], w=[ps],
                           inc=(c == 1 and qi % 2 == 1))
                for qi in range(4):
                    ps = pb[3] if qi < 2 else pb[4]
                    (dv if qi % 2 == 0 else ac)(
                        (lambda qi=qi, ps=ps: V.tensor_copy(tmq[:, :, qi, :], ps[:, (qi % 2) * 256:(qi % 2) * 256 + 256]
                                                            .rearrange("p (h d) -> p h d", h=4))) if qi % 2 == 0 else
                        (lambda qi=qi, ps=ps: S.copy(tmq[:, :, qi, :], ps[:, (qi % 2) * 256:(qi % 2) * 256 + 256]
                                                     .rearrange("p (h d) -> p h d", h=4))), r=[ps], w=[tmq])
                cp('r6')
                for c in range(2):
                    mm(lambda c=c: T.matmul(pb[2][:, 8 + 2 * c:10 + 2 * c], rkr[:, c, :], hsel[:], start=True, stop=True),
                       r=[rkr, hsel], w=[pb[2]], inc=(c == 1))
                dv(lambda: V.tensor_copy(rkb[:], pb[2][:, 8:12]), r=[pb[2]], w=[rkb])
                cp('r7')
                for hp in range(2):
                    for (ps, which) in ((pb[5], 0), (pb[6], 1)):
                        for hh in range(2):
                            h = hp * 2 + hh
                            p0 = (h % 2) * 64
                            mm(lambda h=h, p0=p0, ps=ps, which=which, hh=hh: T.matmul(
                                ps[:, hh * 256:(hh + 1) * 256], BK[p0:p0 + 64, h // 2, which, :],
                                AR[p0:p0 + 64, h // 2, :, :].rearrange("p a n -> p (a n)"), start=True, stop=True),
                               r=[BK, AR], w=[ps], inc=(hh == 1))
                        dv(lambda hp=hp, ps=ps, which=which: V.tensor_tensor(
                            Am[:, hp * 2:hp * 2 + 2, which, :], ps[:].rearrange("p (h n) -> p h n", h=2),
                            m_us[:].unsqueeze(1).to_broadcast([128, 2, 256]), ALU.mult), r=[ps, m_us], w=[Am])
                cp('r8')
                for h in range(4):
                    p0 = (h % 2) * 64
                    mm(lambda h=h, p0=p0: T.matmul(pb[5][:, h * 128:(h + 1) * 128], AR[p0:p0 + 64, h // 2, 0, :],
                                                   BK[p0:p0 + 64, h // 2, 0, :], start=True, stop=True), r=[AR, BK],
                       w=[pb[5]])
                dv(lambda: V.tensor_tensor(Afl[:], pb[5][:].rearrange("p (h n) -> p h n", h=4),
                                           m_ls[:].unsqueeze(1).to_broadcast([128, 4, 128]), ALU.mult),
                   r=[pb[5], m_ls], w=[Afl])
                cp('r9')
                def bm4(m_):
                    return m_[:].unsqueeze(1).to_broadcast([128, 4, 128])
                dv(lambda: V.tensor_tensor(Bm[0][:], Afl[:], bm4(bd32), ALU.mult), r=[Afl, bd32], w=[Bm[0]])
                pl(lambda: G.tensor_tensor(BmT[0][:], Am[:, :, 0, 0:128], bm4(bd32), ALU.mult), r=[Am, bd32], w=[BmT[0]])
                dv(lambda: V.tensor_tensor(Xn[0][:], Bm[0][:], bm4(ident), ALU.add), r=[Bm[0], ident], w=[Xn[0]])
                pl(lambda: G.tensor_tensor(XT[0][:], BmT[0][:], bm4(ident), ALU.add), r=[BmT[0], ident], w=[XT[0]])

                def mm4(ps, lt, rt):
                    for h in range(4):
                        mm(lambda h=h: T.matmul(ps[:, h * 128:(h + 1) * 128], lt[:, h, :], rt[:, h, :], start=True,
                                                stop=True), r=[lt, rt], w=[ps])

                def flat(t_):
                    return t_[:].rearrange("p h n -> p (h n)")
                cb_, cx = 0, 0
                for it in range(4):
                    nb = 1 - cb_
                    mm4(pb[5], BmT[cb_], Bm[cb_])
                    mm4(pb[6], Bm[cb_], BmT[cb_])
                    dv(lambda nb=nb: V.tensor_copy(flat(Bm[nb]), pb[5][:]), r=[pb[5]], w=[Bm[nb]])
                    ac(lambda nb=nb: S.copy(flat(BmT[nb]), pb[6][:]), r=[pb[6]], w=[BmT[nb]])
                    mm4(pb[2], BmT[nb], Xn[cx])
                    mm4(pb[3], Bm[nb], XT[cx])
                    dv(lambda cx=cx: V.tensor_tensor(flat(Xn[1 - cx]), pb[2][:], flat(Xn[cx]), ALU.add),
                       r=[pb[2], Xn[cx]], w=[Xn[1 - cx]])
                    dv(lambda cx=cx: V.tensor_tensor(flat(XT[1 - cx]), pb[3][:], flat(XT[cx]), ALU.add),
                       r=[pb[3], XT[cx]], w=[XT[1 - cx]])
                    cb_, cx = nb, 1 - cx
                dv(lambda: V.tensor_tensor(Bm[0][:], Afl[:], bm4(off1), ALU.mult), r=[Afl, off1], w=[Bm[0]])
                pl(lambda: G.tensor_tensor(BmT[0][:], Am[:, :, 0, 0:128], bm4(off1), ALU.mult), r=[Am, off1], w=[BmT[0]])
                mm4(pb[5], BmT[0], Xn[cx])
                mm4(pb[6], Bm[0], XT[cx])
                dv(lambda: V.tensor_copy(flat(Bm[1]), pb[5][:]), r=[pb[5]], w=[Bm[1]])
                ac(lambda: S.copy(flat(BmT[1]), pb[6][:]), r=[pb[6]], w=[BmT[1]])
                mm4(pb[2], XT[cx], Bm[1])
                mm4(pb[3], Xn[cx], BmT[1])
                dv(lambda cx=cx: V.tensor_tensor(flat(Xn[1 - cx]), pb[2][:], flat(Xn[cx]), ALU.add),
                   r=[pb[2], Xn[cx]], w=[Xn[1 - cx]])
                dv(lambda cx=cx: V.tensor_tensor(flat(XT[1 - cx]), pb[3][:], flat(XT[cx]), ALU.add),
                   r=[pb[3], XT[cx]], w=[XT[1 - cx]])
                cx = 1 - cx
                dv(lambda: V.tensor_tensor(Bm[0][:], Afl[:], bm4(off2), ALU.mult), r=[Afl, off2], w=[Bm[0]])
                mm4(pb[6], Bm[0], XT[cx])
                ac(lambda: S.copy(flat(BmT[1]), pb[6][:]), r=[pb[6]], w=[BmT[1]])
                mm4(pb[3], Xn[cx], BmT[1])
                dv(lambda cx=cx: V.tensor_tensor(flat(XT[1 - cx]), pb[3][:], flat(XT[cx]), ALU.add),
                   r=[pb[3], XT[cx]], w=[XT[1 - cx]])
                XTf = XT[1 - cx]
                cp('r10')
                for h in range(4):
                    mm(lambda h=h: T.matmul(pb[3][:, h * 64:(h + 1) * 64], Am[:, h, 1, 0:128], tmq[:, h, 1, :], start=True,
                                            stop=True), r=[Am, tmq], w=[pb[3]], inc=(h == 3))
                dv(lambda: V.tensor_copy(AZ[:].rearrange("p h (a d) -> p h a d", a=2)[:, :, 1, :],
                                         pb[3][:, 0:256].rearrange("p (h d) -> p h d", h=4)), r=[pb[3]], w=[AZ])
                ac(lambda: S.copy(AZ[:].rearrange("p h (a d) -> p h a d", a=2)[:, :, 0, :], tmq[:, :, 0, :]), r=[tmq], w=[AZ])
                cp('r11')
                for h in range(4):
                    mm(lambda h=h: T.matmul(pb[4][:, h * 128:(h + 1) * 128], XTf[:, h, :], AZ[:, h, :], start=True, stop=True),
                       r=[XTf, AZ], w=[pb[4]], inc=(h == 3))
                dv(lambda: V.tensor_copy(WU[:, :, 0:128], pb[4][:].rearrange("p (h n) -> p h n", h=4)), r=[pb[4]], w=[WU])
                cp('r12')
                for h in range(4):
                    mm(lambda h=h: T.matmul(pb[5][0:64, h * 128:h * 128 + 64], WU[:, h, 0:64], tmq[:, h, 2, :], start=True,
                                            stop=True), r=[WU, tmq], w=[pb[5]], inc=False)
                    mm(lambda h=h: T.matmul(pb[5][0:64, h * 128 + 64:h * 128 + 128], tmq[:, h, 2, :], WU[:, h, 64:128],
                                            start=True, stop=False), r=[WU, tmq], w=[pb[5]], inc=False)
                    mm(lambda h=h: T.matmul(pb[5][0:64, h * 128 + 64:h * 128 + 128], tmq[:, h, 3, :], tmq[:, h, 1, :],
                                            start=False, stop=True), r=[tmq], w=[pb[5]], inc=(h == 3))
                for h in range(4):
                    dv(lambda h=h: V.scalar_tensor_tensor(MTN[:, h, 0:64], ident[0:64, 0:64], dP[:, h:h + 1],
                                                          pb[5][0:64, h * 128:h * 128 + 64], ALU.mult, ALU.add),
                       r=[ident, dP, pb[5]], w=[MTN])
                ac(lambda: S.copy(MTN[:].rearrange("p h (a d) -> p h a d", a=2)[:, :, 1, :],
                                  pb[5][0:64, :].rearrange("p (h a d) -> p h a d", h=4, a=2)[:, :, 1, :]), r=[pb[5]], w=[MTN])
                cp('r13')
                for h in range(4):
                    mm(lambda h=h: T.matmul(pb[6][0:64, h * 128:(h + 1) * 128], ident[:, (h % 2) * 64:(h % 2) * 64 + 64],
                                            AR[:, h // 2, 1, :], start=True, stop=False), r=[ident, AR], w=[pb[6]], inc=False)
                    mm(lambda h=h: T.matmul(pb[6][0:64, h * 128:(h + 1) * 128], WU[:, h, 0:64], Am[:, h, 0, 128:256],
                                            start=False, stop=True), r=[WU, Am], w=[pb[6]], inc=(h == 3))
                dv(lambda: V.tensor_copy(QT[:].rearrange("p h n -> p (h n)"), pb[6][0:64, :]), r=[pb[6]], w=[QT])
                cp('r14')
                for h in range(4):
                    o = pb[3][:, 256 + h * 64:256 + (h + 1) * 64]
                    mm(lambda h=h, o=o: T.matmul(o, Am[:, h, 0, 128:256], WU[:, h, 64:128], start=True, stop=False),
                       r=[Am, WU], w=[pb[3]], inc=False)
                    mm(lambda h=h, o=o: T.matmul(o, Am[:, h, 1, 128:256], tmq[:, h, 1, :], start=False, stop=False),
                       r=[Am, tmq], w=[pb[3]], inc=False)
                    mm(lambda h=h, o=o: T.matmul(o, QT[:, h, :], ST[:, h, :], start=False, stop=True), r=[QT, ST],
                       w=[pb[3]], inc=(h == 3))
                dv(lambda: V.tensor_copy(ywk[:], pb[3][:, 256:512]), r=[pb[3]], w=[ywk])
                cp('r15')
                for h in range(4):
                    mm(lambda h=h: T.matmul(pb[4][0:64, h * 64:(h + 1) * 64], MTN[:, h, 0:64], ST[:, h, :], start=True,
                                            stop=True), r=[MTN, ST], w=[pb[4]], inc=(h == 3))
                dv(lambda: V.tensor_tensor(ST[:], pb[4][0:64, 0:256].rearrange("p (h d) -> p h d", h=4),
                                           MTN[:].rearrange("p h (a d) -> p h a d", a=2)[:, :, 1, :], ALU.add),
                   r=[pb[4], MTN], w=[ST])
                cp('r16')
                y3 = ywk[:].rearrange("p (h d) -> p h d", h=4)
                dv(lambda: V.tensor_reduce(gst[:, 0:4], y3, AX.X, ALU.add), r=[ywk], w=[gst])
                dv(lambda: V.tensor_tensor(ysq[:], ywk[:], ywk[:], ALU.mult), r=[ywk], w=[ysq])
                dv(lambda: V.tensor_reduce(gst[:, 4:8], ysq[:].rearrange("p (h d) -> p h d", h=4), AX.X, ALU.add),
                   r=[ysq], w=[gst])
                dv(lambda: V.tensor_scalar(gst[:, 0:8], gst[:, 0:8], 1.0 / 64, None, ALU.mult), r=[gst], w=[gst])
                dv(lambda: V.tensor_tensor(gst[:, 8:12], gst[:, 0:4], gst[:, 0:4], ALU.mult), r=[gst], w=[gst])
                dv(lambda: V.tensor_tensor(gst[:, 8:12], gst[:, 4:8], gst[:, 8:12], ALU.subtract), r=[gst], w=[gst])
                dv(lambda: V.tensor_scalar(gst[:, 8:12], gst[:, 8:12], GN_EPS, None, ALU.add), r=[gst], w=[gst])
                ac(lambda: S.activation(gst[:, 8:12], gst[:, 8:12], AF.Sqrt), r=[gst], w=[gst])
                dv(lambda: V.reciprocal(gst[:, 12:16], gst[:, 8:12]), r=[gst], w=[gst])
                dv(lambda: V.tensor_tensor(y3, y3, gst[:, 0:4].unsqueeze(2).to_broadcast([128, 4, 64]), ALU.subtract),
                   r=[ywk, gst], w=[ywk])
                dv(lambda: V.tensor_tensor(y3, y3, gst[:, 12:16].unsqueeze(2).to_broadcast([128, 4, 64]), ALU.mult),
                   r=[ywk, gst], w=[ywk])
                dv(lambda: V.tensor_tensor(ywk[:], ywk[:], lngb[:, 0:256], ALU.mult), r=[ywk, lngb], w=[ywk])
                dv(lambda: V.tensor_tensor(ywk[:], ywk[:], lngb[:, 256:512], ALU.add), r=[ywk, lngb], w=[ywk])
                dv(lambda: V.tensor_tensor(ysq[:].rearrange("p (h d) -> p h d", h=4), tmq[:, :, 1, :],
                                           rkb[:].unsqueeze(2).to_broadcast([128, 4, 64]), ALU.mult), r=[tmq, rkb], w=[ysq])
                dv(lambda: V.tensor_tensor(ywk[:], ywk[:], ysq[:], ALU.add), r=[ywk, ysq], w=[ywk])
                dv(lambda: V.tensor_tensor(ymix[:, 0:256], ywk[:], gateT[:], ALU.mult), r=[ywk, gateT], w=[ymix])

                cp('rwkvE')
                if stop == 'rwkv':
                    continue
                kTc, kTp = kT[i % 2], kT[(i + 1) % 2]
                vc, vp = vaug[i % 2], vaug[(i + 1) % 2]
                q6 = bq[:, 0:384].rearrange("p (h d) -> p h d", h=6)
                dv(lambda: V.tensor_tensor(qn[:], bq[:, 0:384], bq[:, 0:384], ALU.mult), r=[bq], w=[qn])
                dv(lambda: V.tensor_reduce(st8[:, 0:6], qn[:].rearrange("p (h d) -> p h d", h=6), AX.X, ALU.add), r=[qn],
                   w=[st8])
                dv(lambda: V.tensor_scalar(st8[:, 0:6], st8[:, 0:6], 1.0 / 64, EPS, ALU.mult, ALU.add), r=[st8], w=[st8])
                ac(lambda: S.activation(st8[:, 0:6], st8[:, 0:6], AF.Sqrt), r=[st8], w=[st8])
                dv(lambda: V.reciprocal(st8[:, 0:6], st8[:, 0:6]), r=[st8], w=[st8])
                dv(lambda: V.tensor_tensor(qn[:].rearrange("p (h d) -> p h d", h=6), q6,
                                           st8[:, 0:6].unsqueeze(2).to_broadcast([128, 6, 64]), ALU.mult), r=[bq, st8], w=[qn])
                dv(lambda: V.tensor_tensor(qn[:, 0:256].rearrange("p (h d) -> p h d", h=4),
                                           qn[:, 0:256].rearrange("p (h d) -> p h d", h=4),
                                           qkg[:, 0:64].unsqueeze(1).to_broadcast([128, 4, 64]), ALU.mult), r=[qn, qkg], w=[qn])
                dv(lambda: V.tensor_tensor(qn[:, 256:384].rearrange("p (h d) -> p h d", h=2),
                                           qn[:, 256:384].rearrange("p (h d) -> p h d", h=2),
                                           qkg[:, 64:128].unsqueeze(1).to_broadcast([128, 2, 64]), ALU.mult), r=[qn, qkg],
                   w=[qn])
                cp('s1')
                ac(lambda vc=vc: S.copy(vc[:, :, 0:64], bq[:, 384:512].rearrange("p (h d) -> p h d", h=2)), r=[bq], w=[vc])
                if i == NT - 1:
                    dma(o_pk[l], qn[:, 256:384], r=[qn], w=[o_pk])
                    dma(o_pv[l], bq[:, 384:512], r=[bq], w=[o_pv])
                cp('s2')
                for h in range(6):
                    ps = pb[0] if h < 4 else pb[1]
                    mm(lambda h=h, ps=ps: T.transpose(ps[0:64, (h % 4) * 128:(h % 4 + 1) * 128], qn[:, h * 64:(h + 1) * 64],
                                                      ident[:]), r=[qn, ident], w=[ps], inc=(h == 3 or h == 5))
                dv(lambda: V.tensor_copy(qT[:].rearrange("p h n -> p (h n)"), pb[0][0:64, :]), r=[pb[0]], w=[qT])
                ac(lambda kTc=kTc: S.copy(kTc[:].rearrange("p h n -> p (h n)"), pb[1][0:64, 0:256]), r=[pb[1]], w=[kTc])
                cp('s3')
                parts = ([(kTp, vp, 0, m_lower)] if i > 0 else []) + [(kTc, vc, 1, tri)]
                for kh in range(2):
                    for (kt_, vt_, w_, msk) in parts:
                        ps = pb[2] if w_ == 0 else pb[3]
                        mm(lambda kh=kh, kt_=kt_, ps=ps: T.matmul(ps[:, kh * 256:(kh + 1) * 256], kt_[:, kh, :],
                                                                  qT[:, 2 * kh:2 * kh + 2, :].rearrange("p h n -> p (h n)"),
                                                                  start=True, stop=True), r=[kt_, qT], w=[ps])
                        ac(lambda kh=kh, w_=w_, ps=ps: S.activation(pexp[:, kh, w_, :], ps[:, kh * 256:(kh + 1) * 256], AF.Exp,
                                                                    scale=0.125), r=[ps], w=[pexp])
                        dv(lambda kh=kh, w_=w_, msk=msk: V.tensor_tensor(
                            pexp[:, kh, w_, :].rearrange("p (h n) -> p h n", h=2),
                            pexp[:, kh, w_, :].rearrange("p (h n) -> p h n", h=2),
                            msk[:].unsqueeze(1).to_broadcast([128, 2, 128]), ALU.mult), r=[pexp, msk], w=[pexp])
                cp('s4')
                for h in range(4):
                    kh = h // 2
                    for pi, (kt_, vt_, w_, msk) in enumerate(parts):
                        mm(lambda h=h, kh=kh, vt_=vt_, w_=w_, pi=pi: T.matmul(
                            pb[4][:, h * 128:h * 128 + 65], pexp[:, kh, w_, (h % 2) * 128:(h % 2 + 1) * 128], vt_[:, kh, :],
                            start=(pi == 0), stop=(pi == len(parts) - 1)), r=[pexp, vt_], w=[pb[4]],
                           inc=(h == 3 and pi == len(parts) - 1))
                cp('s5')
                o4 = pb[4][:].rearrange("p (h d) -> p h d", h=4)
                dv(lambda: V.tensor_tensor(st8[:, 0:4], o4[:, :, 64], esink[:], ALU.add), r=[pb[4], esink], w=[st8])
                cp('s6')
                dv(lambda: V.reciprocal(st8[:, 0:4], st8[:, 0:4]), r=[st8], w=[st8])
                cp('s7')
                ac(lambda: S.copy(qn[:, 0:256].rearrange("p (h d) -> p h d", h=4), o4[:, :, 0:64]), r=[pb[4]], w=[qn])
                dv(lambda: V.tensor_tensor(ymix[:, 256:512].rearrange("p (h d) -> p h d", h=4),
                                           qn[:, 0:256].rearrange("p (h d) -> p h d", h=4),
                                           st8[:, 0:4].unsqueeze(2).to_broadcast([128, 4, 64]), ALU.mult), r=[qn, st8], w=[ymix])

                cp('swaE')
                if stop == 'swa':
                    continue
                if i == NT - 1:
                    for j in range(3):
                        mm(lambda j=j: T.transpose(pb[0][0:8, j * 128:(j + 1) * 128], xbcF[:, :, 128 + j], ident[:]),
                           r=[xbcF, ident], w=[pb[0]], inc=(j == 2))
                    dv(lambda: V.tensor_copy(ostg[0:8, 0:384], pb[0][0:8, 0:384]), r=[pb[0]], w=[ostg])
                    dma(o_pconv[l].rearrange("j (k p) -> k j p", p=128),
                        ostg[0:8, 0:384].rearrange("k (j p) -> k j p", j=3), r=[ostg], w=[o_pconv])
                cp('d1')
                for kc in range(8):
                    eng = dv
                    E = V
                    eng(lambda kc=kc, E=E: E.tensor_scalar(cvF[:, kc, :], xbcF[:, kc, 0:128], cconvw[:, 0, kc:kc + 1],
                                                           cconvb[:, kc:kc + 1], ALU.mult, ALU.add),
                        r=[xbcF, cconvw, cconvb], w=[cvF])
                    for j in range(1, 4):
                        eng(lambda kc=kc, j=j, E=E: E.scalar_tensor_tensor(cvF[:, kc, :], xbcF[:, kc, j:j + 128],
                                                                           cconvw[:, j, kc:kc + 1], cvF[:, kc, :], ALU.mult,
                                                                           ALU.add), r=[xbcF, cconvw, cvF], w=[cvF])
                cp('d2')
                ac(lambda: S.copy(xbcF[:, :, 0:3], xbcF[:, :, 128:131]), r=[xbcF], w=[xbcF])
                ac(lambda: S.activation(cvF[:], cvF[:], AF.Silu), r=[cvF], w=[cvF])
                dv(lambda: V.tensor_tensor(dts[:], dtr[:], r8[:, 0:8], ALU.add), r=[dtr, r8], w=[dts])
                ac(lambda: S.activation(dts[:], dts[:], AF.Exp), r=[dts], w=[dts])
                ac(lambda: S.activation(dts[:], dts[:], AF.Ln, bias=1.0), r=[dts], w=[dts])
                dv(lambda: V.tensor_tensor(das[:], dts[:], r8[:, 8:16], ALU.mult), r=[dts, r8], w=[das])
                cp('d3')
                for c in range(4):
                    mm(lambda c=c: T.transpose(pb[0][:, c * 128:(c + 1) * 128], cvF[:, c, :], ident[:]), r=[cvF, ident],
                       w=[pb[0]], inc=(c == 3))
                cp('d3a')
                ac(lambda: S.copy(xtok[:], pb[0][:]), r=[pb[0]], w=[xtok])
                cp('d3b')
                dv(lambda: V.tensor_tensor(xdt[:].rearrange("p (h d) -> p h d", h=8), xtok[:].rearrange("p (h d) -> p h d", h=8),
                                           dts[:].unsqueeze(2).to_broadcast([128, 8, 64]), ALU.mult), r=[xtok, dts], w=[xdt])
                cp('d4')
                for c in range(2):
                    mm(lambda c=c: T.transpose(pb[1][:, c * 128:(c + 1) * 128], cvF[:, 4 + c, :], ident[:]), r=[cvF, ident],
                       w=[pb[1]], inc=(c == 1))
                ac(lambda: S.copy(Btok[:], pb[1][:, 0:256]), r=[pb[1]], w=[Btok])
                cp('d5')
                mm(lambda: T.matmul(pb[1][:, 256:264], tri[:], das[:], start=True, stop=True), r=[tri, das], w=[pb[1]], inc=False)
                mm(lambda: T.matmul(pb[1][:, 264:272], ones[:], das[:], start=True, stop=True), r=[ones, das], w=[pb[1]])
                dv(lambda: V.tensor_copy(acum[:], pb[1][:, 256:264]), r=[pb[1]], w=[acum])
                dv(lambda: V.tensor_copy(tot[:, 0:8], pb[1][:, 264:272]), r=[pb[1]], w=[tot])
                cp('d6')
                dv(lambda: V.tensor_copy(dabc[:], das[:].unsqueeze(2).to_broadcast([128, 8, 128])), r=[das], w=[dabc])
                for e in range(8):
                    ps = pb[2] if e < 4 else pb[3]
                    mm(lambda e=e, ps=ps: T.matmul(ps[:, (e % 4) * 128:(e % 4 + 1) * 128], dabc[:, e, :], tri[:], start=True,
                                                   stop=True), r=[dabc, tri], w=[ps], inc=(e % 4 == 3))
                cp('d7')
                for g in range(2):
                    mm(lambda g=g: T.matmul(pb[4][:, g * 128:(g + 1) * 128], cvF[:, 4 + g, :], cvF[:, 6 + g, :], start=True,
                                            stop=True), r=[cvF], w=[pb[4]], inc=(g == 1))
                ac(lambda: S.copy(cbs[:], pb[4][:, 0:256]), r=[pb[4]], w=[cbs])
                for g in range(2):
                    ps = pb[2] if g == 0 else pb[3]
                    ac(lambda ps=ps: S.copy(seg[:].rearrange("p e n -> p (e n)"), ps[:]), r=[ps], w=[seg])
                    dv(lambda g=g: V.tensor_tensor(seg[:], seg[:],
                                                   acum[:, g * 4:(g + 1) * 4].unsqueeze(2).to_broadcast([128, 4, 128]),
                                                   ALU.subtract), r=[seg, acum], w=[seg])
                    ac(lambda g=g, ps=ps: S.activation(eac[:, g * 4:(g + 1) * 4, :], ps[:].rearrange("p (e n) -> p e n", e=4),
                                                       AF.Exp), r=[ps], w=[eac])
                    dv(lambda: V.tensor_tensor(seg[:], seg[:], negm[:].unsqueeze(1).to_broadcast([128, 4, 128]), ALU.add),
                       r=[seg, negm], w=[seg])
                    ac(lambda: S.activation(seg[:], seg[:], AF.Exp), r=[seg], w=[seg])
                    dv(lambda g=g: V.tensor_tensor(GT[:, g * 4:(g + 1) * 4, :], seg[:],
                                                   cbs[:, g * 128:(g + 1) * 128].unsqueeze(1).to_broadcast([128, 4, 128]),
                                                   ALU.mult), r=[seg, cbs], w=[GT])
                    dv(lambda g=g: V.tensor_tensor(Cs[:, g * 4:(g + 1) * 4, :], eac[:, g * 4:(g + 1) * 4, :],
                                                   cvF[:, 6 + g, :].unsqueeze(1).to_broadcast([128, 4, 128]), ALU.mult),
                       r=[eac, cvF], w=[Cs])
                cp('d9')
                for e in range(8):
                    mm(lambda e=e: T.matmul(pb[5][:, e * 64:(e + 1) * 64], GT[:, e, :], xdt[:, e * 64:(e + 1) * 64], start=True,
                                            stop=False), r=[GT, xdt], w=[pb[5]], inc=False)
                    mm(lambda e=e: T.matmul(pb[5][:, e * 64:(e + 1) * 64], Cs[:, e, :], hTs[:, e * 64:(e + 1) * 64], start=False,
                                            stop=True), r=[Cs, hTs], w=[pb[5]], inc=(e == 7))
                cp('d10')
                dv(lambda: V.tensor_tensor(tot[:, 8:16], tot[:, 0:8], acum[:], ALU.subtract), r=[tot, acum], w=[tot])
                ac(lambda: S.activation(cdec[:], tot[:], AF.Exp), r=[tot], w=[cdec])
                dv(lambda: V.tensor_tensor(xdte[:].rearrange("p (h d) -> p h d", h=8), xdt[:].rearrange("p (h d) -> p h d", h=8),
                                           cdec[:, 8:16].unsqueeze(2).to_broadcast([128, 8, 64]), ALU.mult), r=[xdt, cdec],
                   w=[xdte])
                for g in range(2):
                    mm(lambda g=g: T.matmul(pb[6][:, g * 256:(g + 1) * 256], Btok[:, g * 128:(g + 1) * 128],
                                            xdte[:, g * 256:(g + 1) * 256], start=True, stop=True), r=[Btok, xdte], w=[pb[6]],
                       inc=(g == 1))
                cp('d11')
                dv(lambda: V.tensor_tensor(yss[:].rearrange("p (h d) -> p h d", h=8), xtok[:].rearrange("p (h d) -> p h d", h=8),
                                           r8[:, 16:24].unsqueeze(2).to_broadcast([128, 8, 64]), ALU.mult), r=[xtok, r8], w=[yss])
                dv(lambda: V.tensor_tensor(yss[:], yss[:], pb[5][:], ALU.add), r=[yss, pb[5]], w=[yss])
                dv(lambda: V.tensor_tensor(hTs[:].rearrange("p (h d) -> p h d", h=8), hTs[:].rearrange("p (h d) -> p h d", h=8),
                                           cdec[:, 0:8].unsqueeze(2).to_broadcast([128, 8, 64]), ALU.mult), r=[hTs, cdec], w=[hTs])
                dv(lambda: V.tensor_tensor(hTs[:], hTs[:], pb[6][:], ALU.add), r=[hTs, pb[6]], w=[hTs])
                dv(lambda: V.tensor_tensor(yss[:], yss[:], zs[:], ALU.mult), r=[yss, zs], w=[yss])
                ac(lambda: S.activation(junk[:, 0:512], yss[:], AF.Square, accum_out=st8[:, 6:7]), r=[yss], w=[junk, st8])
                dv(lambda: V.tensor_scalar(st8[:, 6:7], st8[:, 6:7], 1.0 / 512, EPS, ALU.mult, ALU.add), r=[st8], w=[st8])
                ac(lambda: S.activation(st8[:, 6:7], st8[:, 6:7], AF.Sqrt), r=[st8], w=[st8])
                dv(lambda: V.reciprocal(st8[:, 7:8], st8[:, 6:7]), r=[st8], w=[st8])
                dv(lambda: V.scalar_tensor_tensor(ymix[:, 512:1024], yss[:], st8[:, 7:8], sng[:], ALU.mult, ALU.mult),
                   r=[yss, st8, sng], w=[ymix])

                cp('ssdE')
                if stop == 'ssd':
                    continue
                if dbg and l == 0:
                    dma(dbg_o[i * 128:(i + 1) * 128, :], ymix[:], r=[ymix], w=[dbg_o])
                dv(lambda: V.tensor_copy(ymb[:], ymix[:]), r=[ymix], w=[ymb])
                to_fm(ymb, ymT)
                for n2 in range(2):
                    ps = pb[n2]
                    for kc in range(8):
                        mm(lambda kc=kc, n2=n2, ps=ps: T.matmul(ps[:], ymT[:, kc, :], w_out_v[:, kc, n2 * 512:(n2 + 1) * 512],
                                                                start=(kc == 0), stop=(kc == 7)), r=[ymT, wmix], w=[ps],
                           inc=(kc == 7), dense=True)
                    dv(lambda n2=n2, ps=ps: V.tensor_tensor(hf[:, n2 * 512:(n2 + 1) * 512], ps[:],
                                                            modp[:, 2 * D + n2 * 512:2 * D + (n2 + 1) * 512], ALU.mult),
                       r=[ps, modp], w=[hf])
                dv(lambda xti=xti: V.tensor_tensor(xti[:], xti[:], hf[:], ALU.add), r=[xti, hf], w=[xti])
                dma(xa[i * 128:(i + 1) * 128, :], xti[:], r=[xti], w=[xa])

            if stop in ('proj', 'rwkv', 'swa', 'ssd', 'mixer'):
                break
            for h in range(4):
                mm(lambda h=h: T.transpose(pb[1][0:64, h * 64:(h + 1) * 64], ST[:, h, :], ident[0:64, 0:64]),
                   r=[ST, ident], w=[pb[1]], inc=(h == 3))
            dv(lambda: V.tensor_copy(ostg[0:64, 0:256], pb[1][0:64, 0:256]), r=[pb[1]], w=[ostg])
            dma(o_pwkv[l].rearrange("h v k -> v h k"), ostg[0:64, 0:256].rearrange("v (h k) -> v h k", h=4), r=[ostg],
                w=[o_pwkv])
            for c in range(4):
                mm(lambda c=c: T.transpose(pb[0][:, c * 128:(c + 1) * 128], hTs[:, c * 128:(c + 1) * 128], ident[:]),
                   r=[hTs, ident], w=[pb[0]], inc=(c == 3))
            dv(lambda: V.tensor_copy(xtok[:], pb[0][:]), r=[pb[0]], w=[xtok])
            dma(o_pssm[l].rearrange("(c p) d -> p c d", p=128), xtok[:].rearrange("p (c d) -> p c d", c=4), r=[xtok],
                w=[o_pssm])

            if do_decode:
                decode_mixer(l)
            phase_reset()
            wffn = sb("wffn", [128, 8 * 5632 + 22 * 1024], BF16)
            w_up_v = wffn[:, 0:8 * 5632].rearrange("p (k n) -> p k n", k=8)
            w_dn_v = wffn[:, 8 * 5632:8 * 5632 + 22 * 1024].rearrange("p (k n) -> p k n", k=22)
            fw.dma(fw.pool, w_up_v, Wd['ffn_w_up'][l].rearrange("(k p) n -> p k n", p=128), reads=[Wd['ffn_w_up']],
                   writes=[wffn])
            fw.dma(fw.pool, w_dn_v, Wd['ffn_w_down'][l].rearrange("(k p) n -> p k n", p=128), reads=[Wd['ffn_w_down']],
                   writes=[wffn])
            xt = [sb("xt0", [128, D])]
            junk = sb("junk", [128, D], BF16)
            ss = sb("ss", [128, 8])
            hf = sb("hf", [128, D])
            hb = sb("hb", [128, D], BF16)
            hT = sb("hT", [128, 8, 128], BF16)
            gF = sb("gF", [128, NCH_FF, 130]); gC = sb("gC", [128, 4, 128]); prod = sb("prod", [128, NCH_FF, 128], BF16)
            modp = load_mod(l, 3 * D, 'norm_ffn_g')
            dv(lambda: V.memset(gF[:, :, 0:2], 0.0), w=[gF])
            xdst = y_p if l == L - 1 else xb
            for i in range(NT):
                xti = xt[0]
                dma(xti[:], xa[i * 128:(i + 1) * 128, :], r=[xa], w=[xti])
                rmsnorm_mod(xti, hb)
                to_fm(hb, hT)
                for gi in range(6):
                    ncg = 4 if gi < 5 else 2
                    psg, psv = pb[2 * (gi % 2)], pb[2 * (gi % 2) + 1]
                    for (ps, c0) in ((psg, 0), (psv, DFF)):
                        for oc in range(ncg):
                            ch = gi * 4 + oc
                            for kc in range(8):
                                mm(lambda ps=ps, c0=c0, oc=oc, ch=ch, kc=kc: T.matmul(
                                    ps[:, oc * 128:(oc + 1) * 128], w_up_v[:, kc, c0 + ch * 128:c0 + (ch + 1) * 128],
                                    hT[:, kc, :], start=(kc == 0), stop=(kc == 7)), r=[wffn, hT], w=[ps],
                                   inc=(kc == 7 and oc == ncg - 1), dense=True)
                    ac(lambda gi=gi, ncg=ncg, psg=psg: S.copy(gF[:, gi * 4:gi * 4 + ncg, 2:130],
                                                              psg[:, 0:ncg * 128].rearrange("p (k n) -> p k n", k=ncg)),
                       r=[psg], w=[gF])
                    if i == NT - 1:
                        pass
                    for oc in range(ncg):
                        ch = gi * 4 + oc
                        eng, E = (dv, V)
                        eng(lambda ch=ch, oc=oc, E=E: E.tensor_scalar(gC[:, oc, :], gF[:, ch, 0:128], fcw[:, 0, ch:ch + 1],
                                                               fcb[:, ch:ch + 1], ALU.mult, ALU.add), r=[gF, fcw, fcb], w=[gC])
                        for j in range(1, 3):
                            eng(lambda ch=ch, oc=oc, j=j, E=E: E.scalar_tensor_tensor(gC[:, oc, :], gF[:, ch, j:j + 128],
                                                                               fcw[:, j, ch:ch + 1], gC[:, oc, :], ALU.mult,
                                                                               ALU.add), r=[gF, fcw, gC], w=[gC])
                    ac(lambda gi=gi, ncg=ncg: S.activation(gC[:, 0:ncg, :], gC[:, 0:ncg, :], AF.Silu),
                       r=[gC], w=[gC])
                    dv(lambda gi=gi, ncg=ncg, psv=psv: V.tensor_tensor(
                        prod[:, gi * 4:gi * 4 + ncg, :], gC[:, 0:ncg, :],
                        psv[:, 0:ncg * 128].rearrange("p (k n) -> p k n", k=ncg), ALU.mult), r=[gC, psv], w=[prod])
                if i == NT - 1:
                    for j in range(2):
                        mm(lambda j=j: T.transpose(pb[6][0:NCH_FF, j * 128:(j + 1) * 128], gF[:, :, 128 + j], ident[:]),
                           r=[gF, ident], w=[pb[6]], inc=(j == 1))
                    dv(lambda: V.tensor_copy(ostg[0:NCH_FF, 0:256], pb[6][0:NCH_FF, 0:256]), r=[pb[6]], w=[ostg])
                    dma(o_pffn[l].rearrange("j (k p) -> k j p", p=128),
                        ostg[0:NCH_FF, 0:256].rearrange("k (j p) -> k j p", j=2), r=[ostg], w=[o_pffn])
                ac(lambda: S.copy(gF[:, :, 0:2], gF[:, :, 128:130]), r=[gF], w=[gF])
                for n2 in range(2):
                    ps = pb[4 + n2]
                    for kc in range(NCH_FF):
                        mm(lambda kc=kc, n2=n2, ps=ps: T.matmul(ps[:], prod[:, kc, :], w_dn_v[:, kc, n2 * 512:(n2 + 1) * 512],
                                                                start=(kc == 0), stop=(kc == NCH_FF - 1)), r=[prod, wffn],
                           w=[ps], inc=(kc == NCH_FF - 1), dense=True)
                    dv(lambda n2=n2, ps=ps: V.tensor_tensor(hf[:, n2 * 512:(n2 + 1) * 512], ps[:],
                                                            modp[:, 2 * D + n2 * 512:2 * D + (n2 + 1) * 512], ALU.mult),
                       r=[ps, modp], w=[hf])
                dv(lambda xti=xti: V.tensor_tensor(xti[:], xti[:], hf[:], ALU.add), r=[xti, hf], w=[xti])
                dma(xdst[i * 128:(i + 1) * 128, :], xti[:], r=[xti], w=[xdst])
            if do_decode:
                decode_ffn(l)

    except _Stop:
        pass

    for t in outs:
        fw._wait(fw.sp, t.lw)
    for q in (fw.pe, fw.act, fw.dve, fw.pool):
        if q.cnt > 0:
            fw._wait(fw.sp, Ev(q.sem, q.cnt, id(q.sem)))
    for i in range(fw.ndma):
        fw._wait(fw.sp, fw.dlast[i])
    return nc, fw


_CACHE = {}


def kernel(**inputs):
    SEQ, L, NS, NCORE = 8192, 4, 16, 8
    f = lambda a: np.ascontiguousarray(np.asarray(a), dtype=np.float32)
    inp = {k: f(v) for k, v in inputs.items()}
    if "nc" not in _CACHE:
        _CACHE["nc"] = build(SEQ, L, NS)[0]
    nc = _CACHE["nc"]
    ws = wshapes(L)
    wmaps = {k: inp[k].reshape(ws[k]) for k in WNAMES}
    in_maps = []
    for c in range(NCORE):
        b = c // 4
        sl = slice(c * NS, (c + 1) * NS)
        m = {
            "x_p": inp["x_prompt"][b], "x_s": inp["x_sample"][sl, 0],
            "c_all": np.concatenate([inp["c_sample"][sl], inp["c_prompt"][b:b + 1]], 0),
            "st_shift": inp["state_rwkv_shift"][:, sl], "st_wkv": inp["state_rwkv_wkv"][:, sl].reshape(L, NS * 4, 4096),
            "st_k": inp["cache_swa_k"][:, sl], "st_v": inp["cache_swa_v"][:, sl], "st_conv": inp["state_ssm_conv"][:, sl],
            "st_ssm": inp["state_ssm"][:, sl].reshape(L, NS * 8, 8192), "st_ffn": inp["state_ffn_conv"][:, sl],
        }
        m.update(wmaps)
        in_maps.append({k: np.ascontiguousarray(v, dtype=np.float32) for k, v in m.items()})
    res = run_bass_kernel_spmd(nc, in_maps, core_ids=list(range(NCORE))).results
    pc = [res[0], res[4]]
    st = lambda key, shp: np.stack([np.asarray(r[key]).reshape(shp) for r in pc], axis=1)
    cat = lambda key, shp: np.concatenate([np.asarray(r[key]).reshape(shp) for r in res], axis=1)
    y_p = np.stack([np.asarray(r["y_p"]) for r in pc], 0)
    y_s = np.concatenate([np.asarray(r["y_s"]).reshape(NS, 1, D) for r in res], 0)
    outs = (
        y_p, y_s,
        st("o_pshift", (L, 1024)), st("o_pwkv", (L, 4, 64, 64)), st("o_pk", (L, 128, 2, 64)), st("o_pv", (L, 128, 2, 64)),
        st("o_pconv", (L, 3, 1024)), st("o_pssm", (L, 8, 64, 128)), st("o_pffn", (L, 2, DFF)),
        cat("o_sshift", (L, NS, 1024)), cat("o_swkv", (L, NS, 4, 64, 64)), cat("o_sk", (L, NS, 128, 2, 64)),
        cat("o_sv", (L, NS, 128, 2, 64)), cat("o_sconv", (L, NS, 3, 1024)), cat("o_sssm", (L, NS, 8, 64, 128)),
        cat("o_sffn", (L, NS, 2, DFF)),
    )
    return tuple(np.ascontiguousarray(o, dtype=np.float32) for o in outs)
```

```python
import math
import numpy as np
import concourse.bass as bass
import concourse.mybir as mybir
from concourse.bass_utils import run_bass_kernel_spmd

F32 = mybir.dt.float32
BF16 = mybir.dt.bfloat16
ALU = mybir.AluOpType
AF = mybir.ActivationFunctionType
AX = mybir.AxisListType

SEM_ROLL = 30000
D = 1024
PROJ = 3080
DFF = 2816
NCH_FF = 22
EPS = 1e-6
GN_EPS = 64e-5


class Ev:
    __slots__ = ("sem", "val", "key")

    def __init__(self, sem, val, key):
        self.sem, self.val, self.key = sem, val, key


class Tl:
    __slots__ = ("t", "tr", "name", "off", "excl")

    def __init__(self, t, name="", tr=None, off=0):
        self.t, self.name, self.off = t, name, off
        self.excl = False
        self.tr = tr if tr is not None else [None, []]

    @property
    def lw(self):
        return self.tr[0]

    @lw.setter
    def lw(self, v):
        self.tr[0] = v

    @property
    def rd(self):
        return self.tr[1]

    @rd.setter
    def rd(self, v):
        self.tr[1] = v

    def __getitem__(self, k):
        return self.t[k]


class Q:
    def __init__(self, fw, eng, name):
        self.fw, self.eng, self.name = fw, eng, name
        self.nsem = 0
        self.waited = {}
        self.pend_r, self.pend_w = [], []
        self.new_sem()

    def new_sem(self):
        self.sem = self.fw.nc.alloc_semaphore(f"s_{self.name}_{self.nsem}")
        self.nsem += 1
        self.cnt = 0


class FW:
    def __init__(self, nc):
        self.nc = nc
        self.pe = Q(self, nc.tensor, "pe")
        self.act = Q(self, nc.scalar, "act")
        self.dve = Q(self, nc.vector, "dve")
        self.pool = Q(self, nc.gpsimd, "pool")
        self.sp = Q(self, nc.sync, "sp")
        self.ndma = 32
        self.dsem = [nc.alloc_semaphore(f"s_dma_{i}") for i in range(self.ndma)]
        self.dcnt = [0] * self.ndma
        self.dnext = 0
        self.dlast = [None] * self.ndma
        self.n_inst = 0
        self.swd = []
        self.tmap = {}

    def _wait(self, q, ev):
        if ev is None:
            return
        if q.waited.get(ev.key, 0) >= ev.val:
            return
        q.eng.wait_ge(ev.sem, ev.val)
        q.waited[ev.key] = ev.val

    def _deps(self, q, reads, writes):
        for t in reads:
            self._wait(q, t.lw)
            if t.excl:
                for e in t.rd:
                    self._wait(q, e)
        for t in writes:
            self._wait(q, t.lw)
            for e in t.rd:
                self._wait(q, e)

    def _commit(self, ev, reads, writes):
        for t in writes:
            t.lw = ev
            t.rd = []
        for t in reads:
            if t.lw is not ev:
                t.rd = [e for e in t.rd if e.key != ev.key]
                t.rd.append(ev)

    def op(self, q, fn, reads=(), writes=(), inc=True):
        self._deps(q, reads, writes)
        ins = fn()
        self.n_inst += 1
        q.pend_r.extend(reads)
        q.pend_w.extend(writes)
        if not inc:
            return None
        if q.cnt >= SEM_ROLL:
            q.new_sem()
        q.cnt += 1
        ins.then_inc(q.sem, 1)
        ev = Ev(q.sem, q.cnt, id(q.sem))
        self._commit(ev, q.pend_r, q.pend_w)
        q.pend_r, q.pend_w = [], []
        return ev

    def dma(self, q, out, in_, reads=(), writes=(), **kw):
        self._deps(q, reads, writes)
        if q is self.pool:
            sem = self.nc.alloc_semaphore(f"s_swdma_{self.n_inst}")
            ins = q.eng.dma_start(out=out, in_=in_, **kw)
            self.n_inst += 1
            ins.then_inc(sem, 16)
            ev = Ev(sem, 16, id(sem))
            self.swd.append(ev)
            self._commit(ev, reads, writes)
            return ev
        i = self.dnext
        self.dnext = (self.dnext + 1) % self.ndma
        self._wait(q, self.dlast[i])
        ins = q.eng.dma_start(out=out, in_=in_, **kw)
        self.n_inst += 1
        self.dcnt[i] += 16
        ins.then_inc(self.dsem[i], 16)
        ev = Ev(self.dsem[i], self.dcnt[i], id(self.dsem[i]))
        self.dlast[i] = ev
        self._commit(ev, reads, writes)
        return ev


WNAMES = ['ada_w', 'ada_b', 'norm_mix_g', 'norm_ffn_g', 'w_in', 'w_out', 'rwkv_mu', 'rwkv_w0', 'rwkv_w2',
          'rwkv_a0', 'rwkv_a2', 'rwkv_g2', 'rwkv_k_k', 'rwkv_k_a', 'rwkv_r_k', 'rwkv_ln_g', 'rwkv_ln_b',
          'attn_q_norm_g', 'attn_k_norm_g', 'attn_sinks', 'ssm_conv_w', 'ssm_conv_b', 'ssm_dt_bias',
          'ssm_a_log', 'ssm_d', 'ssm_norm_g', 'ffn_w_up', 'ffn_conv_w', 'ffn_conv_b', 'ffn_w_down']


def wshapes(L):
    return {
        'ada_w': [L, D, 6 * D], 'ada_b': [L, 6 * D], 'norm_mix_g': [L, D], 'norm_ffn_g': [L, D],
        'w_in': [L, D, PROJ], 'w_out': [L, D, D], 'rwkv_mu': [L, 1024], 'rwkv_w0': [L, 256],
        'rwkv_w2': [L, 64, 256], 'rwkv_a0': [L, 256], 'rwkv_a2': [L, 64, 256], 'rwkv_g2': [L, 128, 256],
        'rwkv_k_k': [L, 256], 'rwkv_k_a': [L, 256], 'rwkv_r_k': [L, 256], 'rwkv_ln_g': [L, 256],
        'rwkv_ln_b': [L, 256], 'attn_q_norm_g': [L, 64], 'attn_k_norm_g': [L, 64], 'attn_sinks': [L, 4],
        'ssm_conv_w': [L, 4, 1024], 'ssm_conv_b': [L, 1024], 'ssm_dt_bias': [L, 8], 'ssm_a_log': [L, 8],
        'ssm_d': [L, 8], 'ssm_norm_g': [L, 512], 'ffn_w_up': [L, D, 2 * DFF], 'ffn_conv_w': [L, 3, DFF],
        'ffn_conv_b': [L, DFF], 'ffn_w_down': [L, DFF, D],
    }


class _Stop(Exception):
    pass


def build(SEQ, L, NS=16, do_decode=True, stop=None, dbg=False):
    NT = SEQ // 128
    nc = bass.Bass("TRN2", target_bir_lowering=False)
    fw = FW(nc)
    V, S, G, T = nc.vector, nc.scalar, nc.gpsimd, nc.tensor

    cur = {"i": 0}

    def cp(name):
        if stop == name or stop == f"{name}@{cur['i']}":
            raise _Stop()

    def dv(fn, r=(), w=()):
        return fw.op(fw.dve, fn, r, w)

    def ac(fn, r=(), w=()):
        return fw.op(fw.act, fn, r, w)

    def pl(fn, r=(), w=()):
        return fw.op(fw.pool, fn, r, w)

    def mm(fn, r=(), w=(), inc=True, dense=False):
        return fw.op(fw.pe, fn, r, w, inc if dense else True)

    def dma(out, in_, r=(), w=(), q=None, **kw):
        return fw.dma(q or fw.sp, out, in_, r, w, **kw)

    def din(name, shape):
        return Tl(nc.dram_tensor(name, shape, F32, kind="ExternalInput").ap(), name)

    def dout(name, shape):
        return Tl(nc.dram_tensor(name, shape, F32, kind="ExternalOutput").ap(), name)

    def dscr(name, shape):
        return Tl(nc.dram_tensor(name, shape, F32, kind="Internal").ap(), name)

    BIGN = 53200
    big = nc.alloc_sbuf_tensor("big", [128, BIGN], F32)
    reg = {"off": 0, "mark": 0}

    def sb(name, shape, dt=F32, alias=None):
        n = 1
        for d_ in shape[1:]:
            n *= d_
        words = n if dt == F32 else (n + 1) // 2
        words = (words + 7) // 8 * 8
        if alias is not None:
            off = alias.off
        else:
            off = reg["off"]
            assert off + words <= BIGN, f"SBUF region overflow at {name}: {off}+{words}"
            reg["off"] = off + words
        ap = big[0:shape[0], off:off + words]
        if dt != F32:
            ap = ap.bitcast(dt)
        ap = ap[:, 0:n]
        if len(shape) == 3:
            ap = ap.rearrange("p (a b) -> p a b", a=shape[1])
        elif len(shape) == 4:
            ap = ap.rearrange("p (a b c) -> p a b c", a=shape[1], b=shape[2])
        fw.tmap[name] = (off, tuple(shape), dt)
        return Tl(ap, name, tr=(alias.tr if alias is not None else None), off=off)

    def barrier():
        evs = []
        for q in (fw.pe, fw.act, fw.dve, fw.pool):
            if q.cnt > 0:
                evs.append(Ev(q.sem, q.cnt, id(q.sem)))
        evs += [e for e in fw.dlast if e is not None] + list(fw.swd)
        for q in (fw.pe, fw.act, fw.dve, fw.pool, fw.sp):
            for e in evs:
                fw._wait(q, e)

    def phase_reset():
        barrier()
        reg["off"] = reg["mark"]

    def psb(name, shape=(128, 512), dt=F32):
        t = Tl(nc.alloc_psum_tensor(name, list(shape), dt), name)
        t.excl = True
        return t

    x_p = din("x_p", [SEQ, D])
    x_s = din("x_s", [NS, D])
    c_all = din("c_all", [NS + 1, D])
    st_shift = din("st_shift", [L, NS, 1024])
    st_wkv = din("st_wkv", [L, NS * 4, 4096])
    st_k = din("st_k", [L, NS, 128, 2, 64])
    st_v = din("st_v", [L, NS, 128, 2, 64])
    st_conv = din("st_conv", [L, NS, 3, 1024])
    st_ssm = din("st_ssm", [L, NS * 8, 8192])
    st_ffn = din("st_ffn", [L, NS, 2, DFF])
    Wd = {k: din(k, s) for k, s in wshapes(L).items()}

    y_p = dout("y_p", [SEQ, D])
    y_s = dout("y_s", [NS, D])
    o_pshift = dout("o_pshift", [L, 1024])
    o_pwkv = dout("o_pwkv", [L, 4, 64, 64])
    o_pk = dout("o_pk", [L, 128, 128])
    o_pv = dout("o_pv", [L, 128, 128])
    o_pconv = dout("o_pconv", [L, 3, 1024])
    o_pssm = dout("o_pssm", [L, 512, 128])
    o_pffn = dout("o_pffn", [L, 2, DFF])
    o_sshift = dout("o_sshift", [L, NS, 1024])
    o_swkv = dout("o_swkv", [L, NS * 4, 4096])
    o_sk = dout("o_sk", [L, NS, 128, 2, 64])
    o_sv = dout("o_sv", [L, NS, 128, 2, 64])
    o_sconv = dout("o_sconv", [L, NS, 3, 1024])
    o_sssm = dout("o_sssm", [L, NS * 8, 8192])
    o_sffn = dout("o_sffn", [L, NS, 2, DFF])
    outs = [y_p, y_s, o_pshift, o_pwkv, o_pk, o_pv, o_pconv, o_pssm, o_pffn, o_sshift, o_swkv, o_sk, o_sv,
            o_sconv, o_sssm, o_sffn]

    dbg_o = dout("dbg_o", [SEQ, D]) if dbg else None
    xa = dscr("xa", [SEQ, D])
    xb = dscr("xb", [SEQ, D])
    modrow = dscr("modrow", [L, 6 * D])

    ident = sb("ident", [128, 128])
    identb = sb("identb", [128, 128], BF16)
    ones = sb("ones", [128, 128])
    blk = sb("blk", [128, 128])
    hsel = sb("hsel", [128, 2])
    m_us = sb("m_us", [128, 256])
    m_ls = sb("m_ls", [128, 128])
    tri = sb("tri", [128, 128])
    pl(lambda: G.memset(ones[:], 1.0), w=[ones])
    pl(lambda: G.memset(ident[:], 1.0), w=[ident])
    pl(lambda: G.affine_select(ident[:], ident[:], [[-1, 128]], ALU.is_equal, 0.0, base=0, channel_multiplier=1),
       r=[ident], w=[ident])
    dv(lambda: V.tensor_copy(identb[:], ident[:]), r=[ident], w=[identb])
    pl(lambda: G.memset(m_us[:], 1.0), w=[m_us])
    pl(lambda: G.affine_select(m_us[:, 0:128], m_us[:, 0:128], [[1, 128]], ALU.is_gt, 0.0, base=0,
                               channel_multiplier=-1), r=[m_us], w=[m_us])
    pl(lambda: G.affine_select(m_us[:, 128:256], m_us[:, 128:256], [[1, 128]], ALU.is_ge, 0.0, base=0,
                               channel_multiplier=-1), r=[m_us], w=[m_us])
    pl(lambda: G.memset(m_ls[:], 1.0), w=[m_ls])
    pl(lambda: G.affine_select(m_ls[:], m_ls[:], [[-1, 128]], ALU.is_gt, 0.0, base=0, channel_multiplier=1),
       r=[m_ls], w=[m_ls])
    pl(lambda: G.tensor_copy(tri[:], m_us[:, 128:256]), r=[m_us], w=[tri])
    pl(lambda: G.memset(blk[:], 0.0), w=[blk])
    pl(lambda: G.memset(blk[0:64, 0:64], 1.0), w=[blk])
    pl(lambda: G.memset(blk[64:128, 64:128], 1.0), w=[blk])
    pl(lambda: G.memset(hsel[:], 0.0), w=[hsel])
    pl(lambda: G.memset(hsel[0:64, 0:1], 1.0), w=[hsel])
    pl(lambda: G.memset(hsel[64:128, 1:2], 1.0), w=[hsel])
    bd32 = sb("bd32", [128, 128]); off1 = sb("off1", [128, 128]); off2 = sb("off2", [128, 128])
    for (t_, bs) in ((bd32, 32), (off1, 64)):
        pl(lambda t_=t_: G.memset(t_[:], 1.0), w=[t_])
        v3 = t_[:].rearrange("p (b j) -> p b j", j=bs)
        pl(lambda v3=v3, bs=bs: G.affine_select(v3, v3, [[-bs, 128 // bs], [0, bs]], ALU.is_ge, 0.0, base=0,
                                                channel_multiplier=1), r=[t_], w=[t_])
        pl(lambda v3=v3, bs=bs: G.affine_select(v3, v3, [[bs, 128 // bs], [0, bs]], ALU.is_ge, 0.0, base=bs - 1,
                                                channel_multiplier=-1), r=[t_], w=[t_])
    pl(lambda: G.tensor_scalar(off2[:], off1[:], -1.0, 1.0, ALU.mult, ALU.add), r=[off1], w=[off2])
    pl(lambda: G.tensor_tensor(off1[:], off1[:], bd32[:], ALU.subtract), r=[off1, bd32], w=[off1])
    negm = sb("negm", [128, 128])
    pl(lambda: G.memset(negm[:], 0.0), w=[negm])
    pl(lambda: G.affine_select(negm[:], negm[:], [[1, 128]], ALU.is_ge, -1e30, base=0, channel_multiplier=-1),
       r=[negm], w=[negm])
    m_lower = sb("m_lower", [128, 128])
    pl(lambda: G.memset(m_lower[:], 1.0), w=[m_lower])
    pl(lambda: G.affine_select(m_lower[:], m_lower[:], [[-1, 128]], ALU.is_ge, 0.0, base=0, channel_multiplier=1),
       r=[m_lower], w=[m_lower])


    pT = psb("pT", (128, 1024), BF16)
    pb = [psb(f"pb{i}") for i in range(7)]

    cmu = sb("cmu", [128, 8]); cmu1 = sb("cmu1", [128, 8])
    cw0 = sb("cw0", [128, 2]); ca0 = sb("ca0", [128, 2]); ckk = sb("ckk", [128, 2]); cka = sb("cka", [128, 2])
    cka1 = sb("cka1", [128, 2]); crk = sb("crk", [128, 2])
    w2t = sb("w2t", [128, 256]); a2t = sb("a2t", [128, 256]); g2t = sb("g2t", [128, 256])
    lngb = sb("lngb", [128, 512])
    qkg = sb("qkg", [128, 128])
    esink = sb("esink", [128, 4])
    cconvw = sb("cconvw", [128, 4, 8]); cconvb = sb("cconvb", [128, 8])
    r8 = sb("r8", [128, 24])
    sng = sb("sng", [128, 512])
    fcw = sb("fcw", [128, 3, NCH_FF]); fcb = sb("fcb", [128, NCH_FF])
    ostg = sb("ostg", [64, 512])
    reg["mark"] = reg["off"]
    NR = NS + 1
    modrows_s = dscr("modrows_s", [L, NS, 6 * D])

    def ada_layer(l):
        phase_reset()
        c_sb = sb("c_sb", [NR, D])
        cT = sb("cT", [128, 8, NR])
        mod_s = sb("mod_s", [NR, 6 * D])
        adab = sb("adab", [NR, 6 * D])
        adaw = [sb(f"adaw{i}", [128, 8, 512]) for i in range(2)]
        dma(c_sb[:], c_all[:, :], r=[c_all], w=[c_sb])
        ac(lambda: S.activation(c_sb[:], c_sb[:], AF.Silu), r=[c_sb], w=[c_sb])
        for kc in range(8):
            mm(lambda kc=kc: T.transpose(pb[0][:, kc * NR:(kc + 1) * NR], c_sb[:, kc * 128:(kc + 1) * 128],
                                         ident[0:NR, 0:NR]), r=[c_sb, ident], w=[pb[0]])
        dv(lambda: V.tensor_copy(cT[:].rearrange("p k n -> p (k n)"), pb[0][:, 0:8 * NR]), r=[pb[0]], w=[cT])
        dma(adab[:], Wd['ada_b'][l:l + 1, :].partition_broadcast(NR), r=[Wd['ada_b']], w=[adab])
        for gi in range(12):
            wt = adaw[gi % 2]
            dma(wt[:], Wd['ada_w'][l, :, gi * 512:(gi + 1) * 512].rearrange("(k p) n -> p k n", p=128),
                r=[Wd['ada_w']], w=[wt], q=fw.sp)
            ps = pb[gi % 2]
            for kc in range(8):
                mm(lambda kc=kc, wt=wt, ps=ps: T.matmul(ps[0:NR, :], cT[:, kc, :], wt[:, kc, :], start=(kc == 0),
                                                        stop=(kc == 7)), r=[cT, wt], w=[ps], inc=(kc == 7), dense=True)
            dv(lambda gi=gi, ps=ps: V.tensor_tensor(mod_s[:, gi * 512:(gi + 1) * 512], ps[0:NR, :],
                                                    adab[:, gi * 512:(gi + 1) * 512], ALU.add),
               r=[ps, adab], w=[mod_s])
        dma(modrow[l:l + 1, :], mod_s[NS:NS + 1, :], r=[mod_s], w=[modrow])
        dma(modrows_s[l], mod_s[0:NS, :], r=[mod_s], w=[modrows_s])

    def load_mod(l, base, gname):
        modp = sb("modp", [128, 3 * D])
        gtmp = sb("gtmp", [128, D], alias=hf)
        dma(modp[:], modrow[l:l + 1, base:base + 3 * D].partition_broadcast(128), r=[modrow], w=[modp])
        dma(gtmp[:], Wd[gname][l:l + 1, :].partition_broadcast(128), r=[Wd[gname]], w=[gtmp])
        dv(lambda: V.scalar_tensor_tensor(modp[:, D:2 * D], modp[:, D:2 * D], 1.0, gtmp[:], ALU.add, ALU.mult),
           r=[modp, gtmp], w=[modp])
        return modp

    def load_consts(l):
        W = Wd
        dma(cmu[:], W['rwkv_mu'][l, :].rearrange("(k p) -> p k", p=128), r=[W['rwkv_mu']], w=[cmu],
            allow_slow_non_contiguous=True)
        dv(lambda: V.tensor_scalar(cmu1[:], cmu[:], -1.0, 1.0, ALU.mult, ALU.add), r=[cmu], w=[cmu1])
        for t_, nm in ((cw0, 'rwkv_w0'), (ca0, 'rwkv_a0'), (ckk, 'rwkv_k_k'), (cka, 'rwkv_k_a'), (crk, 'rwkv_r_k')):
            dma(t_[:], W[nm][l, :].rearrange("(k p) -> p k", p=128), r=[W[nm]], w=[t_],
                allow_slow_non_contiguous=True)
        dv(lambda: V.tensor_scalar(cka1[:], cka[:], -1.0, 1.0, ALU.mult, ALU.add), r=[cka], w=[cka1])
        dma(w2t[0:64, :], W['rwkv_w2'][l, :, :], r=[W['rwkv_w2']], w=[w2t])
        dma(a2t[64:128, :], W['rwkv_a2'][l, :, :], r=[W['rwkv_a2']], w=[a2t])
        dma(a2t[0:64, :], W['rwkv_a2'][l, :, :], r=[W['rwkv_a2']], w=[a2t])
        dma(g2t[:], W['rwkv_g2'][l, :, :], r=[W['rwkv_g2']], w=[g2t])
        dma(lngb[:, 0:256], W['rwkv_ln_g'][l:l + 1, :].partition_broadcast(128), r=[W['rwkv_ln_g']], w=[lngb])
        dma(lngb[:, 256:512], W['rwkv_ln_b'][l:l + 1, :].partition_broadcast(128), r=[W['rwkv_ln_b']], w=[lngb])
        dma(qkg[:, 0:64], W['attn_q_norm_g'][l:l + 1, :].partition_broadcast(128), r=[W['attn_q_norm_g']], w=[qkg])
        dma(qkg[:, 64:128], W['attn_k_norm_g'][l:l + 1, :].partition_broadcast(128), r=[W['attn_k_norm_g']],
            w=[qkg])
        dma(esink[:], W['attn_sinks'][l:l + 1, :].partition_broadcast(128), r=[W['attn_sinks']], w=[esink])
        ac(lambda: S.activation(esink[:], esink[:], AF.Exp), r=[esink], w=[esink])
        dma(cconvw[:], W['ssm_conv_w'][l, :, :].rearrange("j (k p) -> p j k", p=128), r=[W['ssm_conv_w']],
            w=[cconvw], allow_slow_non_contiguous=True)
        dma(cconvb[:], W['ssm_conv_b'][l, :].rearrange("(k p) -> p k", p=128), r=[W['ssm_conv_b']], w=[cconvb],
            allow_slow_non_contiguous=True)
        dma(r8[:, 0:8], W['ssm_dt_bias'][l:l + 1, :].partition_broadcast(128), r=[W['ssm_dt_bias']], w=[r8])
        dma(r8[:, 8:16], W['ssm_a_log'][l:l + 1, :].partition_broadcast(128), r=[W['ssm_a_log']], w=[r8])
        dma(r8[:, 16:24], W['ssm_d'][l:l + 1, :].partition_broadcast(128), r=[W['ssm_d']], w=[r8])
        ac(lambda: S.activation(r8[:, 8:16], r8[:, 8:16], AF.Exp), r=[r8], w=[r8])
        dv(lambda: V.tensor_scalar(r8[:, 8:16], r8[:, 8:16], -1.0, None, ALU.mult), r=[r8], w=[r8])
        dma(sng[:], W['ssm_norm_g'][l:l + 1, :].partition_broadcast(128), r=[W['ssm_norm_g']], w=[sng])
        dma(fcw[:], W['ffn_conv_w'][l, :, :].rearrange("j (k p) -> p j k", p=128), r=[W['ffn_conv_w']], w=[fcw],
            allow_slow_non_contiguous=True)
        dma(fcb[:], W['ffn_conv_b'][l, :].rearrange("(k p) -> p k", p=128), r=[W['ffn_conv_b']], w=[fcb],
            allow_slow_non_contiguous=True)

    def rmsnorm_mod(xtile, out_bf):
        ac(lambda: S.activation(junk[:], xtile[:], AF.Square, accum_out=ss[:, 0:1]), r=[xtile], w=[junk, ss])
        dv(lambda: V.tensor_scalar(ss[:, 1:2], ss[:, 0:1], 1.0 / D, EPS, ALU.mult, ALU.add), r=[ss], w=[ss])
        ac(lambda: S.activation(ss[:, 1:2], ss[:, 1:2], AF.Ln), r=[ss], w=[ss])
        ac(lambda: S.activation(ss[:, 2:3], ss[:, 1:2], AF.Exp, scale=-0.5), r=[ss], w=[ss])
        dv(lambda: V.scalar_tensor_tensor(hf[:], xtile[:], ss[:, 2:3], modp[:, D:2 * D], ALU.mult, ALU.mult),
           r=[xtile, ss, modp], w=[hf])
        dv(lambda: V.tensor_tensor(out_bf[:], hf[:], modp[:, 0:D], ALU.add), r=[hf, modp], w=[out_bf])

    def to_fm(src_bf, dst):
        for kc in range(8):
            mm(lambda kc=kc: T.transpose(pT[:, kc * 128:(kc + 1) * 128], src_bf[:, kc * 128:(kc + 1) * 128], identb[:]),
               r=[src_bf, identb], w=[pT])
        ac(lambda: S.copy(dst[:].rearrange("p k n -> p (k n)"), pT[:]), r=[pT], w=[dst])

    xs_d = dscr("xs_d", [NS, D])
    vec_d = dscr("vec_d", [NS, 4, 6, 64])
    yd_d = dscr("yd_d", [NS * 4, 64])
    qkv_d = dscr("qkv_d", [NS, 512])
    od_d = dscr("od_d", [2 * NS, 128])
    ssd_d = dscr("ssd_d", [NS, 8, 321])
    ysd_d = dscr("ysd_d", [NS * 8, 64])
    N2 = 2 * NS

    def drow(dst_ap, dst_t, src_t, src_ap, n=NS):
        dma(dst_ap, src_ap.partition_broadcast(n), r=[src_t], w=[dst_t])

    def d_norm_mod(l, base, gname, xsd, hbd):
        modd = sb("modd", [NS, 3 * D]); hfd = sb("hfd", [NS, D]); gt = sb("gt", [NS, D], alias=hfd); sd = sb("sd", [NS, 8])
        jk = sb("jk", [NS, D], BF16, alias=hbd)
        dma(modd[:], modrows_s[l, :, base:base + 3 * D], r=[modrows_s], w=[modd])
        drow(gt[:], gt, Wd[gname], Wd[gname][l:l + 1, :])
        dv(lambda: V.scalar_tensor_tensor(modd[:, D:2 * D], modd[:, D:2 * D], 1.0, gt[:], ALU.add, ALU.mult),
           r=[modd, gt], w=[modd])
        ac(lambda: S.activation(jk[:], xsd[:], AF.Square, accum_out=sd[:, 0:1]), r=[xsd], w=[jk, sd])
        dv(lambda: V.tensor_scalar(sd[:, 1:2], sd[:, 0:1], 1.0 / D, EPS, ALU.mult, ALU.add), r=[sd], w=[sd])
        ac(lambda: S.activation(sd[:, 1:2], sd[:, 1:2], AF.Sqrt), r=[sd], w=[sd])
        dv(lambda: V.reciprocal(sd[:, 2:3], sd[:, 1:2]), r=[sd], w=[sd])
        dv(lambda: V.scalar_tensor_tensor(hfd[:], xsd[:], sd[:, 2:3], modd[:, D:2 * D], ALU.mult, ALU.mult),
           r=[xsd, sd, modd], w=[hfd])
        dv(lambda: V.tensor_tensor(hbd[:], hfd[:], modd[:, 0:D], ALU.add), r=[hfd, modd], w=[hbd])
        return modd, hfd

    def d_to_fm(src_bf, dst, nch):
        for c0 in range(0, nch, 8):
            n = min(8, nch - c0)
            for kc in range(n):
                mm(lambda kc=kc, c0=c0: T.transpose(pT[:, kc * NS:(kc + 1) * NS], src_bf[:, (c0 + kc) * 128:(c0 + kc + 1) * 128],
                                                    identb[0:NS, 0:NS]), r=[src_bf, identb], w=[pT])
            ac(lambda c0=c0, n=n: S.copy(dst[:, c0:c0 + n, :], pT[:, 0:n * NS].rearrange("p (k n) -> p k n", k=n)),
               r=[pT], w=[dst])

    def d_linear(xT, nk, wv, wt_, c0, ncols, dst, dcol):
        done = 0
        gi = 0
        while done < ncols:
            n = min(512, ncols - done)
            ps = pb[gi % 2]
            for kc in range(nk):
                mm(lambda kc=kc, ps=ps, n=n, done=done: T.matmul(ps[0:NS, 0:n], xT[:, kc, :],
                                                                 wv[:, kc, c0 + done:c0 + done + n], start=(kc == 0),
                                                                 stop=(kc == nk - 1)), r=[xT, wt_], w=[ps],
                   inc=(kc == nk - 1), dense=True)
            if dst is not None:
                ac(lambda ps=ps, n=n, done=done: S.copy(dst[:, dcol + done:dcol + done + n], ps[0:NS, 0:n]), r=[ps], w=[dst])
            done += n
            gi += 1

    def d_residual(xsd, modd, hfd, ps_list):
        for n2, ps in enumerate(ps_list):
            dv(lambda n2=n2, ps=ps: V.tensor_tensor(hfd[:, n2 * 512:(n2 + 1) * 512], ps[0:NS, :],
                                                    modd[:, 2 * D + n2 * 512:2 * D + (n2 + 1) * 512], ALU.mult),
               r=[ps, modd], w=[hfd])
        dv(lambda: V.tensor_tensor(xsd[:], xsd[:], hfd[:], ALU.add), r=[xsd, hfd], w=[xsd])

    def decode_mixer(l):
        W = Wd
        phase_reset()
        wmix = sb("wmix", [128, 8 * PROJ + 8 * D], BF16)
        w_in_v = wmix[:, 0:8 * PROJ].rearrange("p (k n) -> p k n", k=8)
        w_out_v = wmix[:, 8 * PROJ:8 * PROJ + 8 * D].rearrange("p (k n) -> p k n", k=8)
        fw.dma(fw.pool, w_in_v, W['w_in'][l].rearrange("(k p) n -> p k n", p=128), reads=[W['w_in']], writes=[wmix])
        fw.dma(fw.pool, w_out_v, W['w_out'][l].rearrange("(k p) n -> p k n", p=128), reads=[W['w_out']], writes=[wmix])
        xsd = sb("xsd", [NS, D]); hbd = sb("hbd", [NS, D], BF16); hTd = sb("hTd", [128, 8, NS], BF16)
        Pd = sb("Pd", [NS, PROJ]); ymd = sb("ymd", [NS, D])
        src = x_s if l == 0 else xs_d
        dma(xsd[:], src[:, :], r=[src], w=[xsd])
        modd, hfd = d_norm_mod(l, 0, 'norm_mix_g', xsd, hbd)
        d_to_fm(hbd, hTd, 8)
        d_linear(hTd, 8, w_in_v, wmix, 0, PROJ, Pd, 0)
        rows = sb("rows", [NS, 1024 + 5 * 256])
        drow(rows[:, 0:1024], rows, W['rwkv_mu'], W['rwkv_mu'][l:l + 1, :])
        for j, nm in enumerate(('rwkv_w0', 'rwkv_a0', 'rwkv_k_k', 'rwkv_k_a', 'rwkv_r_k')):
            drow(rows[:, 1024 + j * 256:1024 + (j + 1) * 256], rows, W[nm], W[nm][l:l + 1, :])
        R0 = 1024
        prv = sb("prv", [NS, 1024]); xq = sb("xq", [NS, 1024])
        dma(prv[:], st_shift[l], r=[st_shift], w=[prv])
        dma(o_sshift[l], Pd[:, 0:1024], r=[Pd], w=[o_sshift])
        dv(lambda: V.tensor_tensor(prv[:], prv[:], Pd[:, 0:1024], ALU.subtract), r=[prv, Pd], w=[prv])
        dv(lambda: V.tensor_tensor(prv[:], prv[:], rows[:, 0:1024], ALU.mult), r=[prv, rows], w=[prv])
        dv(lambda: V.tensor_tensor(xq[:], prv[:], Pd[:, 0:1024], ALU.add), r=[prv, Pd], w=[xq])
        lsg = sb("lsg", [NS, 256]); lT = sb("lT", [128, 3 * NS])
        ac(lambda: S.activation(lsg[:, 0:64], xq[:, 768:832], AF.Tanh), r=[xq], w=[lsg])
        ac(lambda: S.copy(lsg[:, 64:128], xq[:, 832:896]), r=[xq], w=[lsg])
        ac(lambda: S.activation(lsg[:, 128:256], xq[:, 896:1024], AF.Sigmoid), r=[xq], w=[lsg])
        mm(lambda: T.transpose(pb[2][0:64, 0:NS], lsg[:, 0:64], ident[0:NS, 0:NS]), r=[lsg, ident], w=[pb[2]])
        mm(lambda: T.transpose(pb[2][0:64, NS:2 * NS], lsg[:, 64:128], ident[0:NS, 0:NS]), r=[lsg, ident], w=[pb[2]])
        mm(lambda: T.transpose(pb[2][:, 2 * NS:3 * NS], lsg[:, 128:256], ident[0:NS, 0:NS]), r=[lsg, ident], w=[pb[2]])
        dv(lambda: V.tensor_copy(lT[0:64, 0:2 * NS], pb[2][0:64, 0:2 * NS]), r=[pb[2]], w=[lT])
        dv(lambda: V.tensor_copy(lT[:, 2 * NS:3 * NS], pb[2][:, 2 * NS:3 * NS]), r=[pb[2]], w=[lT])
        mm(lambda: T.matmul(pb[3][0:NS, 0:256], lT[0:64, 0:NS], w2t[0:64, :], start=True, stop=True), r=[lT, w2t], w=[pb[3]])
        mm(lambda: T.matmul(pb[3][0:NS, 256:512], lT[0:64, NS:2 * NS], a2t[0:64, :], start=True, stop=True), r=[lT, a2t],
           w=[pb[3]])
        mm(lambda: T.matmul(pb[4][0:NS, 0:256], lT[:, 2 * NS:3 * NS], g2t[:], start=True, stop=True), r=[lT, g2t], w=[pb[4]])
        v6 = sb("v6", [NS, 6, 256])
        aa_ = sb("aa_", [NS, 256]); kk_ = sb("kk_", [NS, 256]); gate_ = sb("gate_", [NS, 256]); t_ = sb("t_", [NS, 256])
        s4 = sb("s4", [NS, 16])
        dv(lambda: V.tensor_tensor(t_[:], pb[3][0:NS, 0:256], rows[:, R0:R0 + 256], ALU.add), r=[pb[3], rows], w=[t_])
        ac(lambda: S.activation(t_[:], t_[:], AF.Sigmoid), r=[t_], w=[t_])
        ac(lambda: S.activation(v6[:, 3, :], t_[:], AF.Exp, scale=-math.exp(-0.5)), r=[t_], w=[v6])
        dv(lambda: V.tensor_tensor(aa_[:], pb[3][0:NS, 256:512], rows[:, R0 + 256:R0 + 512], ALU.add), r=[pb[3], rows],
           w=[aa_])
        ac(lambda: S.activation(aa_[:], aa_[:], AF.Sigmoid), r=[aa_], w=[aa_])
        ac(lambda: S.copy(gate_[:], pb[4][0:NS, 0:256]), r=[pb[4]], w=[gate_])
        dv(lambda: V.tensor_tensor(kk_[:], xq[:, 256:512], rows[:, R0 + 512:R0 + 768], ALU.mult), r=[xq, rows], w=[kk_])
        dv(lambda: V.tensor_tensor(t_[:], kk_[:], kk_[:], ALU.mult), r=[kk_], w=[t_])
        dv(lambda: V.tensor_reduce(s4[:, 0:4], t_[:].rearrange("p (h d) -> p h d", h=4), AX.X, ALU.add), r=[t_], w=[s4])
        ac(lambda: S.activation(s4[:, 0:4], s4[:, 0:4], AF.Sqrt), r=[s4], w=[s4])
        dv(lambda: V.tensor_scalar(s4[:, 0:4], s4[:, 0:4], 1e-12, None, ALU.max), r=[s4], w=[s4])
        dv(lambda: V.reciprocal(s4[:, 0:4], s4[:, 0:4]), r=[s4], w=[s4])
        dv(lambda: V.tensor_tensor(kk_[:].rearrange("p (h d) -> p h d", h=4), kk_[:].rearrange("p (h d) -> p h d", h=4),
                                   s4[:, 0:4].unsqueeze(2).to_broadcast([NS, 4, 64]), ALU.mult), r=[kk_, s4], w=[kk_])
        dv(lambda: V.tensor_tensor(t_[:], aa_[:], rows[:, R0 + 768:R0 + 1024], ALU.mult), r=[aa_, rows], w=[t_])
        dv(lambda: V.tensor_tensor(t_[:], t_[:], rows[:, R0 + 768:R0 + 1024], ALU.subtract), r=[t_, rows], w=[t_])
        dv(lambda: V.scalar_tensor_tensor(v6[:, 1, :], t_[:], 1.0, xq[:, 256:512], ALU.add, ALU.mult), r=[t_, xq], w=[v6])
        ac(lambda: S.copy(v6[:, 0, :], xq[:, 0:256]), r=[xq], w=[v6])
        ac(lambda: S.copy(v6[:, 2, :], xq[:, 512:768]), r=[xq], w=[v6])
        dv(lambda: V.tensor_scalar(v6[:, 4, :], kk_[:], -1.0, None, ALU.mult), r=[kk_], w=[v6])
        dv(lambda: V.tensor_tensor(v6[:, 5, :], kk_[:], aa_[:], ALU.mult), r=[kk_, aa_], w=[v6])
        for j in range(6):
            dma(vec_d[:, :, j, :], v6[:, j, :].rearrange("p (h d) -> p h d", h=4), r=[v6], w=[vec_d])
        dbig = sb("dbig", [128, 10368])
        Sd = dbig[0:64, 0:4096].rearrange("p (v k) -> p v k", v=64)
        tmpS = dbig[0:64, 4096:8192].rearrange("p (v k) -> p v k", v=64)
        vv = sb("vv", [64, 6, 64]); sa = sb("sa", [64, 64]); yv = sb("yv", [64, 64])
        dma(vv[:], vec_d[:].rearrange("n h j d -> (n h) j d"), r=[vec_d], w=[vv])
        dma(dbig[0:64, 0:4096], st_wkv[l], r=[st_wkv], w=[dbig])

        def bv(j):
            return vv[:, j, :].unsqueeze(1).to_broadcast([64, 64, 64])

        def bk(ap2):
            return ap2.unsqueeze(2).to_broadcast([64, 64, 64])
        dv(lambda: V.tensor_tensor(tmpS, Sd, bv(4), ALU.mult), r=[dbig, vv], w=[dbig])
        dv(lambda: V.tensor_reduce(sa[:], tmpS, AX.X, ALU.add), r=[dbig], w=[sa])
        dv(lambda: V.tensor_tensor(Sd, Sd, bv(3), ALU.mult), r=[dbig, vv], w=[dbig])
        dv(lambda: V.tensor_tensor(tmpS, bk(sa[:]), bv(5), ALU.mult), r=[sa, vv], w=[dbig])
        dv(lambda: V.tensor_tensor(Sd, Sd, tmpS, ALU.add), r=[dbig], w=[dbig])
        dv(lambda: V.tensor_tensor(tmpS, bk(vv[:, 2, :]), bv(1), ALU.mult), r=[vv], w=[dbig])
        dv(lambda: V.tensor_tensor(Sd, Sd, tmpS, ALU.add), r=[dbig], w=[dbig])
        dv(lambda: V.tensor_tensor(tmpS, Sd, bv(0), ALU.mult), r=[dbig, vv], w=[dbig])
        dv(lambda: V.tensor_reduce(yv[:], tmpS, AX.X, ALU.add), r=[dbig], w=[yv])
        dma(o_swkv[l], dbig[0:64, 0:4096], r=[dbig], w=[o_swkv])
        dma(yd_d[:, :], yv[:], r=[yv], w=[yd_d])
        yr = sb("yr", [NS, 256]); y2 = sb("y2", [NS, 256])
        dma(yr[:], yd_d[:].rearrange("(n h) d -> n (h d)", h=4), r=[yd_d], w=[yr])
        y3 = yr[:].rearrange("p (h d) -> p h d", h=4)
        dv(lambda: V.tensor_reduce(s4[:, 0:4], y3, AX.X, ALU.add), r=[yr], w=[s4])
        dv(lambda: V.tensor_tensor(y2[:], yr[:], yr[:], ALU.mult), r=[yr], w=[y2])
        dv(lambda: V.tensor_reduce(s4[:, 4:8], y2[:].rearrange("p (h d) -> p h d", h=4), AX.X, ALU.add), r=[y2], w=[s4])
        dv(lambda: V.tensor_scalar(s4[:, 0:8], s4[:, 0:8], 1.0 / 64, None, ALU.mult), r=[s4], w=[s4])
        dv(lambda: V.tensor_tensor(s4[:, 8:12], s4[:, 0:4], s4[:, 0:4], ALU.mult), r=[s4], w=[s4])
        dv(lambda: V.tensor_tensor(s4[:, 8:12], s4[:, 4:8], s4[:, 8:12], ALU.subtract), r=[s4], w=[s4])
        dv(lambda: V.tensor_scalar(s4[:, 8:12], s4[:, 8:12], GN_EPS, None, ALU.add), r=[s4], w=[s4])
        ac(lambda: S.activation(s4[:, 8:12], s4[:, 8:12], AF.Sqrt), r=[s4], w=[s4])
        dv(lambda: V.reciprocal(s4[:, 12:16], s4[:, 8:12]), r=[s4], w=[s4])
        dv(lambda: V.tensor_tensor(y3, y3, s4[:, 0:4].unsqueeze(2).to_broadcast([NS, 4, 64]), ALU.subtract), r=[yr, s4], w=[yr])
        dv(lambda: V.tensor_tensor(y3, y3, s4[:, 12:16].unsqueeze(2).to_broadcast([NS, 4, 64]), ALU.mult), r=[yr, s4], w=[yr])
        dv(lambda: V.tensor_tensor(yr[:], yr[:], lngb[0:NS, 0:256], ALU.mult), r=[yr, lngb], w=[yr])
        dv(lambda: V.tensor_tensor(yr[:], yr[:], lngb[0:NS, 256:512], ALU.add), r=[yr, lngb], w=[yr])
        dv(lambda: V.tensor_tensor(y2[:], v6[:, 0, :], v6[:, 1, :], ALU.mult), r=[v6], w=[y2])
        dv(lambda: V.tensor_tensor(y2[:], y2[:], rows[:, R0 + 1024:R0 + 1280], ALU.mult), r=[y2, rows], w=[y2])
        dv(lambda: V.tensor_reduce(s4[:, 0:4], y2[:].rearrange("p (h d) -> p h d", h=4), AX.X, ALU.add), r=[y2], w=[s4])
        dv(lambda: V.tensor_tensor(y2[:].rearrange("p (h d) -> p h d", h=4), v6[:, 2, :].rearrange("p (h d) -> p h d", h=4),
                                   s4[:, 0:4].unsqueeze(2).to_broadcast([NS, 4, 64]), ALU.mult), r=[v6, s4], w=[y2])
        dv(lambda: V.tensor_tensor(yr[:], yr[:], y2[:], ALU.add), r=[yr, y2], w=[yr])
        dv(lambda: V.tensor_tensor(ymd[:, 0:256], yr[:], gate_[:], ALU.mult), r=[yr, gate_], w=[ymd])
        qd = sb("qd", [NS, 512], alias=prv); s8 = sb("s8", [NS, 8])
        dv(lambda: V.tensor_tensor(qd[:, 0:384], Pd[:, 1024:1408], Pd[:, 1024:1408], ALU.mult), r=[Pd], w=[qd])
        dv(lambda: V.tensor_reduce(s8[:, 0:6], qd[:, 0:384].rearrange("p (h d) -> p h d", h=6), AX.X, ALU.add), r=[qd], w=[s8])
        dv(lambda: V.tensor_scalar(s8[:, 0:6], s8[:, 0:6], 1.0 / 64, EPS, ALU.mult, ALU.add), r=[s8], w=[s8])
        ac(lambda: S.activation(s8[:, 0:6], s8[:, 0:6], AF.Sqrt), r=[s8], w=[s8])
        dv(lambda: V.reciprocal(s8[:, 0:6], s8[:, 0:6]), r=[s8], w=[s8])
        dv(lambda: V.tensor_tensor(qd[:, 0:384].rearrange("p (h d) -> p h d", h=6),
                                   Pd[:, 1024:1408].rearrange("p (h d) -> p h d", h=6),
                                   s8[:, 0:6].unsqueeze(2).to_broadcast([NS, 6, 64]), ALU.mult), r=[Pd, s8], w=[qd])
        dv(lambda: V.tensor_tensor(qd[:, 0:256].rearrange("p (h d) -> p h d", h=4), qd[:, 0:256].rearrange("p (h d) -> p h d", h=4),
                                   qkg[0:NS, 0:64].unsqueeze(1).to_broadcast([NS, 4, 64]), ALU.mult), r=[qd, qkg], w=[qd])
        dv(lambda: V.tensor_tensor(qd[:, 256:384].rearrange("p (h d) -> p h d", h=2),
                                   qd[:, 256:384].rearrange("p (h d) -> p h d", h=2),
                                   qkg[0:NS, 64:128].unsqueeze(1).to_broadcast([NS, 2, 64]), ALU.mult), r=[qd, qkg], w=[qd])
        ac(lambda: S.copy(qd[:, 384:512], Pd[:, 1408:1536]), r=[Pd], w=[qd])
        dma(qkv_d[:, :], qd[:], r=[qd], w=[qkv_d])
        q2 = sb("q2", [N2, 128]); esd = sb("esd", [N2, 2]); sc = sb("sc", [N2, 2, 129]); dn = sb("dn", [N2, 4])
        o2 = sb("o2", [N2, 128])
        KF = dbig[0:N2, 0:129 * 64].rearrange("p (s d) -> p s d", s=129)
        TM_ = dbig[0:N2, 8256:8256 + 2112]
        for kh in range(2):
            ps_ = slice(kh * NS, (kh + 1) * NS)
            dma(q2[ps_, :], qkv_d[:, kh * 128:(kh + 1) * 128], r=[qkv_d], w=[q2])
            dma(esd[ps_, :], W['attn_sinks'][l:l + 1, 2 * kh:2 * kh + 2].partition_broadcast(NS), r=[W['attn_sinks']], w=[esd])
        ac(lambda: S.activation(esd[:], esd[:], AF.Exp), r=[esd], w=[esd])
        for which, (st_c, off_new, o_c) in enumerate(((st_k, 256, o_sk), (st_v, 384, o_sv))):
            for kh in range(2):
                ps_ = slice(kh * NS, (kh + 1) * NS)
                dma(KF[ps_, 0:128, :], st_c[l, :, :, kh, :], r=[st_c], w=[dbig])
                dma(KF[ps_, 128, :], qkv_d[:, off_new + kh * 64:off_new + (kh + 1) * 64], r=[qkv_d], w=[dbig])
            dma(o_c[l, :, 0:127, :, :], st_c[l, :, 1:128, :, :], r=[st_c], w=[o_c])
            dma(o_c[l, :, 127, :, :], qkv_d[:, off_new:off_new + 128].rearrange("n (k d) -> n k d", k=2), r=[qkv_d], w=[o_c])
            if which == 0:
                for g in range(2):
                    for (s0, s1) in ((0, 33), (33, 66), (66, 99), (99, 129)):
                        tv = TM_[:, 0:(s1 - s0) * 64].rearrange("p (s d) -> p s d", d=64)
                        dv(lambda g=g, s0=s0, s1=s1, tv=tv: V.tensor_tensor(
                            tv, KF[:, s0:s1, :], q2[:, g * 64:(g + 1) * 64].unsqueeze(1).to_broadcast([N2, s1 - s0, 64]),
                            ALU.mult), r=[dbig, q2], w=[dbig])
                        dv(lambda g=g, s0=s0, s1=s1, tv=tv: V.tensor_reduce(sc[:, g, s0:s1], tv, AX.X, ALU.add), r=[dbig],
                           w=[sc])
                ac(lambda: S.activation(sc[:], sc[:], AF.Exp, scale=0.125), r=[sc], w=[sc])
                dv(lambda: V.tensor_reduce(dn[:, 0:2], sc[:], AX.X, ALU.add), r=[sc], w=[dn])
                dv(lambda: V.tensor_tensor(dn[:, 0:2], dn[:, 0:2], esd[:], ALU.add), r=[dn, esd], w=[dn])
                dv(lambda: V.reciprocal(dn[:, 2:4], dn[:, 0:2]), r=[dn], w=[dn])
                dv(lambda: V.tensor_tensor(sc[:], sc[:], dn[:, 2:4].unsqueeze(2).to_broadcast([N2, 2, 129]), ALU.mult),
                   r=[sc, dn], w=[sc])
            else:
                for g in range(2):
                    for (d0, d1) in ((0, 16), (16, 32), (32, 48), (48, 64)):
                        tv = TM_[:, 0:(d1 - d0) * 129].rearrange("p (d s) -> p d s", s=129)
                        dv(lambda g=g, d0=d0, d1=d1, tv=tv: V.tensor_tensor(
                            tv, KF[:, :, d0:d1].rearrange("p s d -> p d s"),
                            sc[:, g, :].unsqueeze(1).to_broadcast([N2, d1 - d0, 129]), ALU.mult), r=[dbig, sc], w=[dbig])
                        dv(lambda g=g, d0=d0, d1=d1, tv=tv: V.tensor_reduce(o2[:, g * 64 + d0:g * 64 + d1], tv, AX.X, ALU.add),
                           r=[dbig], w=[o2])
        dma(od_d[:, :], o2[:], r=[o2], w=[od_d])
        for kh in range(2):
            dma(ymd[:, 256 + kh * 128:256 + (kh + 1) * 128], od_d[kh * NS:(kh + 1) * NS, :], r=[od_d], w=[ymd])
        cbuf = sb("cbuf", [NS, 3, 256]); cw = sb("cw", [NS, 5, 256], alias=rows); cv = sb("cv", [NS, 1024], alias=prv)
        for j in range(2):
            dma(o_sconv[l, :, j, :], st_conv[l, :, j + 1, :], r=[st_conv], w=[o_sconv])
        dma(o_sconv[l, :, 2, :], Pd[:, 2048:3072], r=[Pd], w=[o_sconv])
        for cc in range(4):
            cs = slice(cc * 256, (cc + 1) * 256)
            dma(cbuf[:], st_conv[l, :, :, cs], r=[st_conv], w=[cbuf])
            for j in range(4):
                drow(cw[:, j, :], cw, W['ssm_conv_w'], W['ssm_conv_w'][l, j:j + 1, cs])
            drow(cw[:, 4, :], cw, W['ssm_conv_b'], W['ssm_conv_b'][l:l + 1, cs])
            dv(lambda cs=cs: V.tensor_tensor(cv[:, cs], Pd[:, 2048 + cs.start:2048 + cs.stop], cw[:, 3, :], ALU.mult),
               r=[Pd, cw], w=[cv])
            dv(lambda cs=cs: V.tensor_tensor(cv[:, cs], cv[:, cs], cw[:, 4, :], ALU.add), r=[cv, cw], w=[cv])
            for j in range(3):
                dv(lambda cs=cs, j=j: V.tensor_tensor(t_[:], cbuf[:, j, :], cw[:, j, :], ALU.mult), r=[cbuf, cw], w=[t_])
                dv(lambda cs=cs: V.tensor_tensor(cv[:, cs], cv[:, cs], t_[:], ALU.add), r=[cv, t_], w=[cv])
        ac(lambda: S.activation(cv[:], cv[:], AF.Silu), r=[cv], w=[cv])
        d8 = sb("d8", [NS, 24]); zsd = sb("zsd", [NS, 512], alias=xq); repb = sb("repb", [NS, 8, 128]); yc_ = sb("yc_", [NS, 512])
        dv(lambda: V.tensor_tensor(d8[:, 0:8], Pd[:, 3072:3080], r8[0:NS, 0:8], ALU.add), r=[Pd, r8], w=[d8])
        ac(lambda: S.activation(d8[:, 0:8], d8[:, 0:8], AF.Exp), r=[d8], w=[d8])
        ac(lambda: S.activation(d8[:, 0:8], d8[:, 0:8], AF.Ln, bias=1.0), r=[d8], w=[d8])
        dv(lambda: V.tensor_tensor(d8[:, 8:16], d8[:, 0:8], r8[0:NS, 8:16], ALU.mult), r=[d8, r8], w=[d8])
        ac(lambda: S.activation(d8[:, 16:24], d8[:, 8:16], AF.Exp), r=[d8], w=[d8])
        dma(ssd_d[:, :, 320], d8[:, 16:24], r=[d8], w=[ssd_d], allow_slow_non_contiguous=True)
        dv(lambda: V.tensor_tensor(yc_[:].rearrange("p (h d) -> p h d", h=8), cv[:, 0:512].rearrange("p (h d) -> p h d", h=8),
                                   d8[:, 0:8].unsqueeze(2).to_broadcast([NS, 8, 64]), ALU.mult), r=[cv, d8], w=[yc_])
        dma(ssd_d[:, :, 0:64], yc_[:].rearrange("p (h d) -> p h d", h=8), r=[yc_], w=[ssd_d])
        for (o_, c0) in ((64, 512), (192, 768)):
            for g in range(2):
                dv(lambda c0=c0, g=g: V.tensor_copy(
                    repb[:, g * 4:(g + 1) * 4, :],
                    cv[:, c0 + g * 128:c0 + (g + 1) * 128].unsqueeze(1).to_broadcast([NS, 4, 128])), r=[cv], w=[repb])
            dma(ssd_d[:, :, o_:o_ + 128], repb[:], r=[repb], w=[ssd_d])
        pv = sb("pv", [128, 321]); yh = sb("yh", [128, 64])
        dma(pv[:], ssd_d[:].rearrange("n h f -> (n h) f"), r=[ssd_d], w=[pv])
        dma(dbig[:, 0:8192], st_ssm[l], r=[st_ssm], w=[dbig])
        for (p0_, p1_) in ((0, 16), (16, 32), (32, 48), (48, 64)):
            Hh = dbig[:, p0_ * 128:p1_ * 128].rearrange("p (q s) -> p q s", s=128)
            Th = dbig[:, 8192:8192 + 2048].rearrange("p (q s) -> p q s", s=128)
            dv(lambda Th=Th, p0_=p0_, p1_=p1_: V.tensor_tensor(
                Th, pv[:, p0_:p1_].unsqueeze(2).to_broadcast([128, 16, 128]),
                pv[:, 64:192].unsqueeze(1).to_broadcast([128, 16, 128]), ALU.mult), r=[pv], w=[dbig])
            dv(lambda Hh=Hh, Th=Th: V.scalar_tensor_tensor(Hh, Hh, pv[:, 320:321], Th, ALU.mult, ALU.add), r=[dbig, pv],
               w=[dbig])
            dv(lambda Hh=Hh, Th=Th: V.tensor_tensor(Th, Hh, pv[:, 192:320].unsqueeze(1).to_broadcast([128, 16, 128]),
                                                    ALU.mult), r=[dbig, pv], w=[dbig])
            dv(lambda Th=Th, p0_=p0_, p1_=p1_: V.tensor_reduce(yh[:, p0_:p1_], Th, AX.X, ALU.add), r=[dbig], w=[yh])
        dma(o_sssm[l], dbig[:, 0:8192], r=[dbig], w=[o_sssm])
        dma(ysd_d[:, :], yh[:], r=[yh], w=[ysd_d])
        dma(yc_[:], ysd_d[:].rearrange("(n h) d -> n (h d)", h=8), r=[ysd_d], w=[yc_])
        dv(lambda: V.tensor_tensor(zsd[:].rearrange("p (h d) -> p h d", h=8), cv[:, 0:512].rearrange("p (h d) -> p h d", h=8),
                                   r8[0:NS, 16:24].unsqueeze(2).to_broadcast([NS, 8, 64]), ALU.mult), r=[cv, r8], w=[zsd])
        dv(lambda: V.tensor_tensor(yc_[:], yc_[:], zsd[:], ALU.add), r=[yc_, zsd], w=[yc_])
        ac(lambda: S.activation(zsd[:], Pd[:, 1536:2048], AF.Silu), r=[Pd], w=[zsd])
        dv(lambda: V.tensor_tensor(yc_[:], yc_[:], zsd[:], ALU.mult), r=[yc_, zsd], w=[yc_])
        ac(lambda: S.activation(zsd[:], yc_[:], AF.Square, accum_out=s8[:, 6:7]), r=[yc_], w=[zsd, s8])
        dv(lambda: V.tensor_scalar(s8[:, 6:7], s8[:, 6:7], 1.0 / 512, EPS, ALU.mult, ALU.add), r=[s8], w=[s8])
        ac(lambda: S.activation(s8[:, 6:7], s8[:, 6:7], AF.Sqrt), r=[s8], w=[s8])
        dv(lambda: V.reciprocal(s8[:, 7:8], s8[:, 6:7]), r=[s8], w=[s8])
        dv(lambda: V.scalar_tensor_tensor(ymd[:, 512:1024], yc_[:], s8[:, 7:8], sng[0:NS, :], ALU.mult, ALU.mult),
           r=[yc_, s8, sng], w=[ymd])
        dv(lambda: V.tensor_copy(hbd[:], ymd[:]), r=[ymd], w=[hbd])
        d_to_fm(hbd, hTd, 8)
        d_linear(hTd, 8, w_out_v, wmix, 0, D, None, 0)
        d_residual(xsd, modd, hfd, [pb[0], pb[1]])
        dma(xs_d[:, :], xsd[:], r=[xsd], w=[xs_d])

    def decode_ffn(l):
        W = Wd
        phase_reset()
        wffn = sb("wffn", [128, 8 * 5632 + 22 * 1024], BF16)
        w_up_v = wffn[:, 0:8 * 5632].rearrange("p (k n) -> p k n", k=8)
        w_dn_v = wffn[:, 8 * 5632:8 * 5632 + 22 * 1024].rearrange("p (k n) -> p k n", k=22)
        fw.dma(fw.pool, w_up_v, W['ffn_w_up'][l].rearrange("(k p) n -> p k n", p=128), reads=[W['ffn_w_up']], writes=[wffn])
        fw.dma(fw.pool, w_dn_v, W['ffn_w_down'][l].rearrange("(k p) n -> p k n", p=128), reads=[W['ffn_w_down']],
               writes=[wffn])
        xsd = sb("xsd", [NS, D]); hbd = sb("hbd", [NS, D], BF16); hTd = sb("hTd", [128, 8, NS], BF16)
        dma(xsd[:], xs_d[:, :], r=[xs_d], w=[xsd])
        modd, hfd = d_norm_mod(l, 3 * D, 'norm_ffn_g', xsd, hbd)
        d_to_fm(hbd, hTd, 8)
        gv = sb("gv", [NS, 2 * DFF])
        d_linear(hTd, 8, w_up_v, wffn, 0, 2 * DFF, gv, 0)
        fbuf = sb("fbuf", [NS, 2, 256]); fcw_ = sb("fcw_", [NS, 4, 256]); ft = sb("ft", [NS, 256])
        pbf = sb("pbf", [NS, DFF], BF16, alias=gv); pT_ = sb("pT_", [128, NCH_FF, NS], BF16)
        dma(o_sffn[l, :, 1, :], gv[:, 0:DFF], r=[gv], w=[o_sffn])
        dma(o_sffn[l, :, 0, :], st_ffn[l, :, 1, :], r=[st_ffn], w=[o_sffn])
        for c0 in range(0, DFF, 256):
            n = min(256, DFF - c0)
            dma(fbuf[:, :, 0:n], st_ffn[l, :, :, c0:c0 + n], r=[st_ffn], w=[fbuf])
            for j in range(3):
                drow(fcw_[:, j, 0:n], fcw_, W['ffn_conv_w'], W['ffn_conv_w'][l, j:j + 1, c0:c0 + n])
            drow(fcw_[:, 3, 0:n], fcw_, W['ffn_conv_b'], W['ffn_conv_b'][l:l + 1, c0:c0 + n])
            dv(lambda c0=c0, n=n: V.tensor_tensor(ft[:, 0:n], gv[:, c0:c0 + n], fcw_[:, 2, 0:n], ALU.mult), r=[gv, fcw_], w=[ft])
            dv(lambda n=n: V.tensor_tensor(ft[:, 0:n], ft[:, 0:n], fcw_[:, 3, 0:n], ALU.add), r=[ft, fcw_], w=[ft])
            for j in range(2):
                dv(lambda j=j, n=n: V.tensor_tensor(fbuf[:, j, 0:n], fbuf[:, j, 0:n], fcw_[:, j, 0:n], ALU.mult),
                   r=[fbuf, fcw_], w=[fbuf])
                dv(lambda j=j, n=n: V.tensor_tensor(ft[:, 0:n], ft[:, 0:n], fbuf[:, j, 0:n], ALU.add), r=[ft, fbuf], w=[ft])
            ac(lambda n=n: S.activation(ft[:, 0:n], ft[:, 0:n], AF.Silu), r=[ft], w=[ft])
            dv(lambda c0=c0, n=n: V.tensor_tensor(pbf[:, c0:c0 + n], ft[:, 0:n], gv[:, DFF + c0:DFF + c0 + n], ALU.mult),
               r=[ft, gv], w=[pbf])
        d_to_fm(pbf, pT_, NCH_FF)
        d_linear(pT_, NCH_FF, w_dn_v, wffn, 0, D, None, 0)
        d_residual(xsd, modd, hfd, [pb[0], pb[1]])
        dma(xs_d[:, :], xsd[:], r=[xsd], w=[xs_d])
        if l == L - 1:
            dma(y_s[:, :], xsd[:], r=[xsd], w=[y_s])

    try:
        for l in range(L):
            ada_layer(l)
            if stop == 'ada':
                break
            phase_reset()
            load_consts(l)
            xsrc = x_p if l == 0 else xb
            wmix = sb("wmix", [128, 8 * PROJ + 8 * D], BF16)
            w_in_v = wmix[:, 0:8 * PROJ].rearrange("p (k n) -> p k n", k=8)
            w_out_v = wmix[:, 8 * PROJ:8 * PROJ + 8 * D].rearrange("p (k n) -> p k n", k=8)
            fw.dma(fw.pool, w_in_v, Wd['w_in'][l].rearrange("(k p) n -> p k n", p=128), reads=[Wd['w_in']],
                   writes=[wmix])
            fw.dma(fw.pool, w_out_v, Wd['w_out'][l].rearrange("(k p) n -> p k n", p=128), reads=[Wd['w_out']],
                   writes=[wmix])
            xt = [sb("xt0", [128, D])]
            ss = sb("ss", [128, 8])
            hf = sb("hf", [128, D])
            hb = sb("hb", [128, D], BF16)
            hT = sb("hT", [128, 8, 128], BF16)
            junk = sb("junk", [128, D], BF16, alias=hT)
            paF = sb("paF", [128, 8, 129])
            xs = sb("xs", [128, 8, 128])
            tmpF = sb("tmpF", [128, 8, 128])
            xbcF = sb("xbcF", [128, 8, 131])
            cvF = sb("cvF", [128, 8, 128])
            bq = sb("bq", [128, 512])
            zs = sb("zs", [128, 512])
            dtr = sb("dtr", [128, 8])
            ymix = sb("ymix", [128, D])
            ymb = sb("ymb", [128, D], BF16, alias=hb)
            ymT = sb("ymT", [128, 8, 128], BF16, alias=hT)

            lwT = sb("lwT", [128, 128])
            sgl = sb("sgl", [128, 128])
            logd = sb("logd", [128, 2, 128]); aa = sb("aa", [128, 2, 128]); kk = sb("kk", [128, 2, 128])
            kmod = sb("kmod", [128, 2, 128]); cum = sb("cum", [128, 2, 128]); t2 = sb("t2", [128, 2, 128])
            ePt = sb("ePt", [128, 2, 128]); ePi = sb("ePi", [128, 2, 128]); ePm = sb("ePm", [128, 2, 128])
            ePe = sb("ePe", [128, 2, 128]); kka = sb("kka", [128, 2, 128])
            AR = sb("AR", [128, 2, 2, 128])
            BK = sb("BK", [128, 2, 2, 128])
            BhKh = sb("BhKh", [128, 2, 2, 128])
            rkr = sb("rkr", [128, 2, 128])
            tmq = sb("tmq", [128, 4, 4, 64])
            Am = sb("Am", [128, 4, 2, 256])
            Bm = [sb(f"Bm{i}", [128, 4, 128]) for i in range(2)]
            BmT = [sb(f"BmT{i}", [128, 4, 128]) for i in range(2)]
            XT = [sb(f"XT{i}", [128, 4, 128]) for i in range(2)]
            Xn = [sb(f"Xn{i}", [128, 4, 128]) for i in range(2)]
            Afl = sb("Afl", [128, 4, 128])
            AZ = sb("AZ", [128, 4, 128])
            WU = sb("WU", [128, 4, 256])
            MTN = sb("MTN", [64, 4, 128])
            QT = sb("QT", [64, 4, 128])
            ST = sb("ST", [64, 4, 64])
            dP = sb("dP", [64, 4])
            ywk = sb("ywk", [128, 256]); ysq = sb("ysq", [128, 256]); gst = sb("gst", [128, 16]); gateT = sb("gateT", [128, 256])
            rkb = sb("rkb", [128, 4])

            qn = sb("qn", [128, 384]); st8 = sb("st8", [128, 8])
            qT = sb("qT", [64, 4, 128])
            kT = [sb(f"kT{i}", [64, 2, 128]) for i in range(2)]
            vaug = [sb(f"vaug{i}", [128, 2, 65]) for i in range(2)]
            pexp = sb("pexp", [128, 2, 2, 256], alias=xs)

            xtok = sb("xtok", [128, 512], alias=Bm[0]); xdt = sb("xdt", [128, 512], alias=Bm[1]); dts = sb("dts", [128, 8]); das = sb("das", [128, 8])
            dabc = sb("dabc", [128, 8, 128], alias=tmpF); acum = sb("acum", [128, 8]); tot = sb("tot", [128, 16])
            seg = sb("seg", [128, 4, 128], alias=XT[0]); GT = sb("GT", [128, 8, 128], alias=Am); Cs = sb("Cs", [128, 8, 128], alias=WU)
            Btok = sb("Btok", [128, 256]); cbs = sb("cbs", [128, 256]); xdte = sb("xdte", [128, 512], alias=BmT[0]); hTs = sb("hTs", [128, 512])
            eac = sb("eac", [128, 8, 128], alias=tmq); yss = sb("yss", [128, 512], alias=BmT[1]); cdec = sb("cdec", [128, 16])

            modp = load_mod(l, 0, 'norm_mix_g')
            dv(lambda: V.memset(ST[:], 0.0), w=[ST])
            dv(lambda: V.memset(hTs[:], 0.0), w=[hTs])
            dv(lambda: V.memset(paF[:, :, 0:1], 0.0), w=[paF])
            dv(lambda: V.memset(xbcF[:, :, 0:3], 0.0), w=[xbcF])
            for i in range(2):
                dv(lambda i=i: V.memset(vaug[i][:], 1.0), w=[vaug[i]])

            for i in range(NT):
                cur['i'] = i
                xti = xt[0]
                dma(xti[:], xsrc[i * 128:(i + 1) * 128, :], r=[xsrc], w=[xti])
                rmsnorm_mod(xti, hb)
                to_fm(hb, hT)
                for (c0, dstT, off, pss) in ((0, paF, 1, (pb[0], pb[1])), (2048, xbcF, 3, (pb[2], pb[3]))):
                    for oc in range(8):
                        ps = pss[oc // 4]
                        for kc in range(8):
                            mm(lambda oc=oc, kc=kc, ps=ps, c0=c0: T.matmul(
                                ps[:, (oc % 4) * 128:(oc % 4 + 1) * 128],
                                w_in_v[:, kc, c0 + oc * 128:c0 + (oc + 1) * 128], hT[:, kc, :],
                                start=(kc == 0), stop=(kc == 7)), r=[wmix, hT], w=[ps], inc=(kc == 7 and oc % 4 == 3), dense=True)
                    for h2 in range(2):
                        ac(lambda h2=h2, dstT=dstT, off=off, pss=pss: S.copy(
                            dstT[:, h2 * 4:(h2 + 1) * 4, off:off + 128],
                            pss[h2][:].rearrange("p (k n) -> p k n", k=4)), r=[pss[h2]], w=[dstT])
                for (c0, n, ps) in ((1024, 512, pb[4]), (1536, 512, pb[5]), (3072, 8, pb[6])):
                    for kc in range(8):
                        mm(lambda kc=kc, c0=c0, n=n, ps=ps: T.matmul(ps[:, 0:n], hT[:, kc, :], w_in_v[:, kc, c0:c0 + n],
                                                                     start=(kc == 0), stop=(kc == 7)),
                           r=[wmix, hT], w=[ps], inc=(kc == 7), dense=True)
                dv(lambda: V.tensor_copy(bq[:], pb[4][:]), r=[pb[4]], w=[bq])
                ac(lambda: S.activation(zs[:], pb[5][:], AF.Silu), r=[pb[5]], w=[zs])
                dv(lambda: V.tensor_copy(dtr[:], pb[6][:, 0:8]), r=[pb[6]], w=[dtr])

                cp('projE')
                if stop == 'proj':
                    continue
                kTc, kTp = kT[i % 2], kT[(i + 1) % 2]
                vc, vp = vaug[i % 2], vaug[(i + 1) % 2]
                parts = ([(kTp, vp, 0, m_lower)] if i > 0 else []) + [(kTc, vc, 1, tri)]
                def swaA():
                    q6 = bq[:, 0:384].rearrange("p (h d) -> p h d", h=6)
                    dv(lambda: V.tensor_tensor(qn[:], bq[:, 0:384], bq[:, 0:384], ALU.mult), r=[bq], w=[qn])
                    dv(lambda: V.tensor_reduce(st8[:, 0:6], qn[:].rearrange("p (h d) -> p h d", h=6), AX.X, ALU.add), r=[qn],
                       w=[st8])
                    dv(lambda: V.tensor_scalar(st8[:, 0:6], st8[:, 0:6], 1.0 / 64, EPS, ALU.mult, ALU.add), r=[st8], w=[st8])
                    ac(lambda: S.activation(st8[:, 0:6], st8[:, 0:6], AF.Ln), r=[st8], w=[st8])
                    ac(lambda: S.activation(st8[:, 0:6], st8[:, 0:6], AF.Exp, scale=-0.5), r=[st8], w=[st8])
                    dv(lambda: V.tensor_tensor(qn[:].rearrange("p (h d) -> p h d", h=6), q6,
                                               st8[:, 0:6].unsqueeze(2).to_broadcast([128, 6, 64]), ALU.mult), r=[bq, st8], w=[qn])
                    dv(lambda: V.tensor_tensor(qn[:, 0:256].rearrange("p (h d) -> p h d", h=4),
                                               qn[:, 0:256].rearrange("p (h d) -> p h d", h=4),
                                               qkg[:, 0:64].unsqueeze(1).to_broadcast([128, 4, 64]), ALU.mult), r=[qn, qkg], w=[qn])
                    dv(lambda: V.tensor_tensor(qn[:, 256:384].rearrange("p (h d) -> p h d", h=2),
                                               qn[:, 256:384].rearrange("p (h d) -> p h d", h=2),
                                               qkg[:, 64:128].unsqueeze(1).to_broadcast([128, 2, 64]), ALU.mult), r=[qn, qkg],
                       w=[qn])
                    ac(lambda vc=vc: S.copy(vc[:, :, 0:64], bq[:, 384:512].rearrange("p (h d) -> p h d", h=2)), r=[bq], w=[vc])
                    if i == NT - 1:
                        dma(o_pk[l], qn[:, 256:384], r=[qn], w=[o_pk])
                        dma(o_pv[l], bq[:, 384:512], r=[bq], w=[o_pv])
                    for h in range(6):
                        ps = pb[0] if h < 4 else pb[1]
                        mm(lambda h=h, ps=ps: T.transpose(ps[0:64, (h % 4) * 128:(h % 4 + 1) * 128], qn[:, h * 64:(h + 1) * 64],
                                                          ident[:]), r=[qn, ident], w=[ps], inc=(h == 3 or h == 5))
                    dv(lambda: V.tensor_copy(qT[:].rearrange("p h n -> p (h n)"), pb[0][0:64, :]), r=[pb[0]], w=[qT])
                    ac(lambda kTc=kTc: S.copy(kTc[:].rearrange("p h n -> p (h n)"), pb[1][0:64, 0:256]), r=[pb[1]], w=[kTc])
                def swaB():
                    for kh in range(2):
                        for (kt_, vt_, w_, msk) in parts:
                            ps = pb[0] if w_ == 0 else pb[1]
                            mm(lambda kh=kh, kt_=kt_, ps=ps: T.matmul(ps[:, kh * 256:(kh + 1) * 256], kt_[:, kh, :],
                                                                      qT[:, 2 * kh:2 * kh + 2, :].rearrange("p h n -> p (h n)"),
                                                                      start=True, stop=True), r=[kt_, qT], w=[ps])
                            ac(lambda kh=kh, w_=w_, ps=ps: S.activation(pexp[:, kh, w_, :], ps[:, kh * 256:(kh + 1) * 256], AF.Exp,
                                                                        scale=0.125), r=[ps], w=[pexp])
                            dv(lambda kh=kh, w_=w_, msk=msk: V.tensor_tensor(
                                pexp[:, kh, w_, :].rearrange("p (h n) -> p h n", h=2),
                                pexp[:, kh, w_, :].rearrange("p (h n) -> p h n", h=2),
                                msk[:].unsqueeze(1).to_broadcast([128, 2, 128]), ALU.mult), r=[pexp, msk], w=[pexp])
                def swaC():
                    for h in range(4):
                        kh = h // 2
                        for pi, (kt_, vt_, w_, msk) in enumerate(parts):
                            mm(lambda h=h, kh=kh, vt_=vt_, w_=w_, pi=pi: T.matmul(
                                pb[4][:, h * 128:h * 128 + 65], pexp[:, kh, w_, (h % 2) * 128:(h % 2 + 1) * 128], vt_[:, kh, :],
                                start=(pi == 0), stop=(pi == len(parts) - 1)), r=[pexp, vt_], w=[pb[4]],
                               inc=(h == 3 and pi == len(parts) - 1))
                    o4 = pb[4][:].rearrange("p (h d) -> p h d", h=4)
                    dv(lambda: V.tensor_tensor(st8[:, 0:4], o4[:, :, 64], esink[:], ALU.add), r=[pb[4], esink], w=[st8])
                    dv(lambda: V.reciprocal(st8[:, 0:4], st8[:, 0:4]), r=[st8], w=[st8])
                    ac(lambda: S.copy(qn[:, 0:256].rearrange("p (h d) -> p h d", h=4), o4[:, :, 0:64]), r=[pb[4]], w=[qn])
                    dv(lambda: V.tensor_tensor(ymix[:, 256:512].rearrange("p (h d) -> p h d", h=4),
                                               qn[:, 0:256].rearrange("p (h d) -> p h d", h=4),
                                               st8[:, 0:4].unsqueeze(2).to_broadcast([128, 4, 64]), ALU.mult), r=[qn, st8], w=[ymix])

                if i == NT - 1:
                    for j in range(3):
                        mm(lambda j=j: T.transpose(pb[0][0:8, j * 128:(j + 1) * 128], xbcF[:, :, 128 + j], ident[:]),
                           r=[xbcF, ident], w=[pb[0]], inc=(j == 2))
                    dv(lambda: V.tensor_copy(ostg[0:8, 0:384], pb[0][0:8, 0:384]), r=[pb[0]], w=[ostg])
                    dma(o_pconv[l].rearrange("j (k p) -> k j p", p=128),
                        ostg[0:8, 0:384].rearrange("k (j p) -> k j p", j=3), r=[ostg], w=[o_pconv])
                for kc in range(8):
                    eng = dv
                    E = V
                    eng(lambda kc=kc, E=E: E.tensor_scalar(cvF[:, kc, :], xbcF[:, kc, 0:128], cconvw[:, 0, kc:kc + 1],
                                                           cconvb[:, kc:kc + 1], ALU.mult, ALU.add),
                        r=[xbcF, cconvw, cconvb], w=[cvF])
                    for j in range(1, 4):
                        eng(lambda kc=kc, j=j, E=E: E.scalar_tensor_tensor(cvF[:, kc, :], xbcF[:, kc, j:j + 128],
                                                                           cconvw[:, j, kc:kc + 1], cvF[:, kc, :], ALU.mult,
                                                                           ALU.add), r=[xbcF, cconvw, cvF], w=[cvF])
                ac(lambda: S.copy(xbcF[:, :, 0:3], xbcF[:, :, 128:131]), r=[xbcF], w=[xbcF])
                ac(lambda: S.activation(cvF[:], cvF[:], AF.Silu), r=[cvF], w=[cvF])
                dv(lambda: V.tensor_tensor(dts[:], dtr[:], r8[:, 0:8], ALU.add), r=[dtr, r8], w=[dts])
                ac(lambda: S.activation(dts[:], dts[:], AF.Exp), r=[dts], w=[dts])
                ac(lambda: S.activation(dts[:], dts[:], AF.Ln, bias=1.0), r=[dts], w=[dts])
                dv(lambda: V.tensor_tensor(das[:], dts[:], r8[:, 8:16], ALU.mult), r=[dts, r8], w=[das])
                dv(lambda: V.tensor_tensor(tmpF[:], paF[:, :, 0:128], cmu[:].unsqueeze(2).to_broadcast([128, 8, 128]),
                                           ALU.mult), r=[paF, cmu], w=[tmpF])
                dv(lambda: V.tensor_tensor(xs[:], paF[:, :, 1:129], cmu1[:].unsqueeze(2).to_broadcast([128, 8, 128]),
                                           ALU.mult), r=[paF, cmu1], w=[xs])
                dv(lambda: V.tensor_tensor(xs[:], xs[:], tmpF[:], ALU.add), r=[xs, tmpF], w=[xs])
                cp('q0')
                if i == NT - 1:
                    mm(lambda: T.transpose(pb[0][0:8, 0:128], paF[:, :, 128], ident[:]), r=[paF, ident], w=[pb[0]])
                    dv(lambda: V.tensor_copy(ostg[0:8, 0:128], pb[0][0:8, 0:128]), r=[pb[0]], w=[ostg])
                    dma(o_pshift[l].rearrange("(k p) -> k p", p=128), ostg[0:8, 0:128], r=[ostg], w=[o_pshift])
                ac(lambda: S.copy(paF[:, :, 0:1], paF[:, :, 128:129]), r=[paF], w=[paF])
                cp('q1')
                ac(lambda: S.activation(lwT[0:64, :], xs[0:64, 6, :], AF.Tanh), r=[xs], w=[lwT])
                ac(lambda: S.activation(sgl[:], xs[:, 7, :], AF.Sigmoid), r=[xs], w=[sgl])
                cp('q2')
                for c in range(2):
                    mm(lambda c=c: T.matmul(pb[0][:, c * 128:(c + 1) * 128], w2t[0:64, c * 128:(c + 1) * 128], lwT[0:64, :],
                                            start=True, stop=True), r=[w2t, lwT], w=[pb[0]], inc=False)
                    mm(lambda c=c: T.matmul(pb[0][:, 256 + c * 128:256 + (c + 1) * 128],
                                            a2t[64:128, c * 128:(c + 1) * 128], xs[64:128, 6, :], start=True, stop=True),
                       r=[a2t, xs], w=[pb[0]], inc=(c == 1))
                mm(lambda: T.matmul(pb[1][:, 0:256], sgl[:], g2t[:], start=True, stop=True), r=[sgl, g2t], w=[pb[1]])
                cp('q3')
                for c in range(2):
                    ac(lambda c=c: S.activation(logd[:, c, :], pb[0][:, c * 128:(c + 1) * 128], AF.Sigmoid,
                                                bias=cw0[:, c:c + 1]), r=[pb[0], cw0], w=[logd])
                    ac(lambda c=c: S.activation(aa[:, c, :], pb[0][:, 256 + c * 128:256 + (c + 1) * 128], AF.Sigmoid,
                                                bias=ca0[:, c:c + 1]), r=[pb[0], ca0], w=[aa])
                cp('q4')
                ac(lambda: S.copy(gateT[:], pb[1][:, 0:256]), r=[pb[1]], w=[gateT])
                dv(lambda: V.tensor_scalar(logd[:], logd[:], -math.exp(-0.5), None, ALU.mult), r=[logd], w=[logd])
                cp('r1')
                dv(lambda: V.tensor_tensor(kk[:], xs[:, 2:4, :], ckk[:].unsqueeze(2).to_broadcast([128, 2, 128]), ALU.mult),
                   r=[xs, ckk], w=[kk])
                dv(lambda: V.tensor_tensor(t2[:], kk[:], kk[:], ALU.mult), r=[kk], w=[t2])
                mm(lambda: T.matmul(pb[1][:, 256:512], blk[:], t2[:].rearrange("p c n -> p (c n)"), start=True, stop=True),
                   r=[blk, t2], w=[pb[1]])
                dv(lambda: V.tensor_scalar(t2[:].rearrange("p c n -> p (c n)"), pb[1][:, 256:512], 1e-24, None, ALU.max),
                   r=[pb[1]], w=[t2])
                ac(lambda: S.activation(t2[:], t2[:], AF.Ln), r=[t2], w=[t2])
                ac(lambda: S.activation(t2[:], t2[:], AF.Exp, scale=-0.5), r=[t2], w=[t2])
                dv(lambda: V.tensor_tensor(kk[:], kk[:], t2[:], ALU.mult), r=[kk, t2], w=[kk])
                cp('r2')
                dv(lambda: V.tensor_tensor(kmod[:], aa[:], cka[:].unsqueeze(2).to_broadcast([128, 2, 128]), ALU.mult),
                   r=[aa, cka], w=[kmod])
                dv(lambda: V.tensor_tensor(kmod[:], kmod[:], cka1[:].unsqueeze(2).to_broadcast([128, 2, 128]), ALU.add),
                   r=[kmod, cka1], w=[kmod])
                dv(lambda: V.tensor_tensor(kmod[:], kmod[:], xs[:, 2:4, :], ALU.mult), r=[kmod, xs], w=[kmod])
                cp('r3')
                for c in range(2):
                    dv(lambda c=c: V.tensor_tensor_scan(cum[:, c, :], ones[:, 0:128], logd[:, c, :], 0.0, ALU.mult, ALU.add),
                       r=[ones, logd], w=[cum])
                dv(lambda: V.tensor_tensor(t2[:], cum[:], logd[:], ALU.subtract), r=[cum, logd], w=[t2])
                ac(lambda: S.activation(ePt[:], cum[:], AF.Exp), r=[cum], w=[ePt])
                ac(lambda: S.activation(ePi[:], cum[:], AF.Exp, scale=-1.0), r=[cum], w=[ePi])
                ac(lambda: S.activation(ePm[:], t2[:], AF.Exp), r=[t2], w=[ePm])
                for c in range(2):
                    ac(lambda c=c: S.activation(ePe[:, c, :], cum[:, c, :], AF.Exp, scale=-1.0, bias=cum[:, c, 127:128]),
                       r=[cum], w=[ePe])
                dv(lambda: V.scalar_tensor_tensor(AR[:, :, 0, :], kk[:], -1.0, ePm[:], ALU.mult, ALU.mult),
                   r=[kk, ePm], w=[AR])
                dv(lambda: V.tensor_tensor(AR[:, :, 1, :], xs[:, 0:2, :], ePt[:], ALU.mult), r=[xs, ePt], w=[AR])
                dv(lambda: V.tensor_tensor(kka[:], kk[:], aa[:], ALU.mult), r=[kk, aa], w=[kka])
                dv(lambda: V.tensor_tensor(BK[:, :, 0, :], kka[:], ePi[:], ALU.mult), r=[kka, ePi], w=[BK])
                dv(lambda: V.tensor_tensor(BK[:, :, 1, :], kmod[:], ePi[:], ALU.mult), r=[kmod, ePi], w=[BK])
                dv(lambda: V.tensor_tensor(BhKh[:, :, 0, :], kka[:], ePe[:], ALU.mult), r=[kka, ePe], w=[BhKh])
                dv(lambda: V.tensor_tensor(BhKh[:, :, 1, :], kmod[:], ePe[:], ALU.mult), r=[kmod, ePe], w=[BhKh])
                dv(lambda: V.tensor_tensor(rkr[:], xs[:, 0:2, :], kmod[:], ALU.mult), r=[xs, kmod], w=[rkr])
                dv(lambda: V.tensor_tensor(rkr[:], rkr[:], crk[:].unsqueeze(2).to_broadcast([128, 2, 128]), ALU.mult),
                   r=[rkr, crk], w=[rkr])
                cp('r4')
                for h in range(4):
                    mm(lambda h=h: T.matmul(pb[2][0:64, 2 * h:2 * h + 2], ident[:, (h % 2) * 64:(h % 2) * 64 + 64],
                                            ePt[:, h // 2, 126:128], start=True, stop=True), r=[ident, ePt], w=[pb[2]],
                       inc=(h == 3))
                dv(lambda: V.tensor_copy(dP[:], pb[2][0:64, 0:8].rearrange("p (h t) -> p h t", t=2)[:, :, 1]), r=[pb[2]],
                   w=[dP])
                cp('r5')
                srcs = [lambda c: AR[:, c, 0, :], lambda c: xs[:, 4 + c, :], lambda c: BhKh[:, c, 0, :],
                        lambda c: BhKh[:, c, 1, :]]
                srct = [AR, xs, BhKh, BhKh]
                for qi in range(4):
                    ps = pb[3] if qi < 2 else pb[4]
                    for c in range(2):
                        mm(lambda qi=qi, c=c, ps=ps: T.transpose(ps[:, ((qi % 2) * 2 + c) * 128:((qi % 2) * 2 + c + 1) * 128],
                                                                 srcs[qi](c), ident[:]), r=[srct[qi], ident], w=[ps],
                           inc=(c == 1 and qi % 2 == 1))
                for qi in range(4):
                    ps = pb[3] if qi < 2 else pb[4]
                    (dv if qi % 2 == 0 else ac)(
                        (lambda qi=qi, ps=ps: V.tensor_copy(tmq[:, :, qi, :], ps[:, (qi % 2) * 256:(qi % 2) * 256 + 256]
                                                            .rearrange("p (h d) -> p h d", h=4))) if qi % 2 == 0 else
                        (lambda qi=qi, ps=ps: S.copy(tmq[:, :, qi, :], ps[:, (qi % 2) * 256:(qi % 2) * 256 + 256]
                                                     .rearrange("p (h d) -> p h d", h=4))), r=[ps], w=[tmq])
                cp('r6')
                for c in range(2):
                    mm(lambda c=c: T.matmul(pb[2][:, 8 + 2 * c:10 + 2 * c], rkr[:, c, :], hsel[:], start=True, stop=True),
                       r=[rkr, hsel], w=[pb[2]], inc=(c == 1))
                dv(lambda: V.tensor_copy(rkb[:], pb[2][:, 8:12]), r=[pb[2]], w=[rkb])
                cp('r7')
                for hp in range(2):
                    for (ps, which) in ((pb[5], 0), (pb[6], 1)):
                        for hh in range(2):
                            h = hp * 2 + hh
                            p0 = (h % 2) * 64
                            mm(lambda h=h, p0=p0, ps=ps, which=which, hh=hh: T.matmul(
                                ps[:, hh * 256:(hh + 1) * 256], BK[p0:p0 + 64, h // 2, which, :],
                                AR[p0:p0 + 64, h // 2, :, :].rearrange("p a n -> p (a n)"), start=True, stop=True),
                               r=[BK, AR], w=[ps], inc=(hh == 1))
                        dv(lambda hp=hp, ps=ps, which=which: V.tensor_tensor(
                            Am[:, hp * 2:hp * 2 + 2, which, :], ps[:].rearrange("p (h n) -> p h n", h=2),
                            m_us[:].unsqueeze(1).to_broadcast([128, 2, 256]), ALU.mult), r=[ps, m_us], w=[Am])
                cp('r8')
                for h in range(4):
                    p0 = (h % 2) * 64
                    mm(lambda h=h, p0=p0: T.matmul(pb[5][:, h * 128:(h + 1) * 128], AR[p0:p0 + 64, h // 2, 0, :],
                                                   BK[p0:p0 + 64, h // 2, 0, :], start=True, stop=True), r=[AR, BK],
                       w=[pb[5]])
                dv(lambda: V.tensor_tensor(Afl[:], pb[5][:].rearrange("p (h n) -> p h n", h=4),
                                           m_ls[:].unsqueeze(1).to_broadcast([128, 4, 128]), ALU.mult),
                   r=[pb[5], m_ls], w=[Afl])
                cp('r9')
                def bm4(m_):
                    return m_[:].unsqueeze(1).to_broadcast([128, 4, 128])
                dv(lambda: V.tensor_tensor(Bm[0][:], Afl[:], bm4(bd32), ALU.mult), r=[Afl, bd32], w=[Bm[0]])
                pl(lambda: G.tensor_tensor(BmT[0][:], Am[:, :, 0, 0:128], bm4(bd32), ALU.mult), r=[Am, bd32], w=[BmT[0]])
                dv(lambda: V.tensor_tensor(Xn[0][:], Bm[0][:], bm4(ident), ALU.add), r=[Bm[0], ident], w=[Xn[0]])
                pl(lambda: G.tensor_tensor(XT[0][:], BmT[0][:], bm4(ident), ALU.add), r=[BmT[0], ident], w=[XT[0]])

                def mm4(ps, lt, rt):
                    for h in range(4):
                        mm(lambda h=h: T.matmul(ps[:, h * 128:(h + 1) * 128], lt[:, h, :], rt[:, h, :], start=True,
                                                stop=True), r=[lt, rt], w=[ps])

                def flat(t_):
                    return t_[:].rearrange("p h n -> p (h n)")
                cb_, cx = 0, 0
                for it in range(4):
                    if it == 0:
                        swaA()
                    if it == 2:
                        swaB()
                    nb = 1 - cb_
                    mm4(pb[5], BmT[cb_], Bm[cb_])
                    mm4(pb[6], Bm[cb_], BmT[cb_])
                    dv(lambda nb=nb: V.tensor_copy(flat(Bm[nb]), pb[5][:]), r=[pb[5]], w=[Bm[nb]])
                    ac(lambda nb=nb: S.copy(flat(BmT[nb]), pb[6][:]), r=[pb[6]], w=[BmT[nb]])
                    mm4(pb[2], BmT[nb], Xn[cx])
                    mm4(pb[3], Bm[nb], XT[cx])
                    dv(lambda cx=cx: V.tensor_tensor(flat(Xn[1 - cx]), pb[2][:], flat(Xn[cx]), ALU.add),
                       r=[pb[2], Xn[cx]], w=[Xn[1 - cx]])
                    dv(lambda cx=cx: V.tensor_tensor(flat(XT[1 - cx]), pb[3][:], flat(XT[cx]), ALU.add),
                       r=[pb[3], XT[cx]], w=[XT[1 - cx]])
                    cb_, cx = nb, 1 - cx
                dv(lambda: V.tensor_tensor(Bm[0][:], Afl[:], bm4(off1), ALU.mult), r=[Afl, off1], w=[Bm[0]])
                pl(lambda: G.tensor_tensor(BmT[0][:], Am[:, :, 0, 0:128], bm4(off1), ALU.mult), r=[Am, off1], w=[BmT[0]])
                mm4(pb[5], BmT[0], Xn[cx])
                mm4(pb[6], Bm[0], XT[cx])
                dv(lambda: V.tensor_copy(flat(Bm[1]), pb[5][:]), r=[pb[5]], w=[Bm[1]])
                ac(lambda: S.copy(flat(BmT[1]), pb[6][:]), r=[pb[6]], w=[BmT[1]])
                mm4(pb[2], XT[cx], Bm[1])
                mm4(pb[3], Xn[cx], BmT[1])
                dv(lambda cx=cx: V.tensor_tensor(flat(Xn[1 - cx]), pb[2][:], flat(Xn[cx]), ALU.add),
                   r=[pb[2], Xn[cx]], w=[Xn[1 - cx]])
                dv(lambda cx=cx: V.tensor_tensor(flat(XT[1 - cx]), pb[3][:], flat(XT[cx]), ALU.add),
                   r=[pb[3], XT[cx]], w=[XT[1 - cx]])
                cx = 1 - cx
                swaC()
                dv(lambda: V.tensor_tensor(Bm[0][:], Afl[:], bm4(off2), ALU.mult), r=[Afl, off2], w=[Bm[0]])
                mm4(pb[6], Bm[0], XT[cx])
                ac(lambda: S.copy(flat(BmT[1]), pb[6][:]), r=[pb[6]], w=[BmT[1]])
                mm4(pb[3], Xn[cx], BmT[1])
                dv(lambda cx=cx: V.tensor_tensor(flat(XT[1 - cx]), pb[3][:], flat(XT[cx]), ALU.add),
                   r=[pb[3], XT[cx]], w=[XT[1 - cx]])
                XTf = XT[1 - cx]
                cp('r10')
                for h in range(4):
                    mm(lambda h=h: T.matmul(pb[3][:, h * 64:(h + 1) * 64], Am[:, h, 1, 0:128], tmq[:, h, 1, :], start=True,
                                            stop=True), r=[Am, tmq], w=[pb[3]], inc=(h == 3))
                dv(lambda: V.tensor_copy(AZ[:].rearrange("p h (a d) -> p h a d", a=2)[:, :, 1, :],
                                         pb[3][:, 0:256].rearrange("p (h d) -> p h d", h=4)), r=[pb[3]], w=[AZ])
                ac(lambda: S.copy(AZ[:].rearrange("p h (a d) -> p h a d", a=2)[:, :, 0, :], tmq[:, :, 0, :]), r=[tmq], w=[AZ])
                cp('r11')
                for h in range(4):
                    mm(lambda h=h: T.matmul(pb[4][:, h * 128:(h + 1) * 128], XTf[:, h, :], AZ[:, h, :], start=True, stop=True),
                       r=[XTf, AZ], w=[pb[4]], inc=(h == 3))
                dv(lambda: V.tensor_copy(WU[:, :, 0:128], pb[4][:].rearrange("p (h n) -> p h n", h=4)), r=[pb[4]], w=[WU])
                cp('r12')
                for h in range(4):
                    mm(lambda h=h: T.matmul(pb[5][0:64, h * 128:h * 128 + 64], WU[:, h, 0:64], tmq[:, h, 2, :], start=True,
                                            stop=True), r=[WU, tmq], w=[pb[5]], inc=False)
                    mm(lambda h=h: T.matmul(pb[5][0:64, h * 128 + 64:h * 128 + 128], tmq[:, h, 2, :], WU[:, h, 64:128],
                                            start=True, stop=False), r=[WU, tmq], w=[pb[5]], inc=False)
                    mm(lambda h=h: T.matmul(pb[5][0:64, h * 128 + 64:h * 128 + 128], tmq[:, h, 3, :], tmq[:, h, 1, :],
                                            start=False, stop=True), r=[tmq], w=[pb[5]], inc=(h == 3))
                for h in range(4):
                    dv(lambda h=h: V.scalar_tensor_tensor(MTN[:, h, 0:64], ident[0:64, 0:64], dP[:, h:h + 1],
                                                          pb[5][0:64, h * 128:h * 128 + 64], ALU.mult, ALU.add),
                       r=[ident, dP, pb[5]], w=[MTN])
                ac(lambda: S.copy(MTN[:].rearrange("p h (a d) -> p h a d", a=2)[:, :, 1, :],
                                  pb[5][0:64, :].rearrange("p (h a d) -> p h a d", h=4, a=2)[:, :, 1, :]), r=[pb[5]], w=[MTN])
                cp('r13')
                for h in range(4):
                    mm(lambda h=h: T.matmul(pb[6][0:64, h * 128:(h + 1) * 128], ident[:, (h % 2) * 64:(h % 2) * 64 + 64],
                                            AR[:, h // 2, 1, :], start=True, stop=False), r=[ident, AR], w=[pb[6]], inc=False)
                    mm(lambda h=h: T.matmul(pb[6][0:64, h * 128:(h + 1) * 128], WU[:, h, 0:64], Am[:, h, 0, 128:256],
                                            start=False, stop=True), r=[WU, Am], w=[pb[6]], inc=(h == 3))
                dv(lambda: V.tensor_copy(QT[:].rearrange("p h n -> p (h n)"), pb[6][0:64, :]), r=[pb[6]], w=[QT])
                cp('r14')
                for h in range(4):
                    o = pb[3][:, 256 + h * 64:256 + (h + 1) * 64]
                    mm(lambda h=h, o=o: T.matmul(o, Am[:, h, 0, 128:256], WU[:, h, 64:128], start=True, stop=False),
                       r=[Am, WU], w=[pb[3]], inc=False)
                    mm(lambda h=h, o=o: T.matmul(o, Am[:, h, 1, 128:256], tmq[:, h, 1, :], start=False, stop=False),
                       r=[Am, tmq], w=[pb[3]], inc=False)
                    mm(lambda h=h, o=o: T.matmul(o, QT[:, h, :], ST[:, h, :], start=False, stop=True), r=[QT, ST],
                       w=[pb[3]], inc=(h == 3))
                dv(lambda: V.tensor_copy(ywk[:], pb[3][:, 256:512]), r=[pb[3]], w=[ywk])
                cp('r15')
                for h in range(4):
                    mm(lambda h=h: T.matmul(pb[4][0:64, h * 64:(h + 1) * 64], MTN[:, h, 0:64], ST[:, h, :], start=True,
                                            stop=True), r=[MTN, ST], w=[pb[4]], inc=(h == 3))
                dv(lambda: V.tensor_tensor(ST[:], pb[4][0:64, 0:256].rearrange("p (h d) -> p h d", h=4),
                                           MTN[:].rearrange("p h (a d) -> p h a d", a=2)[:, :, 1, :], ALU.add),
                   r=[pb[4], MTN], w=[ST])
                cp('r16')
                y3 = ywk[:].rearrange("p (h d) -> p h d", h=4)
                dv(lambda: V.tensor_reduce(gst[:, 0:4], y3, AX.X, ALU.add), r=[ywk], w=[gst])
                dv(lambda: V.tensor_tensor(ysq[:], ywk[:], ywk[:], ALU.mult), r=[ywk], w=[ysq])
                dv(lambda: V.tensor_reduce(gst[:, 4:8], ysq[:].rearrange("p (h d) -> p h d", h=4), AX.X, ALU.add),
                   r=[ysq], w=[gst])
                dv(lambda: V.tensor_scalar(gst[:, 0:8], gst[:, 0:8], 1.0 / 64, None, ALU.mult), r=[gst], w=[gst])
                dv(lambda: V.tensor_tensor(gst[:, 8:12], gst[:, 0:4], gst[:, 0:4], ALU.mult), r=[gst], w=[gst])
                dv(lambda: V.tensor_tensor(gst[:, 8:12], gst[:, 4:8], gst[:, 8:12], ALU.subtract), r=[gst], w=[gst])
                dv(lambda: V.tensor_scalar(gst[:, 8:12], gst[:, 8:12], GN_EPS, None, ALU.add), r=[gst], w=[gst])
                ac(lambda: S.activation(gst[:, 8:12], gst[:, 8:12], AF.Ln), r=[gst], w=[gst])
                ac(lambda: S.activation(gst[:, 12:16], gst[:, 8:12], AF.Exp, scale=-0.5), r=[gst], w=[gst])
                dv(lambda: V.tensor_tensor(y3, y3, gst[:, 0:4].unsqueeze(2).to_broadcast([128, 4, 64]), ALU.subtract),
                   r=[ywk, gst], w=[ywk])
                dv(lambda: V.tensor_tensor(y3, y3, gst[:, 12:16].unsqueeze(2).to_broadcast([128, 4, 64]), ALU.mult),
                   r=[ywk, gst], w=[ywk])
                dv(lambda: V.tensor_tensor(ywk[:], ywk[:], lngb[:, 0:256], ALU.mult), r=[ywk, lngb], w=[ywk])
                dv(lambda: V.tensor_tensor(ywk[:], ywk[:], lngb[:, 256:512], ALU.add), r=[ywk, lngb], w=[ywk])
                dv(lambda: V.tensor_tensor(ysq[:].rearrange("p (h d) -> p h d", h=4), tmq[:, :, 1, :],
                                           rkb[:].unsqueeze(2).to_broadcast([128, 4, 64]), ALU.mult), r=[tmq, rkb], w=[ysq])
                dv(lambda: V.tensor_tensor(ywk[:], ywk[:], ysq[:], ALU.add), r=[ywk, ysq], w=[ywk])
                dv(lambda: V.tensor_tensor(ymix[:, 0:256], ywk[:], gateT[:], ALU.mult), r=[ywk, gateT], w=[ymix])

                cp('rwkvE')
                if stop == 'rwkv':
                    continue
                pass
                cp('swaE')
                if stop == 'swa':
                    continue
                for c in range(4):
                    mm(lambda c=c: T.transpose(pb[0][:, c * 128:(c + 1) * 128], cvF[:, c, :], ident[:]), r=[cvF, ident],
                       w=[pb[0]], inc=(c == 3))
                cp('d3a')
                ac(lambda: S.copy(xtok[:], pb[0][:]), r=[pb[0]], w=[xtok])
                cp('d3b')
                dv(lambda: V.tensor_tensor(xdt[:].rearrange("p (h d) -> p h d", h=8), xtok[:].rearrange("p (h d) -> p h d", h=8),
                                           dts[:].unsqueeze(2).to_broadcast([128, 8, 64]), ALU.mult), r=[xtok, dts], w=[xdt])
                cp('d4')
                for c in range(2):
                    mm(lambda c=c: T.transpose(pb[1][:, c * 128:(c + 1) * 128], cvF[:, 4 + c, :], ident[:]), r=[cvF, ident],
                       w=[pb[1]], inc=(c == 1))
                ac(lambda: S.copy(Btok[:], pb[1][:, 0:256]), r=[pb[1]], w=[Btok])
                cp('d5')
                mm(lambda: T.matmul(pb[1][:, 256:264], tri[:], das[:], start=True, stop=True), r=[tri, das], w=[pb[1]], inc=False)
                mm(lambda: T.matmul(pb[1][:, 264:272], ones[:], das[:], start=True, stop=True), r=[ones, das], w=[pb[1]])
                dv(lambda: V.tensor_copy(acum[:], pb[1][:, 256:264]), r=[pb[1]], w=[acum])
                dv(lambda: V.tensor_copy(tot[:, 0:8], pb[1][:, 264:272]), r=[pb[1]], w=[tot])
                cp('d6')
                dv(lambda: V.tensor_copy(dabc[:], das[:].unsqueeze(2).to_broadcast([128, 8, 128])), r=[das], w=[dabc])
                for e in range(8):
                    ps = pb[2] if e < 4 else pb[3]
                    mm(lambda e=e, ps=ps: T.matmul(ps[:, (e % 4) * 128:(e % 4 + 1) * 128], dabc[:, e, :], tri[:], start=True,
                                                   stop=True), r=[dabc, tri], w=[ps], inc=(e % 4 == 3))
                cp('d7')
                for g in range(2):
                    mm(lambda g=g: T.matmul(pb[4][:, g * 128:(g + 1) * 128], cvF[:, 4 + g, :], cvF[:, 6 + g, :], start=True,
                                            stop=True), r=[cvF], w=[pb[4]], inc=(g == 1))
                ac(lambda: S.copy(cbs[:], pb[4][:, 0:256]), r=[pb[4]], w=[cbs])
                for g in range(2):
                    ps = pb[2] if g == 0 else pb[3]
                    ac(lambda ps=ps: S.copy(seg[:].rearrange("p e n -> p (e n)"), ps[:]), r=[ps], w=[seg])
                    dv(lambda g=g: V.tensor_tensor(seg[:], seg[:],
                                                   acum[:, g * 4:(g + 1) * 4].unsqueeze(2).to_broadcast([128, 4, 128]),
                                                   ALU.subtract), r=[seg, acum], w=[seg])
                    ac(lambda g=g, ps=ps: S.activation(eac[:, g * 4:(g + 1) * 4, :], ps[:].rearrange("p (e n) -> p e n", e=4),
                                                       AF.Exp), r=[ps], w=[eac])
                    dv(lambda: V.tensor_tensor(seg[:], seg[:], negm[:].unsqueeze(1).to_broadcast([128, 4, 128]), ALU.add),
                       r=[seg, negm], w=[seg])
                    ac(lambda: S.activation(seg[:], seg[:], AF.Exp), r=[seg], w=[seg])
                    dv(lambda g=g: V.tensor_tensor(GT[:, g * 4:(g + 1) * 4, :], seg[:],
                                                   cbs[:, g * 128:(g + 1) * 128].unsqueeze(1).to_broadcast([128, 4, 128]),
                                                   ALU.mult), r=[seg, cbs], w=[GT])
                    dv(lambda g=g: V.tensor_tensor(Cs[:, g * 4:(g + 1) * 4, :], eac[:, g * 4:(g + 1) * 4, :],
                                                   cvF[:, 6 + g, :].unsqueeze(1).to_broadcast([128, 4, 128]), ALU.mult),
                       r=[eac, cvF], w=[Cs])
                cp('d9')
                for e in range(8):
                    mm(lambda e=e: T.matmul(pb[5][:, e * 64:(e + 1) * 64], GT[:, e, :], xdt[:, e * 64:(e + 1) * 64], start=True,
                                            stop=False), r=[GT, xdt], w=[pb[5]], inc=False)
                    mm(lambda e=e: T.matmul(pb[5][:, e * 64:(e + 1) * 64], Cs[:, e, :], hTs[:, e * 64:(e + 1) * 64], start=False,
                                            stop=True), r=[Cs, hTs], w=[pb[5]], inc=(e == 7))
                cp('d10')
                dv(lambda: V.tensor_tensor(tot[:, 8:16], tot[:, 0:8], acum[:], ALU.subtract), r=[tot, acum], w=[tot])
                ac(lambda: S.activation(cdec[:], tot[:], AF.Exp), r=[tot], w=[cdec])
                dv(lambda: V.tensor_tensor(xdte[:].rearrange("p (h d) -> p h d", h=8), xdt[:].rearrange("p (h d) -> p h d", h=8),
                                           cdec[:, 8:16].unsqueeze(2).to_broadcast([128, 8, 64]), ALU.mult), r=[xdt, cdec],
                   w=[xdte])
                for g in range(2):
                    mm(lambda g=g: T.matmul(pb[6][:, g * 256:(g + 1) * 256], Btok[:, g * 128:(g + 1) * 128],
                                            xdte[:, g * 256:(g + 1) * 256], start=True, stop=True), r=[Btok, xdte], w=[pb[6]],
                       inc=(g == 1))
                cp('d11')
                dv(lambda: V.tensor_tensor(yss[:].rearrange("p (h d) -> p h d", h=8), xtok[:].rearrange("p (h d) -> p h d", h=8),
                                           r8[:, 16:24].unsqueeze(2).to_broadcast([128, 8, 64]), ALU.mult), r=[xtok, r8], w=[yss])
                dv(lambda: V.tensor_tensor(yss[:], yss[:], pb[5][:], ALU.add), r=[yss, pb[5]], w=[yss])
                dv(lambda: V.tensor_tensor(hTs[:].rearrange("p (h d) -> p h d", h=8), hTs[:].rearrange("p (h d) -> p h d", h=8),
                                           cdec[:, 0:8].unsqueeze(2).to_broadcast([128, 8, 64]), ALU.mult), r=[hTs, cdec], w=[hTs])
                dv(lambda: V.tensor_tensor(hTs[:], hTs[:], pb[6][:], ALU.add), r=[hTs, pb[6]], w=[hTs])
                dv(lambda: V.tensor_tensor(yss[:], yss[:], zs[:], ALU.mult), r=[yss, zs], w=[yss])
                ac(lambda: S.activation(junk[:, 0:512], yss[:], AF.Square, accum_out=st8[:, 6:7]), r=[yss], w=[junk, st8])
                dv(lambda: V.tensor_scalar(st8[:, 6:7], st8[:, 6:7], 1.0 / 512, EPS, ALU.mult, ALU.add), r=[st8], w=[st8])
                ac(lambda: S.activation(st8[:, 6:7], st8[:, 6:7], AF.Ln), r=[st8], w=[st8])
                ac(lambda: S.activation(st8[:, 7:8], st8[:, 6:7], AF.Exp, scale=-0.5), r=[st8], w=[st8])
                dv(lambda: V.scalar_tensor_tensor(ymix[:, 512:1024], yss[:], st8[:, 7:8], sng[:], ALU.mult, ALU.mult),
                   r=[yss, st8, sng], w=[ymix])

                cp('ssdE')
                if stop == 'ssd':
                    continue
                if dbg and l == 0:
                    dma(dbg_o[i * 128:(i + 1) * 128, :], ymix[:], r=[ymix], w=[dbg_o])
                dv(lambda: V.tensor_copy(ymb[:], ymix[:]), r=[ymix], w=[ymb])
                to_fm(ymb, ymT)
                for n2 in range(2):
                    ps = pb[n2]
                    for kc in range(8):
                        mm(lambda kc=kc, n2=n2, ps=ps: T.matmul(ps[:], ymT[:, kc, :], w_out_v[:, kc, n2 * 512:(n2 + 1) * 512],
                                                                start=(kc == 0), stop=(kc == 7)), r=[ymT, wmix], w=[ps],
                           inc=(kc == 7), dense=True)
                    dv(lambda n2=n2, ps=ps: V.tensor_tensor(hf[:, n2 * 512:(n2 + 1) * 512], ps[:],
                                                            modp[:, 2 * D + n2 * 512:2 * D + (n2 + 1) * 512], ALU.mult),
                       r=[ps, modp], w=[hf])
                dv(lambda xti=xti: V.tensor_tensor(xti[:], xti[:], hf[:], ALU.add), r=[xti, hf], w=[xti])
                dma(xa[i * 128:(i + 1) * 128, :], xti[:], r=[xti], w=[xa])

            if stop in ('proj', 'rwkv', 'swa', 'ssd', 'mixer'):
                break
            for h in range(4):
                mm(lambda h=h: T.transpose(pb[1][0:64, h * 64:(h + 1) * 64], ST[:, h, :], ident[0:64, 0:64]),
                   r=[ST, ident], w=[pb[1]], inc=(h == 3))
            dv(lambda: V.tensor_copy(ostg[0:64, 0:256], pb[1][0:64, 0:256]), r=[pb[1]], w=[ostg])
            dma(o_pwkv[l].rearrange("h v k -> v h k"), ostg[0:64, 0:256].rearrange("v (h k) -> v h k", h=4), r=[ostg],
                w=[o_pwkv])
            for c in range(4):
                mm(lambda c=c: T.transpose(pb[0][:, c * 128:(c + 1) * 128], hTs[:, c * 128:(c + 1) * 128], ident[:]),
                   r=[hTs, ident], w=[pb[0]], inc=(c == 3))
            dv(lambda: V.tensor_copy(xtok[:], pb[0][:]), r=[pb[0]], w=[xtok])
            dma(o_pssm[l].rearrange("(c p) d -> p c d", p=128), xtok[:].rearrange("p (c d) -> p c d", c=4), r=[xtok],
                w=[o_pssm])

            if do_decode:
                decode_mixer(l)
            phase_reset()
            wffn = sb("wffn", [128, 8 * 5632 + 22 * 1024], BF16)
            w_up_v = wffn[:, 0:8 * 5632].rearrange("p (k n) -> p k n", k=8)
            w_dn_v = wffn[:, 8 * 5632:8 * 5632 + 22 * 1024].rearrange("p (k n) -> p k n", k=22)
            fw.dma(fw.pool, w_up_v, Wd['ffn_w_up'][l].rearrange("(k p) n -> p k n", p=128), reads=[Wd['ffn_w_up']],
                   writes=[wffn])
            fw.dma(fw.pool, w_dn_v, Wd['ffn_w_down'][l].rearrange("(k p) n -> p k n", p=128), reads=[Wd['ffn_w_down']],
                   writes=[wffn])
            xt = [sb("xt0", [128, D])]
            junk = sb("junk", [128, D], BF16)
            ss = sb("ss", [128, 8])
            hf = sb("hf", [128, D])
            hb = sb("hb", [128, D], BF16)
            hT = sb("hT", [128, 8, 128], BF16)
            gF = sb("gF", [128, NCH_FF, 130]); gC = sb("gC", [128, 4, 128]); prod = sb("prod", [128, NCH_FF, 128], BF16)
            modp = load_mod(l, 3 * D, 'norm_ffn_g')
            dv(lambda: V.memset(gF[:, :, 0:2], 0.0), w=[gF])
            xdst = y_p if l == L - 1 else xb
            for i in range(NT):
                xti = xt[0]
                dma(xti[:], xa[i * 128:(i + 1) * 128, :], r=[xa], w=[xti])
                rmsnorm_mod(xti, hb)
                to_fm(hb, hT)
                for gi in range(6):
                    ncg = 4 if gi < 5 else 2
                    psg, psv = pb[2 * (gi % 2)], pb[2 * (gi % 2) + 1]
                    for (ps, c0) in ((psg, 0), (psv, DFF)):
                        for oc in range(ncg):
                            ch = gi * 4 + oc
                            for kc in range(8):
                                mm(lambda ps=ps, c0=c0, oc=oc, ch=ch, kc=kc: T.matmul(
                                    ps[:, oc * 128:(oc + 1) * 128], w_up_v[:, kc, c0 + ch * 128:c0 + (ch + 1) * 128],
                                    hT[:, kc, :], start=(kc == 0), stop=(kc == 7)), r=[wffn, hT], w=[ps],
                                   inc=(kc == 7 and oc == ncg - 1), dense=True)
                    ac(lambda gi=gi, ncg=ncg, psg=psg: S.copy(gF[:, gi * 4:gi * 4 + ncg, 2:130],
                                                              psg[:, 0:ncg * 128].rearrange("p (k n) -> p k n", k=ncg)),
                       r=[psg], w=[gF])
                    if i == NT - 1:
                        pass
                    for oc in range(ncg):
                        ch = gi * 4 + oc
                        eng, E = (dv, V)
                        eng(lambda ch=ch, oc=oc, E=E: E.tensor_scalar(gC[:, oc, :], gF[:, ch, 0:128], fcw[:, 0, ch:ch + 1],
                                                               fcb[:, ch:ch + 1], ALU.mult, ALU.add), r=[gF, fcw, fcb], w=[gC])
                        for j in range(1, 3):
                            eng(lambda ch=ch, oc=oc, j=j, E=E: E.scalar_tensor_tensor(gC[:, oc, :], gF[:, ch, j:j + 128],
                                                                               fcw[:, j, ch:ch + 1], gC[:, oc, :], ALU.mult,
                                                                               ALU.add), r=[gF, fcw, gC], w=[gC])
                    ac(lambda gi=gi, ncg=ncg: S.activation(gC[:, 0:ncg, :], gC[:, 0:ncg, :], AF.Silu),
                       r=[gC], w=[gC])
                    dv(lambda gi=gi, ncg=ncg, psv=psv: V.tensor_tensor(
                        prod[:, gi * 4:gi * 4 + ncg, :], gC[:, 0:ncg, :],
                        psv[:, 0:ncg * 128].rearrange("p (k n) -> p k n", k=ncg), ALU.mult), r=[gC, psv], w=[prod])
                if i == NT - 1:
                    for j in range(2):
                        mm(lambda j=j: T.transpose(pb[6][0:NCH_FF, j * 128:(j + 1) * 128], gF[:, :, 128 + j], ident[:]),
                           r=[gF, ident], w=[pb[6]], inc=(j == 1))
                    dv(lambda: V.tensor_copy(ostg[0:NCH_FF, 0:256], pb[6][0:NCH_FF, 0:256]), r=[pb[6]], w=[ostg])
                    dma(o_pffn[l].rearrange("j (k p) -> k j p", p=128),
                        ostg[0:NCH_FF, 0:256].rearrange("k (j p) -> k j p", j=2), r=[ostg], w=[o_pffn])
                ac(lambda: S.copy(gF[:, :, 0:2], gF[:, :, 128:130]), r=[gF], w=[gF])
                for n2 in range(2):
                    ps = pb[4 + n2]
                    for kc in range(NCH_FF):
                        mm(lambda kc=kc, n2=n2, ps=ps: T.matmul(ps[:], prod[:, kc, :], w_dn_v[:, kc, n2 * 512:(n2 + 1) * 512],
                                                                start=(kc == 0), stop=(kc == NCH_FF - 1)), r=[prod, wffn],
                           w=[ps], inc=(kc == NCH_FF - 1), dense=True)
                    dv(lambda n2=n2, ps=ps: V.tensor_tensor(hf[:, n2 * 512:(n2 + 1) * 512], ps[:],
                                                            modp[:, 2 * D + n2 * 512:2 * D + (n2 + 1) * 512], ALU.mult),
                       r=[ps, modp], w=[hf])
                dv(lambda xti=xti: V.tensor_tensor(xti[:], xti[:], hf[:], ALU.add), r=[xti, hf], w=[xti])
                dma(xdst[i * 128:(i + 1) * 128, :], xti[:], r=[xti], w=[xdst])
            if do_decode:
                decode_ffn(l)

    except _Stop:
        pass

    for t in outs:
        fw._wait(fw.sp, t.lw)
    for q in (fw.pe, fw.act, fw.dve, fw.pool):
        if q.cnt > 0:
            fw._wait(fw.sp, Ev(q.sem, q.cnt, id(q.sem)))
    for i in range(fw.ndma):
        fw._wait(fw.sp, fw.dlast[i])
    return nc, fw


_CACHE = {}


def kernel(**inputs):
    SEQ, L, NS, NCORE = 8192, 4, 16, 8
    f = lambda a: np.ascontiguousarray(np.asarray(a), dtype=np.float32)
    inp = {k: f(v) for k, v in inputs.items()}
    if "nc" not in _CACHE:
        _CACHE["nc"] = build(SEQ, L, NS)[0]
    nc = _CACHE["nc"]
    ws = wshapes(L)
    wmaps = {k: inp[k].reshape(ws[k]) for k in WNAMES}
    in_maps = []
    for c in range(NCORE):
        b = c // 4
        sl = slice(c * NS, (c + 1) * NS)
        m = {
            "x_p": inp["x_prompt"][b], "x_s": inp["x_sample"][sl, 0],
            "c_all": np.concatenate([inp["c_sample"][sl], inp["c_prompt"][b:b + 1]], 0),
            "st_shift": inp["state_rwkv_shift"][:, sl], "st_wkv": inp["state_rwkv_wkv"][:, sl].reshape(L, NS * 4, 4096),
            "st_k": inp["cache_swa_k"][:, sl], "st_v": inp["cache_swa_v"][:, sl], "st_conv": inp["state_ssm_conv"][:, sl],
            "st_ssm": inp["state_ssm"][:, sl].reshape(L, NS * 8, 8192), "st_ffn": inp["state_ffn_conv"][:, sl],
        }
        m.update(wmaps)
        in_maps.append({k: np.ascontiguousarray(v, dtype=np.float32) for k, v in m.items()})
    res = run_bass_kernel_spmd(nc, in_maps, core_ids=list(range(NCORE))).results
    pc = [res[0], res[4]]
    st = lambda key, shp: np.stack([np.asarray(r[key]).reshape(shp) for r in pc], axis=1)
    cat = lambda key, shp: np.concatenate([np.asarray(r[key]).reshape(shp) for r in res], axis=1)
    y_p = np.stack([np.asarray(r["y_p"]) for r in pc], 0)
    y_s = np.concatenate([np.asarray(r["y_s"]).reshape(NS, 1, D) for r in res], 0)
    outs = (
        y_p, y_s,
        st("o_pshift", (L, 1024)), st("o_pwkv", (L, 4, 64, 64)), st("o_pk", (L, 128, 2, 64)), st("o_pv", (L, 128, 2, 64)),
        st("o_pconv", (L, 3, 1024)), st("o_pssm", (L, 8, 64, 128)), st("o_pffn", (L, 2, DFF)),
        cat("o_sshift", (L, NS, 1024)), cat("o_swkv", (L, NS, 4, 64, 64)), cat("o_sk", (L, NS, 128, 2, 64)),
        cat("o_sv", (L, NS, 128, 2, 64)), cat("o_sconv", (L, NS, 3, 1024)), cat("o_sssm", (L, NS, 8, 64, 128)),
        cat("o_sffn", (L, NS, 2, DFF)),
    )
    return tuple(np.ascontiguousarray(o, dtype=np.float32) for o in outs)
```
